# Optimizing a Trainium2 kernel written in Bass

```python
import jax, jax.numpy as jnp
from jax import lax
import numpy as np

D_MODEL = 1024
BATCH = 8
SEQ = 8192
DEPTH = 2

CHUNK = 64
PLE_DIM = 256
D_FF = 2816
CONV_W = 4
RET_HEADS = 4
RET_DK = 64
RET_DV = 64
RET_W = RET_HEADS * RET_DV
LRU_W = 384
LRU_BLOCKS = 6
LRU_BLOCK = LRU_W // LRU_BLOCKS
LRU_C = 8.0
GDN_HEADS = 6
GDN_DK = 64
GDN_DV = 64
GDN_W = GDN_HEADS * GDN_DV
D_MIX = RET_W + LRU_W + GDN_W
IN_WIDTHS = (RET_W, RET_W, RET_W, RET_W, LRU_W, LRU_W, GDN_W, GDN_W, GDN_W, GDN_W, GDN_HEADS, GDN_HEADS)
D_IN = sum(IN_WIDTHS)
ROPE_THETA = 10000.0
ALPHA = (2 * DEPTH) ** 0.25
BETA_INIT = (8 * DEPTH) ** -0.25
LN_EPS = 1e-5

kernel_name = 'hybrid_retention_rglru_gdn_macaron_deepnorm'


def layer_norm(x, g, b):
    xf = x.astype(jnp.float32)
    mu = jnp.mean(xf, -1, keepdims=True)
    var = jnp.mean(jnp.square(xf - mu), -1, keepdims=True)
    return ((xf - mu) * lax.rsqrt(var + LN_EPS) * g + b).astype(x.dtype)


def head_norm(t, eps=1e-5):
    mu = jnp.mean(t, -1, keepdims=True)
    return (t - mu) * lax.rsqrt(jnp.var(t, -1, keepdims=True) + eps)


def rms_norm(t, g, eps=1e-6):
    return t * lax.rsqrt(jnp.mean(t * t, -1, keepdims=True) + eps) * g


def l2_normalize(t, eps=1e-6):
    return t * lax.rsqrt(jnp.sum(t * t, -1, keepdims=True) + eps)


def swiglu(x, w_gate, w_up, w_down):
    return (jax.nn.silu(x @ w_gate) * (x @ w_up)) @ w_down


def causal_depthwise_conv(x, w, b=None):
    K = w.shape[0]
    S = x.shape[1]
    xp = jnp.pad(x, ((0, 0), (K - 1, 0), (0, 0)))
    y = sum(w[k] * xp[:, k:k + S] for k in range(K))
    return y if b is None else y + b


def rotary(t, positions):
    half = t.shape[-1] // 2
    inv_freq = ROPE_THETA ** (-jnp.arange(half, dtype=jnp.float32) / half)
    ang = positions.astype(jnp.float32)[..., None] * inv_freq
    cos = jnp.cos(ang)[:, :, None, :]
    sin = jnp.sin(ang)[:, :, None, :]
    t1, t2 = t[..., :half], t[..., half:]
    return jnp.concatenate([t1 * cos - t2 * sin, t2 * cos + t1 * sin], -1)


def retention_chunked(q, k, v, positions):
    B, S, H, _ = q.shape
    N = S // CHUNK
    q = rotary(q, positions) * RET_DK ** -0.5
    k = rotary(k, positions)
    log_gamma = jnp.log1p(-jnp.exp2(-5.0 - jnp.arange(H, dtype=jnp.float32)))
    idx = jnp.arange(CHUNK, dtype=jnp.float32)
    intra = jnp.exp(jnp.abs(idx[:, None] - idx[None, :])[None] * log_gamma[:, None, None])
    cross = jnp.exp((idx + 1.0)[None] * log_gamma[:, None])
    tail = jnp.exp((CHUNK - 1.0 - idx)[None] * log_gamma[:, None])
    chunk_decay = jnp.exp(CHUNK * log_gamma)
    qc = q.reshape(B, N, CHUNK, H, RET_DK)
    kc = k.reshape(B, N, CHUNK, H, RET_DK)
    vc = v.reshape(B, N, CHUNK, H, RET_DV)
    scores = jnp.einsum('bnihd,bnjhd->bnhij', qc, kc) * intra
    o_intra = jnp.einsum('bnhij,bnjhe->bnihe', scores, vc)
    kv = jnp.einsum('bnjhd,hj,bnjhe->nbhde', kc, tail, vc)

    def step(state, kv_n):
        return state * chunk_decay[None, :, None, None] + kv_n, state

    _, states = lax.scan(step, jnp.zeros((B, H, RET_DK, RET_DV), jnp.float32), kv)
    o_cross = jnp.einsum('bnihd,hi,nbhde->bnihe', qc, cross, states)
    return (o_intra + o_cross).reshape(B, S, H, RET_DV)


def rg_lru(x, conv_w, conv_b, w_a, b_a, w_x, b_x, lam):
    B, S, _ = x.shape
    x = causal_depthwise_conv(x, conv_w, conv_b)
    xb = x.reshape(B, S, LRU_BLOCKS, LRU_BLOCK)
    r = jax.nn.sigmoid(jnp.einsum('bsgi,gij->bsgj', xb, w_a).reshape(B, S, LRU_W) + b_a)
    i = jax.nn.sigmoid(jnp.einsum('bsgi,gij->bsgj', xb, w_x).reshape(B, S, LRU_W) + b_x)
    log_a = LRU_C * r * jax.nn.log_sigmoid(lam)
    a = jnp.exp(log_a)
    b = jnp.sqrt(-jnp.expm1(2.0 * log_a)) * (i * x)

    def combine(left, right):
        a1, b1 = left
        a2, b2 = right
        return a1 * a2, a2 * b1 + b2

    _, h = lax.associative_scan(combine, (a, b), axis=1)
    return h


def gated_delta_chunked(q, k, v, g, beta):
    B, S, H, DK = q.shape
    DV = v.shape[-1]
    N = S // CHUNK

    def to_chunks(t):
        return jnp.moveaxis(t.reshape(B, N, CHUNK, H, *t.shape[3:]), 3, 2)

    q = to_chunks(q) * DK ** -0.5
    k = to_chunks(k)
    v = to_chunks(v)
    g = to_chunks(g)
    beta = to_chunks(beta)
    gc = jnp.cumsum(g, axis=-1)
    incl = jnp.tril(jnp.ones((CHUNK, CHUNK), bool))
    strict = jnp.tril(jnp.ones((CHUNK, CHUNK), bool), -1)
    decay = jnp.where(incl, jnp.exp(jnp.minimum(gc[..., :, None] - gc[..., None, :], 0.0)), 0.0)
    kk = jnp.einsum('bnhid,bnhjd->bnhij', k, k)
    a_mat = jnp.where(strict, beta[..., :, None] * kk * decay, 0.0) + jnp.eye(CHUNK, dtype=jnp.float32)
    rhs = jnp.concatenate([v * beta[..., None], k * (beta * jnp.exp(gc))[..., None]], -1)
    sol = lax.linalg.triangular_solve(a_mat, rhs, left_side=True, lower=True, unit_diagonal=True)
    u, w = sol[..., :DV], sol[..., DV:]
    qk = jnp.einsum('bnhid,bnhjd->bnhij', q, k) * decay
    q_dec = q * jnp.exp(gc)[..., None]
    g_last = gc[..., -1]
    k_tail = k * jnp.exp(g_last[..., None] - gc)[..., None]
    xs = (jnp.moveaxis(u, 1, 0), jnp.moveaxis(w, 1, 0), jnp.moveaxis(qk, 1, 0),
          jnp.moveaxis(q_dec, 1, 0), jnp.moveaxis(k_tail, 1, 0), jnp.moveaxis(g_last, 1, 0))

    def step(state, inp):
        u_n, w_n, qk_n, qd_n, kt_n, gl_n = inp
        v_new = u_n - jnp.einsum('bhcd,bhde->bhce', w_n, state)
        o = jnp.einsum('bhcd,bhde->bhce', qd_n, state) + jnp.einsum('bhij,bhje->bhie', qk_n, v_new)
        state = state * jnp.exp(gl_n)[..., None, None] + jnp.einsum('bhcd,bhce->bhde', kt_n, v_new)
        return state, o

    _, o = lax.scan(step, jnp.zeros((B, H, DK, DV), jnp.float32), xs)
    return jnp.transpose(o, (1, 0, 3, 2, 4)).reshape(B, S, H, DV)


def hybrid_mixer(x, positions, w_in, ret_norm_g, lru_conv_w, lru_conv_b, lru_w_a, lru_b_a,
                 lru_w_x, lru_b_x, lru_lambda, gdn_conv_w, gdn_a_log, gdn_dt_bias, gdn_norm_g, w_out):
    B, S, _ = x.shape
    h = (x @ w_in).astype(jnp.float32)
    splits = np.cumsum(IN_WIDTHS)[:-1].tolist()
    q_r, k_r, v_r, g_r, x_l, gate_l, q_g, k_g, v_g, z_g, a_g, b_g = jnp.split(h, splits, axis=-1)

    o_r = retention_chunked(q_r.reshape(B, S, RET_HEADS, RET_DK), k_r.reshape(B, S, RET_HEADS, RET_DK),
                            v_r.reshape(B, S, RET_HEADS, RET_DV), positions)
    o_r = head_norm(o_r).reshape(B, S, RET_W) * ret_norm_g * jax.nn.silu(g_r)

    o_l = rg_lru(x_l, lru_conv_w, lru_conv_b, lru_w_a, lru_b_a, lru_w_x, lru_b_x, lru_lambda) * jax.nn.gelu(gate_l)

    qkv = jax.nn.silu(causal_depthwise_conv(jnp.concatenate([q_g, k_g, v_g], -1), gdn_conv_w))
    q_g, k_g, v_g = jnp.split(qkv, 3, axis=-1)
    q_g = l2_normalize(q_g.reshape(B, S, GDN_HEADS, GDN_DK))
    k_g = l2_normalize(k_g.reshape(B, S, GDN_HEADS, GDN_DK))
    v_g = v_g.reshape(B, S, GDN_HEADS, GDN_DV)
    beta = jax.nn.sigmoid(b_g)
    g = -jnp.exp(gdn_a_log) * jax.nn.softplus(a_g + gdn_dt_bias)
    o_g = gated_delta_chunked(q_g, k_g, v_g, g, beta)
    o_g = (rms_norm(o_g, gdn_norm_g) * jax.nn.silu(z_g.reshape(B, S, GDN_HEADS, GDN_DV))).reshape(B, S, GDN_W)

    o = jnp.concatenate([o_r, o_l, o_g], -1).astype(x.dtype)
    return o @ w_out


def setup_inputs(seed: int = 0) -> dict:
    key = jax.random.key(seed)
    counter = [0]

    def nk():
        counter[0] += 1
        return jax.random.fold_in(key, counter[0])

    f32 = jnp.float32
    L = DEPTH

    def nrm(shape, fan_in, scale=1.0):
        return jax.random.normal(nk(), shape, f32) * (scale * fan_in ** -0.5)

    def gain(shape):
        return 1.0 + 0.02 * jax.random.normal(nk(), shape, f32)

    def bias(shape):
        return 0.02 * jax.random.normal(nk(), shape, f32)

    x = jax.random.normal(nk(), (BATCH, SEQ, D_MODEL), f32)
    p = jax.random.normal(nk(), (DEPTH, BATCH, SEQ, PLE_DIM), f32)
    start = jax.random.randint(nk(), (BATCH, 1), 0, 4096, jnp.int32)
    positions = start + jnp.arange(SEQ, dtype=jnp.int32)[None, :]
    a0 = jax.random.uniform(nk(), (L, LRU_W), f32, 0.9, 0.999)
    lru_lambda = jnp.log(a0) - jnp.log1p(-a0)
    gdn_a_log = jnp.log(jax.random.uniform(nk(), (L, GDN_HEADS), f32, 1.0, 16.0))
    dt = jnp.exp(jax.random.uniform(nk(), (L, GDN_HEADS), f32, np.log(1e-3), np.log(1e-1)))
    gdn_dt_bias = dt + jnp.log(-jnp.expm1(-dt))
    return {
        'x': x,
        'p': p,
        'positions': positions,
        'ln_ffn1_g': gain((L, D_MODEL)),
        'ln_ffn1_b': bias((L, D_MODEL)),
        'ffn1_w_gate': nrm((L, D_MODEL, D_FF), D_MODEL),
        'ffn1_w_up': nrm((L, D_MODEL, D_FF), D_MODEL),
        'ffn1_w_down': nrm((L, D_FF, D_MODEL), D_FF, BETA_INIT),
        'w_in': nrm((L, D_MODEL, D_IN), D_MODEL),
        'ret_norm_g': gain((L, RET_W)),
        'lru_conv_w': nrm((L, CONV_W, LRU_W), CONV_W),
        'lru_conv_b': bias((L, LRU_W)),
        'lru_w_a': nrm((L, LRU_BLOCKS, LRU_BLOCK, LRU_BLOCK), LRU_BLOCK),
        'lru_b_a': bias((L, LRU_W)),
        'lru_w_x': nrm((L, LRU_BLOCKS, LRU_BLOCK, LRU_BLOCK), LRU_BLOCK),
        'lru_b_x': bias((L, LRU_W)),
        'lru_lambda': lru_lambda,
        'gdn_conv_w': nrm((L, CONV_W, 3 * GDN_W), CONV_W),
        'gdn_a_log': gdn_a_log,
        'gdn_dt_bias': gdn_dt_bias,
        'gdn_norm_g': gain((L, GDN_DV)),
        'w_out': nrm((L, D_MIX, D_MODEL), D_MIX, BETA_INIT),
        'ln_mix_g': gain((L, D_MODEL)),
        'ln_mix_b': bias((L, D_MODEL)),
        'ffn2_w_gate': nrm((L, D_MODEL, D_FF), D_MODEL),
        'ffn2_w_up': nrm((L, D_MODEL, D_FF), D_MODEL),
        'ffn2_w_down': nrm((L, D_FF, D_MODEL), D_FF, BETA_INIT),
        'ple_w_gate': nrm((L, D_MODEL, D_MODEL), D_MODEL),
        'ple_w_proj': nrm((L, PLE_DIM, D_MODEL), PLE_DIM, BETA_INIT),
        'ln_ffn2_g': gain((L, D_MODEL)),
        'ln_ffn2_b': bias((L, D_MODEL)),
    }


def reference(x, p, positions, ln_ffn1_g, ln_ffn1_b, ffn1_w_gate, ffn1_w_up, ffn1_w_down,
              w_in, ret_norm_g, lru_conv_w, lru_conv_b, lru_w_a, lru_b_a, lru_w_x, lru_b_x, lru_lambda,
              gdn_conv_w, gdn_a_log, gdn_dt_bias, gdn_norm_g, w_out, ln_mix_g, ln_mix_b,
              ffn2_w_gate, ffn2_w_up, ffn2_w_down, ple_w_gate, ple_w_proj, ln_ffn2_g, ln_ffn2_b):
    for i in range(DEPTH):
        x = layer_norm(ALPHA * x + 0.5 * swiglu(x, ffn1_w_gate[i], ffn1_w_up[i], ffn1_w_down[i]),
                       ln_ffn1_g[i], ln_ffn1_b[i])
        mix = hybrid_mixer(x, positions, w_in[i], ret_norm_g[i], lru_conv_w[i], lru_conv_b[i],
                           lru_w_a[i], lru_b_a[i], lru_w_x[i], lru_b_x[i], lru_lambda[i],
                           gdn_conv_w[i], gdn_a_log[i], gdn_dt_bias[i], gdn_norm_g[i], w_out[i])
        x = layer_norm(ALPHA * x + mix, ln_mix_g[i], ln_mix_b[i])
        ffn = 0.5 * swiglu(x, ffn2_w_gate[i], ffn2_w_up[i], ffn2_w_down[i])
        ple = jax.nn.sigmoid(x @ ple_w_gate[i]) * (p[i] @ ple_w_proj[i])
        x = layer_norm(ALPHA * x + ffn + ple, ln_ffn2_g[i], ln_ffn2_b[i])
    return x
```

```python
import contextlib
import math
import os
import numpy as np
import concourse.bass as bass
import concourse.mybir as mybir
from concourse.bass_utils import run_bass_kernel_spmd

F32 = mybir.dt.float32
BF16 = mybir.dt.bfloat16
I32 = mybir.dt.int32
AF = mybir.ActivationFunctionType
ALU = mybir.AluOpType
AX = mybir.AxisListType

D = 1024
DFF = 2816
NFC = DFF // 128
T = 512
NS = 4
L = 2
SEQ = 8192
NCORES = 8
ALPHA = 4.0 ** 0.25
LN_EPS = 1e-5
SLAB = 4096
TWO_PI = 2.0 * math.pi


class Tok:
    __slots__ = ("sem", "val", "eng", "sid")

    def __init__(self, sem, val, eng, sid):
        self.sem, self.val, self.eng, self.sid = sem, val, eng, sid


class Buf:
    __slots__ = ("name", "t", "last_w", "readers", "dsem", "dcount", "dsid", "psum", "rg", "pe_tok")

    def __init__(self, name, t=None):
        self.name, self.t = name, t
        self.psum = False
        self.rg = None
        self.pe_tok = None
        self.last_w = None
        self.readers = []
        self.dsem = None
        self.dcount = 0
        self.dsid = None

    def __getitem__(self, k):
        return self.t[k]


class Ctx:
    SELF_SYNC = True

    def __init__(self, nc):
        self.nc = nc
        self.es = contextlib.ExitStack()
        self.eng = {"pe": nc.tensor, "act": nc.scalar, "dve": nc.vector,
                    "pool": nc.gpsimd, "sp": nc.sync}
        self.sem = {}
        self.cnt = {}
        self.known = {e: {} for e in self.eng}
        self.nsem = 0
        for e in ("pe", "act", "dve", "pool"):
            self.sem[e] = self.new_sem("s_" + e)
            self.cnt[e] = 0
        self.nops = 0
        self.nwaits = 0

    def new_sem(self, name):
        s = self.es.enter_context(self.nc.semaphore(name))
        self.nsem += 1
        return (s, self.nsem)

    def sbuf(self, name, shape, dt, es=None):
        t = (es or self.es).enter_context(self.nc.sbuf_tensor(name, list(shape), dt))
        return Buf(name, t)

    def psum(self, name, shape, dt=F32):
        t = self.es.enter_context(self.nc.psum_tensor(name, list(shape), dt))
        b = Buf(name, t)
        b.psum = True
        return b

    def _waits(self, e, reads, writes, rg=None):
        toks = []
        for b in reads:
            if b.last_w is not None:
                toks.append(b.last_w)
            if b.psum:
                toks.extend(r for r in b.readers if r.eng != e)
        for b in writes:
            if b.psum and e == "pe" and rg is not None:
                if b.rg is not None and b.rg != rg and b.pe_tok is not None:
                    t = b.pe_tok
                    if self.known[e].get(t.sid, 0) < t.val:
                        self.eng[e].wait_ge(t.sem, t.val)
                        self.known[e][t.sid] = t.val
                        self.nwaits += 1
                b.rg = rg
            if b.last_w is not None:
                toks.append(b.last_w)
            toks.extend(b.readers)
        E = self.eng[e]
        kn = self.known[e]
        raw = set()
        for b in reads:
            if b.last_w is not None:
                raw.add(id(b.last_w))
        for t in toks:
            if t.eng == e and (e == "pe" or not self.SELF_SYNC or id(t) not in raw):
                continue
            if kn.get(t.sid, 0) >= t.val:
                continue
            E.wait_ge(t.sem, t.val)
            kn[t.sid] = t.val
            self.nwaits += 1

    def _commit(self, tok, reads, writes):
        for b in reads:
            b.readers = [r for r in b.readers if r.sid != tok.sid]
            b.readers.append(tok)
        for b in writes:
            b.last_w = tok
            b.readers = []

    def op(self, e, fn, reads=(), writes=(), rg=None):
        self._waits(e, reads, writes, rg)
        ins = fn(self.eng[e])
        self.cnt[e] += 1
        s, sid = self.sem[e]
        ins.then_inc(s, 1)
        tok = Tok(s, self.cnt[e], e, sid)
        self._commit(tok, reads, writes)
        if e == "pe":
            for b in writes:
                b.pe_tok = tok
        self.nops += 1
        return tok

    def dma(self, out_ap, in_ap, reads, writes, sb, q="sp"):
        self._waits(q, reads, writes)
        if sb.dsem is None:
            sb.dsem, sb.dsid = self.new_sem("d_" + sb.name)
        ins = self.eng[q].dma_start(out=out_ap, in_=in_ap)
        sb.dcount += 16
        ins.then_inc(sb.dsem, 16)
        tok = Tok(sb.dsem, sb.dcount, "dma", sb.dsid)
        self._commit(tok, reads, writes)
        self.nops += 1
        return tok

    def wait_all(self, e, bufs):
        self._waits(e, bufs, bufs)

    def close(self):
        self.es.close()


def _swap_half(cols):
    c = np.asarray(cols).reshape(-1, 64)
    return np.concatenate([c[:, 32:], c[:, :32]], axis=1).reshape(-1)


def _fm_chunks():
    rq = np.arange(0, 256)
    rk = np.arange(256, 512)
    rqs, rks = _swap_half(rq), _swap_half(rk)
    ch = [rq[:128], rq[128:], rqs[:128], rqs[128:], rk[:128], rk[128:], rks[:128], rks[128:]]
    for base in (1024, 1408, 1792, 2176, 2560):
        for c in range(3):
            ch.append(np.arange(base + c * 128, base + (c + 1) * 128))
    return ch


def _unit_table():
    tab = {}
    off = 0

    def add(name, kc, ncols):
        nonlocal off
        tab[name] = (off, kc, ncols)
        off += kc * ncols

    for f in ("f1", "f2"):
        for u in range(11):
            add(f"{f}_gu{u}", 8, 512)
        for j in range(4):
            add(f"{f}_d{j}", NFC, 256)
    for u in range(6):
        add(f"fm{u}", 8, 512 if u < 5 else 384)
    add("ba", 8, 12)
    add("tmA", 8, 512)
    add("tmB", 8, 396)
    for j in range(2):
        add(f"wo{j}", 8, 512)
    for j in range(2):
        add(f"pg{j}", 8, 512)
    add("pp", 2, 1024)
    add("lruw", 6, 128)
    return tab, off


UNITS, WC = _unit_table()
NSLAB = (WC + SLAB - 1) // SLAB
WCP = NSLAB * SLAB


def _unit(W, cols):
    K = W.shape[0]
    kc = K // 128
    cols = np.asarray(cols)
    return W[:, cols].reshape(kc, 128, len(cols)).transpose(1, 0, 2)


def _build_wblob(inp):
    blob = np.zeros((L, 128, WCP), np.float32)
    fmch = _fm_chunks()
    for l in range(L):
        def put(name, arr):
            off, kc, ncols = UNITS[name]
            assert arr.shape == (128, kc, ncols), (name, arr.shape)
            blob[l, :, off:off + kc * ncols] = arr.reshape(128, kc * ncols)
        for f, pre in (("f1", "ffn1"), ("f2", "ffn2")):
            Wg, Wu, Wd = inp[pre + "_w_gate"][l], inp[pre + "_w_up"][l], inp[pre + "_w_down"][l]
            for u in range(11):
                parts = []
                for cc in range(2):
                    cols = np.arange((2 * u + cc) * 128, (2 * u + cc + 1) * 128)
                    parts += [_unit(Wg, cols), _unit(Wu, cols)]
                put(f"{f}_gu{u}", np.concatenate(parts, axis=2))
            for j in range(4):
                put(f"{f}_d{j}", _unit(Wd, np.arange(j * 256, (j + 1) * 256)))
        Win = inp["w_in"][l]
        for u in range(6):
            cols = np.concatenate(fmch[4 * u:4 * u + 4])
            put(f"fm{u}", _unit(Win, cols))
        put("ba", _unit(Win, np.concatenate([np.arange(3334, 3340), np.arange(3328, 3334)])))
        put("tmA", _unit(Win, np.arange(512, 1024)))
        put("tmB", _unit(Win, np.arange(2944, 3340)))
        for j in range(2):
            put(f"wo{j}", _unit(inp["w_out"][l], np.arange(j * 512, (j + 1) * 512)))
            put(f"pg{j}", _unit(inp["ple_w_gate"][l], np.arange(j * 512, (j + 1) * 512)))
        put("pp", _unit(inp["ple_w_proj"][l], np.arange(1024)))
        bd = np.zeros((128, 6, 128), np.float32)
        for c in range(3):
            for gi, nm in enumerate(("lru_w_a", "lru_w_x")):
                w = inp[nm][l]
                bd[0:64, c * 2 + gi, 0:64] = w[2 * c]
                bd[64:128, c * 2 + gi, 64:128] = w[2 * c + 1]
        put("lruw", bd)
    return blob


CST = {}
_coff = 0
for _n, _w in (("ident", 128), ("triU", 128), ("ones", 128), ("MI", 128), ("MSu", 128), ("MSl", 128),
               ("sel63", 128), ("sel127", 128), ("selB", 128), ("bones", 128), ("intraT", 512),
               ("cross", 1024), ("tailf", 256), ("cdcol", 2), ("invf", 1), ("sgn", 1), ("selb", 384), ("oh6", 768)):
    CST[_n] = (_coff, _w)
    _coff += _w
NCST = _coff


def _build_consts():
    c = np.zeros((128, NCST), np.float32)

    def put(name, arr):
        off, w = CST[name]
        c[:, off:off + w] = arr

    p = np.arange(128)
    same = (p[:, None] // 64) == (p[None, :] // 64)
    put("ident", np.eye(128))
    put("triU", (same & (p[:, None] <= p[None, :])).astype(np.float32))
    put("ones", np.ones((128, 128)))
    put("MI", (same & (p[:, None] <= p[None, :])).astype(np.float32))
    put("MSu", (same & (p[:, None] < p[None, :])).astype(np.float32))
    put("MSl", (same & (p[:, None] > p[None, :])).astype(np.float32))
    s63 = np.zeros((128, 128)); s63[63, :] = 1
    s127 = np.zeros((128, 128)); s127[127, :] = 1
    sB = np.zeros((128, 128)); sB[63, :64] = 1; sB[127, 64:] = 1
    put("sel63", s63); put("sel127", s127); put("selB", sB)
    put("bones", same.astype(np.float32))
    lg = np.log1p(-np.exp2(-5.0 - np.arange(4, dtype=np.float32))).astype(np.float32)
    idx = (p % 64).astype(np.float32)
    intra = np.zeros((128, 4, 128), np.float32)
    for h in range(4):
        m = np.exp(np.abs(idx[:, None] - idx[None, :]) * lg[h]).astype(np.float32)
        intra[:, h, :] = np.where(same, m, 0.0)
    put("intraT", intra.reshape(128, 512))
    cross = np.zeros((128, 2, 512), np.float32)
    ii = (np.arange(512) % 64).astype(np.float32)
    for c2 in range(2):
        for hh in range(2):
            cross[hh * 64:(hh + 1) * 64, c2, :] = np.exp((ii + 1.0) * lg[2 * c2 + hh])[None, :]
    put("cross", cross.reshape(128, 1024))
    tail = np.zeros((128, 4, 64), np.float32)
    for h in range(4):
        tail[:, h, :] = np.exp((63.0 - idx) * lg[h])[:, None]
    put("tailf", tail.reshape(128, 256))
    cd = np.zeros((128, 2), np.float32)
    for c2 in range(2):
        for hh in range(2):
            cd[hh * 64:(hh + 1) * 64, c2] = np.exp(np.float32(64.0) * lg[2 * c2 + hh])
    put("cdcol", cd)
    half = 32
    invf = (10000.0 ** (-np.arange(half, dtype=np.float32) / half)).astype(np.float32)
    put("invf", invf[p % 32][:, None])
    put("sgn", np.where((p % 64) < 32, -1.0, 1.0)[:, None])
    selb = np.zeros((128, 384), np.float32)
    for h in range(6):
        selb[h, h * 64:(h + 1) * 64] = 1.0
    put("selb", selb)
    oh = np.zeros((128, 768), np.float32)
    for h in range(6):
        oh[h, h * 128:(h + 1) * 128] = 1.0
    put("oh6", oh)
    return c


NTM = 256 + 384 + 6 + 6


def _build_small(inp):
    fms = np.zeros((128, L, 60), np.float32)
    tms = np.zeros((1, L * NTM), np.float32)
    lnp = np.zeros((L * 3, 1, 2048), np.float32)
    for l in range(L):
        for c in range(3):
            sl = slice(c * 128, (c + 1) * 128)
            fms[:, l, c * 8 + 0:c * 8 + 4] = inp["lru_conv_w"][l][:, sl].T
            fms[:, l, c * 8 + 4] = inp["lru_conv_b"][l][sl]
            fms[:, l, c * 8 + 5] = inp["lru_b_a"][l][sl]
            fms[:, l, c * 8 + 6] = inp["lru_b_x"][l][sl]
            fms[:, l, c * 8 + 7] = inp["lru_lambda"][l][sl]
        for ch in range(9):
            sl = slice(ch * 128, (ch + 1) * 128)
            fms[:, l, 24 + ch * 4:24 + ch * 4 + 4] = inp["gdn_conv_w"][l][:, sl].T
        o = l * NTM
        tms[0, o:o + 256] = inp["ret_norm_g"][l]
        tms[0, o + 256:o + 640] = np.tile(inp["gdn_norm_g"][l], 6)
        tms[0, o + 640:o + 646] = inp["gdn_a_log"][l]
        tms[0, o + 646:o + 652] = inp["gdn_dt_bias"][l]
        for w, (g, b) in enumerate((("ln_ffn1_g", "ln_ffn1_b"), ("ln_mix_g", "ln_mix_b"), ("ln_ffn2_g", "ln_ffn2_b"))):
            lnp[l * 3 + w, 0, :1024] = inp[g][l]
            lnp[l * 3 + w, 0, 1024:] = inp[b][l]
    return fms.reshape(128, L * 60), tms, lnp


def run_interleaved(gens):
    gens = list(gens)
    while gens:
        for g in list(gens):
            try:
                next(g)
            except StopIteration:
                gens.remove(g)


class KB:
    STAGES = ["setup", "rot", "ffn1", "ret", "lru", "gdn", "wout", "ple", "all"]

    def upto(self, name):
        return self.STAGES.index(self.stop) >= self.STAGES.index(name)

    def __init__(self, nc, NT, dbg=(), stop="all"):
        self.nc = nc
        self.NT = NT
        self.stop = stop
        self.c = Ctx(nc)
        self.dbg = set(dbg)
        S = NT * T
        dt = nc.dram_tensor
        self.x_d = dt("x", [S, D], F32, kind="ExternalInput").ap()
        self.p_d = dt("p", [L, S, 256], F32, kind="ExternalInput").ap()
        self.pos_d = dt("pos", [1, S], I32, kind="ExternalInput").ap()
        self.wb_d = dt("wblob", [L, 128, WCP], F32, kind="ExternalInput").ap()
        self.cst_d = dt("cst", [128, NCST], F32, kind="ExternalInput").ap()
        self.fms_d = dt("fms", [128, L * 60], F32, kind="ExternalInput").ap()
        self.tms_d = dt("tms", [1, L * NTM], F32, kind="ExternalInput").ap()
        self.lnp_d = dt("lnp", [L * 3, 1, 2048], F32, kind="ExternalInput").ap()
        self.out_d = dt("out", [S, D], F32, kind="ExternalOutput").ap()
        self.wsc = Buf("wscr", dt("wscr", [L, 128, WCP], BF16, kind="Internal").ap())
        self.dbg_d = {}
        for n, shp in (("x1", [S, D]), ("mix", [D, S]), ("x2", [S, D]), ("x3", [S, D])):
            if n in self.dbg:
                self.dbg_d[n] = dt("dbg_" + n, shp, F32, kind="ExternalOutput").ap()

    def mm(self, psb, out, lhsT, rhs, reads, start=True, stop=True):
        rg = (lhsT.base_partition(), lhsT.partition_size())
        return self.c.op("pe", lambda e: e.matmul(out, lhsT=lhsT, rhs=rhs, start=start, stop=stop), reads, [psb], rg)

    def tr(self, psb, out, in_, reads):
        idn = self.cs("ident")
        rg = (in_.base_partition(), in_.partition_size())
        return self.c.op("pe", lambda e: e.transpose(out, in_, idn), list(reads) + [self.CST], [psb], rg)

    def act(self, out, in_, func, reads, writes, bias=None, scale=None):
        kw = {}
        if bias is not None:
            kw["bias"] = bias
        if scale is not None:
            kw["scale"] = scale
        return self.c.op("act", lambda e: e.activation(out=out, in_=in_, func=func, **kw), reads, writes)

    def tt(self, eng, out, in0, in1, op, reads, writes):
        return self.c.op(eng, lambda e: e.tensor_tensor(out=out, in0=in0, in1=in1, op=op), reads, writes)

    def ts(self, eng, out, in0, s1, s2, op0, op1, reads, writes):
        if s2 is None:
            return self.c.op(eng, lambda e: e.tensor_scalar(out=out, in0=in0, scalar1=s1, scalar2=None, op0=op0), reads, writes)
        return self.c.op(eng, lambda e: e.tensor_scalar(out=out, in0=in0, scalar1=s1, scalar2=s2, op0=op0, op1=op1), reads, writes)

    def stt(self, out, in0, scalar, in1, op0, op1, reads, writes):
        return self.c.op("dve", lambda e: e.scalar_tensor_tensor(out=out, in0=in0, scalar=scalar, in1=in1, op0=op0, op1=op1), reads, writes)

    def cp(self, eng, out, in_, reads, writes):
        if eng == "act":
            return self.c.op("act", lambda e: e.copy(out=out, in_=in_), reads, writes)
        return self.c.op(eng, lambda e: e.tensor_copy(out=out, in_=in_), reads, writes)

    def ps(self):
        b = self.PS[self.psi % self.ps_n]
        self.psi += 1
        return b

    def cs(self, name, rows=slice(0, 128)):
        off, w = CST[name]
        return self.CST[rows, off:off + w]

    def alloc(self):
        c = self.c
        self.PS = [c.psum(f"ps{i}", [128, 512]) for i in range(8)]
        self.psi = 0
        self.ps_n = 8
        self.CST = c.sbuf("cst_sb", [128, NCST], F32)
        self.FMS = c.sbuf("fms_sb", [128, L * 60], F32)
        self.TMS = c.sbuf("tms_sb", [128, L * NTM], F32)
        self.DER = c.sbuf("der_sb", [128, L * 16], F32)
        self.BONES = c.sbuf("bones_bf", [128, 128], BF16)
        self.LRUW = [c.sbuf(f"lruw{l}", [128, 768], BF16) for l in range(L)]
        self.SR = [c.sbuf(f"sr{l}", [128, 128], F32) for l in range(L)]
        self.SG = [c.sbuf(f"sg{l}", [128, 192], F32) for l in range(L)]
        self.SGB = [c.sbuf(f"sgb{l}", [128, 192], BF16) for l in range(L)]
        self.HS = [c.sbuf(f"hs{l}", [128, 4], F32) for l in range(L)]
        self.HL = [c.sbuf(f"hl{l}", [128, 3 * 3], F32) for l in range(L)]
        self.HG = [c.sbuf(f"hg{l}", [128, 9 * 3], F32) for l in range(L)]

    def alloc_main(self):
        c = self.c
        self.X = [c.sbuf(f"x{s}", [128, D], F32) for s in range(NS)]
        self.XT = [c.sbuf(f"xT{k}", [128, T], BF16) for k in range(8)]
        self.WA = [c.sbuf(f"wA{i}", [128, 4096], BF16) for i in range(3)]
        self.wai = 0
        self.WD = [c.sbuf(f"wD{i}", [128, NFC * 256], BF16) for i in range(2)]
        self.wdi = 0
        self.MIXT = [c.sbuf(f"mixT{k}", [128, T], BF16) for k in range(8)]
        self.COS = c.sbuf("cosT", [128, T], F32)
        self.SIN = c.sbuf("sinT", [128, T], F32)
        self.SM = [c.sbuf(f"sm{i}", [128, 64], F32) for i in range(8)]
        self.LNS = [c.sbuf(f"lns{i}", [128, 20], F32) for i in range(NS)]
        self.RST = [c.sbuf(f"rst{i}", [128, 32], F32) for i in range(NS)]
        self.smi = 0
        self.POOL = [c.sbuf(f"scr{i}", [128, 512], F32) for i in range(NPOOL)]
        self.GH = {}
        for k, (w, dtp) in {"F": (256, F32), "E": (384, BF16), "M": (384, BF16), "T": (128, BF16), "Na": (256, BF16),
                            "Nb": (256, BF16), "V": (192, BF16), "U": (128, F32)}.items():
            self.GH[k] = [c.sbuf(f"gh{k}{h}", [128, w], dtp) for h in range(6)]
        self.GSC = [c.sbuf(f"gsc{i}", [128, 64], F32) for i in range(4)]
        self.GCTB = [c.sbuf(f"gct{i}", [128, 128], F32) for i in range(4)]
        self.GVB = [c.sbuf(f"gv{i}", [128, 1152], BF16) for i in range(2)]
        self.GOG = [c.sbuf(f"gog{i}", [128, 384], F32) for i in range(2)]

    def sm(self):
        b = self.SM[self.smi % len(self.SM)]
        self.smi += 1
        return b

    def setup(self):
        c = self.c
        c.dma(self.CST[:], self.cst_d, [], [self.CST], self.CST)
        c.dma(self.FMS[:], self.fms_d, [], [self.FMS], self.FMS)
        c.dma(self.TMS[:], self.tms_d.partition_broadcast(128), [], [self.TMS], self.TMS)
        self.cp("dve", self.BONES[:], self.cs("bones"), [self.CST], [self.BONES])
        for l in range(L):
            o = l * 16
            self.act(self.DER[:, o:o + 6], self.TMS[:, l * NTM + 640:l * NTM + 646], AF.Exp, [self.TMS], [self.DER])
            self.ts("dve", self.DER[:, o:o + 6], self.DER[:, o:o + 6], -1.0, None, ALU.mult, None, [self.DER], [self.DER])
            for ch in range(3):
                lam = self.FMS[:, l * 60 + ch * 8 + 7:l * 60 + ch * 8 + 8]
                self.act(self.DER[:, o + 6 + ch:o + 7 + ch], lam, AF.Exp, [self.FMS], [self.DER], scale=-1.0)
            self.act(self.DER[:, o + 6:o + 9], self.DER[:, o + 6:o + 9], AF.Ln, [self.DER], [self.DER], bias=1.0)
            self.ts("dve", self.DER[:, o + 9:o + 12], self.DER[:, o + 6:o + 9], -16.0, None, ALU.mult, None, [self.DER], [self.DER])
            self.ts("dve", self.DER[:, o + 6:o + 9], self.DER[:, o + 6:o + 9], -8.0, None, ALU.mult, None, [self.DER], [self.DER])
            for b in (self.SR[l], self.SG[l], self.HS[l], self.HL[l], self.HG[l]):
                c.op("pool", lambda e, b=b: e.memset(b[:], 0.0), [], [b])
            c.op("pool", lambda e, b=self.SGB[l]: e.memset(b[:], 0.0), [], [self.SGB[l]])
        with contextlib.ExitStack() as es:
            st_f = [c.sbuf(f"cvf{i}", [128, SLAB], F32, es) for i in range(2)]
            st_b = [c.sbuf(f"cvb{i}", [128, SLAB], BF16, es) for i in range(2)]
            engs = ("dve", "pool", "act")
            k = 0
            for l in range(L):
                for s in range(NSLAB):
                    f, b = st_f[k % 2], st_b[k % 2]
                    c.dma(f[:], self.wb_d[l, :, s * SLAB:(s + 1) * SLAB], [], [f], f)
                    self.cp(engs[k % 3], b[:], f[:], [f], [b])
                    c.dma(self.wsc[l, :, s * SLAB:(s + 1) * SLAB], b[:], [b], [self.wsc], b)
                    k += 1
            for e in ("sp", "pe", "act", "dve", "pool"):
                c.wait_all(e, st_f + st_b)
        for l in range(L):
            off, kc, ncols = UNITS["lruw"]
            c.dma(self.LRUW[l][:], self.wsc[l, :, off:off + 768], [self.wsc], [self.LRUW[l]], self.LRUW[l])

    def load_wa(self, l, name):
        off, kc, ncols = UNITS[name]
        b = self.WA[self.wai % len(self.WA)]
        self.wai += 1
        n = kc * ncols
        self.c.dma(b[:, 0:n], self.wsc[l, :, off:off + n], [self.wsc], [b], b)
        return b, b[:, 0:n].rearrange("p (k n) -> p k n", k=kc)

    def load_wd(self, l, name):
        off, kc, ncols = UNITS[name]
        b = self.WD[self.wdi % len(self.WD)]
        self.wdi += 1
        n = kc * ncols
        self.c.dma(b[:, 0:n], self.wsc[l, :, off:off + n], [self.wsc], [b], b)
        return b, b[:, 0:n].rearrange("p (k n) -> p k n", k=kc)

    def make_xT(self):
        for kc in range(8):
            ps = self.ps()
            for s in range(NS):
                self.tr(ps, ps[:, s * 128:(s + 1) * 128], self.X[s][:, kc * 128:(kc + 1) * 128], [self.X[s]])
            self.cp("act" if kc % 2 else "dve", self.XT[kc][:], ps[:], [ps], [self.XT[kc]])

    def hT(self, fc):
        b = self.POOL[fc // 2]
        return b, b[:].bitcast(BF16)[:, (fc % 2) * 512:(fc % 2 + 1) * 512]

    def ffn(self, l, f, alpha):
        sg = [self.POOL[11], self.POOL[12]]
        for u in range(11):
            wb, wv = self.load_wa(l, f"{f}_gu{u}")
            for cc in range(2):
                fc = 2 * u + cc
                psg, psu = self.ps(), self.ps()
                for kc in range(8):
                    self.mm(psg, psg[:], wv[:, kc, cc * 256:cc * 256 + 128], self.XT[kc][:], [wb, self.XT[kc]], kc == 0, kc == 7)
                for kc in range(8):
                    self.mm(psu, psu[:], wv[:, kc, cc * 256 + 128:cc * 256 + 256], self.XT[kc][:], [wb, self.XT[kc]], kc == 0, kc == 7)
                sb = sg[fc % 2]
                self.act(sb[:], psg[:], AF.Silu, [psg], [sb])
                hb, hv = self.hT(fc)
                self.stt(hv, psu[:], 0.5, sb[:], ALU.mult, ALU.mult, [psu, sb], [hb])
        for j in range(4):
            wb, wv = self.load_wd(l, f"{f}_d{j}")
            for s in range(NS):
                ps = self.ps()
                for fc in range(NFC):
                    hb, hv = self.hT(fc)
                    self.mm(ps, ps[:, 0:256], hv[:, s * 128:(s + 1) * 128], wv[:, fc, :], [wb, hb], fc == 0, fc == NFC - 1)
                xs = self.X[s][:, j * 256:(j + 1) * 256]
                self.stt(xs, xs, alpha, ps[:, 0:256], ALU.mult, ALU.add, [self.X[s], ps], [self.X[s]])

    def ln(self, l, w, with_xT=True):
        c = self.c
        LB = [self.POOL[16 + k] for k in range(4)]
        for k in range(4):
            c.dma(LB[k][:], self.lnp_d[l * 3 + w][:, k * 512:(k + 1) * 512].partition_broadcast(128), [], [LB[k]], LB[k])
        run_interleaved([self.ln_chain(s, LB, with_xT) for s in range(NS)])

    def ln_chain(self, s, LB, with_xT):
        c = self.c
        x = self.X[s]
        st = self.LNS[s]
        c.op("dve", lambda e: e.bn_stats(out=st[:, 0:6], in_=x[:, 0:512]), [x], [st])
        c.op("dve", lambda e: e.bn_stats(out=st[:, 6:12], in_=x[:, 512:1024]), [x], [st])
        yield
        c.op("dve", lambda e: e.bn_aggr(out=st[:, 12:14], in_=st[:, 0:12]), [st], [st])
        yield
        self.act(st[:, 14:15], st[:, 13:14], AF.Sqrt, [st], [st], bias=LN_EPS)
        yield
        c.op("dve", lambda e: e.reciprocal(out=st[:, 15:16], in_=st[:, 14:15]), [st], [st])
        yield
        self.stt(st[:, 16:17], st[:, 12:13], -1.0, st[:, 15:16], ALU.mult, ALU.mult, [st], [st])
        yield
        self.act(x[:], x[:], AF.Identity, [x, st], [x], bias=st[:, 16:17], scale=st[:, 15:16])
        yield
        self.tt("dve", x[:, 0:512], x[:, 0:512], LB[0][:], ALU.mult, [x, LB[0]], [x])
        self.tt("pool", x[:, 512:1024], x[:, 512:1024], LB[1][:], ALU.mult, [x, LB[1]], [x])
        yield
        self.tt("dve", x[:, 0:512], x[:, 0:512], LB[2][:], ALU.add, [x, LB[2]], [x])
        self.tt("pool", x[:, 512:1024], x[:, 512:1024], LB[3][:], ALU.add, [x, LB[3]], [x])
        yield
        if not with_xT:
            return
        for j in range(2):
            ps = self.ps()
            for k in range(4):
                kc = 4 * j + k
                self.tr(ps, ps[:, k * 128:(k + 1) * 128], x[:, kc * 128:(kc + 1) * 128], [x])
            yield
            for k in range(4):
                kc = 4 * j + k
                self.cp("act" if k % 2 else "dve", self.XT[kc][:, s * 128:(s + 1) * 128], ps[:, k * 128:(k + 1) * 128], [ps], [self.XT[kc]])
            yield

    def dump(self, name, t, src=None):
        if name not in self.dbg_d:
            return
        src = src or self.X
        dd = self.dbg_d[name]
        for s in range(NS):
            self.c.dma(dd[t * T + s * 128:t * T + (s + 1) * 128, :], src[s][:], [src[s]], [], src[s])

    def rotary_tables(self, t):
        c = self.c
        pi_b = self.POOL[0]
        pf, ang, kf, r = self.POOL[1], self.POOL[2], self.POOL[3], self.POOL[4]
        m = self.POOL[5]
        piv = pi_b[:].bitcast(I32)
        c.dma(piv, self.pos_d[:, t * T:(t + 1) * T].partition_broadcast(128), [], [pi_b], pi_b)
        self.cp("dve", pf[:], piv, [pi_b], [pf])
        self.ts("dve", ang[:], pf[:], self.cs("invf"), None, ALU.mult, None, [pf, self.CST], [ang])
        kiv = pi_b[:].bitcast(I32)
        self.ts("dve", kiv, ang[:], 1.0 / TWO_PI, None, ALU.mult, None, [ang], [pi_b])
        self.cp("dve", kf[:], kiv, [pi_b], [kf])
        C1 = float(np.float32(6.28125))
        C2 = float(TWO_PI - 6.28125)
        self.stt(r[:], kf[:], -C1, ang[:], ALU.mult, ALU.add, [kf, ang], [r])
        self.stt(r[:], kf[:], -C2, r[:], ALU.mult, ALU.add, [kf, r], [r])
        for dst, shift, sgn in ((self.SIN, 0.0, True), (self.COS, math.pi / 2, False)):
            rr = dst
            self.ts("dve", rr[:], r[:], shift, None, ALU.add, None, [r], [rr])
            self.ts("dve", m[:], rr[:], math.pi, None, ALU.is_gt, None, [rr], [m])
            self.stt(rr[:], m[:], -TWO_PI, rr[:], ALU.mult, ALU.add, [m, rr], [rr])
            self.ts("dve", m[:], rr[:], -math.pi, None, ALU.is_lt, None, [rr], [m])
            self.stt(rr[:], m[:], TWO_PI, rr[:], ALU.mult, ALU.add, [m, rr], [rr])
            self.ts("dve", rr[:], rr[:], math.pi, -math.pi, ALU.min, ALU.max, [rr], [rr])
            self.act(rr[:], rr[:], AF.Sin, [rr], [rr])
            if sgn:
                self.ts("dve", rr[:], rr[:], self.cs("sgn"), None, ALU.mult, None, [rr, self.CST], [rr])

    def fm_mm(self, l, i, ps):
        u = i // 4
        if self.fm_cur is None or self.fm_cur[0] != (l, u):
            wb, wv = self.load_wa(l, f"fm{u}")
            self.fm_cur = ((l, u), wb, wv)
        _, wb, wv = self.fm_cur
        o = (i % 4) * 128
        for kc in range(8):
            self.mm(ps, ps[:], wv[:, kc, o:o + 128], self.XT[kc][:], [wb, self.XT[kc]], kc == 0, kc == 7)

    def mixer(self, l, t):
        self.fm_cur = None
        self.retention(l)
        if self.upto("lru"):
            self.lru(l)
        if self.upto("gdn"):
            self.gdn(l)
        if not self.upto("wout"):
            return
        for j in range(2):
            wb, wv = self.load_wa(l, f"wo{j}")
            for s in range(NS):
                ps = self.ps()
                for k in range(8):
                    self.mm(ps, ps[:], self.MIXT[k][:, s * 128:(s + 1) * 128], wv[:, k, :], [wb, self.MIXT[k]], k == 0, k == 7)
                xs = self.X[s][:, j * 512:(j + 1) * 512]
                self.stt(xs, xs, ALPHA, ps[:], ALU.mult, ALU.add, [self.X[s], ps], [self.X[s]])

    def retention(self, l):
        c = self.c
        P = self.POOL
        QK, KK, QC, KTM, VTM, VTT = P[2], P[3], P[4], P[5], P[6], P[7]
        GS = [P[8], P[9]]
        SRB = P[10]
        bufs = dict(QK=QK, KK=KK, QC=QC, KTM=KTM, VTM=VTM, VTT=VTT, GS=GS, SRB=SRB)
        run_interleaved([self.ret_q(l, c2, P[0 + c2], P[11 + c2], bufs) for c2 in range(2)])
        wb, wv = self.load_wa(l, "tmA")
        self.psi = 0
        run_interleaved([self.ret_k(l, c2, P[13 + 3 * c2], P[14 + 3 * c2], P[15 + 3 * c2], bufs) for c2 in range(2)]
                        + [self.ret_vg(l, s, wb, wv, bufs) for s in range(NS)])
        ktm = KTM[:].bitcast(BF16); vtt = VTT[:].bitcast(BF16); srb = SRB[:].bitcast(BF16)
        SR = self.SR[l]
        pk = [self.ps(), self.ps()]
        for n in range(8):
            s, cc = n // 2, n % 2
            pc = slice(cc * 64, cc * 64 + 64)
            pb = pk[n // 4]
            o = (n % 4) * 128
            for h in range(4):
                hp, c2 = (h % 2) * 64, h // 2
                self.mm(pb, pb[hp:hp + 64, o + c2 * 64:o + c2 * 64 + 64], ktm[pc, s * 256 + h * 64:s * 256 + h * 64 + 64],
                        vtt[pc, s * 256 + h * 64:s * 256 + h * 64 + 64], [KTM, VTT])
        for n in range(8):
            pb = pk[n // 4]
            o = (n % 4) * 128
            self.cp("dve", srb[:, n * 128:(n + 1) * 128], SR[:], [SR], [SRB])
            for c2 in range(2):
                cd = self.cs("cdcol")[:, c2:c2 + 1]
                self.stt(SR[:, c2 * 64:c2 * 64 + 64], SR[:, c2 * 64:c2 * 64 + 64], cd, pb[:, o + c2 * 64:o + c2 * 64 + 64],
                         ALU.mult, ALU.add, [SR, self.CST, pb], [SR])
        self.psi = 0
        self.ps_n = 6
        Q = [P[0], P[1], P[11], P[12]]
        R = [P[13], P[14], P[15], P[16]]
        run_interleaved([self.ret_out(l, s, Q[s], R[s], self.PS[6 + s // 2], bufs) for s in range(NS)])
        self.ps_n = 8

    def ret_q(self, l, c2, ta, tb, B):
        QK, QC = B["QK"], B["QC"]
        qr = QK[:].bitcast(BF16); qc = QC[:].bitcast(BF16)
        cross = self.cs("cross")
        psq, psqs = self.ps(), self.ps()
        self.fm_mm(l, 0 + c2, psq)
        self.fm_mm(l, 2 + c2, psqs)
        yield
        self.stt(ta[:], psq[:], 0.125, self.COS[:], ALU.mult, ALU.mult, [psq, self.COS], [ta])
        self.stt(tb[:], psqs[:], 0.125, self.SIN[:], ALU.mult, ALU.mult, [psqs, self.SIN], [tb])
        yield
        self.tt("pool", qr[:, c2 * 512:(c2 + 1) * 512], ta[:], tb[:], ALU.add, [ta, tb], [QK])
        yield
        self.tt("pool", qc[:, c2 * 512:(c2 + 1) * 512], qr[:, c2 * 512:(c2 + 1) * 512], cross[:, c2 * 512:(c2 + 1) * 512],
                ALU.mult, [QK, self.CST], [QC])

    def ret_k(self, l, c2, ta, tb, KF, B):
        KK, KTM = B["KK"], B["KTM"]
        kr = KK[:].bitcast(BF16); ktm = KTM[:].bitcast(BF16)
        psk, psks = self.ps(), self.ps()
        self.fm_mm(l, 4 + c2, psk)
        self.fm_mm(l, 6 + c2, psks)
        yield
        self.tt("dve", ta[:], psk[:], self.COS[:], ALU.mult, [psk, self.COS], [ta])
        self.tt("dve", tb[:], psks[:], self.SIN[:], ALU.mult, [psks, self.SIN], [tb])
        yield
        self.tt("pool", KF[:], ta[:], tb[:], ALU.add, [ta, tb], [KF])
        yield
        self.cp("pool", kr[:, c2 * 512:(c2 + 1) * 512], KF[:], [KF], [KK])
        ps = self.ps()
        for s in range(NS):
            self.tr(ps, ps[:, s * 128:(s + 1) * 128], KF[:, s * 128:(s + 1) * 128], [KF])
        yield
        for s in range(NS):
            self.cp("act", ktm[:, s * 256 + c2 * 128:s * 256 + (c2 + 1) * 128], ps[:, s * 128:(s + 1) * 128], [ps], [KTM])

    def ret_vg(self, l, s, wb, wv, B):
        VTM, VTT, GS = B["VTM"], B["VTT"], B["GS"]
        vtm = VTM[:].bitcast(BF16); vtt = VTT[:].bitcast(BF16)
        ps = self.ps()
        for kc in range(8):
            self.mm(ps, ps[:], self.XT[kc][:, s * 128:(s + 1) * 128], wv[:, kc, :], [wb, self.XT[kc]], kc == 0, kc == 7)
        yield
        self.cp("dve", vtm[:, s * 256:(s + 1) * 256], ps[:, 0:256], [ps], [VTM])
        self.tt("dve", vtt[:, s * 256:(s + 1) * 256], ps[:, 0:256], self.cs("tailf"), ALU.mult, [ps, self.CST], [VTT])
        yield
        g = GS[s // 2]
        self.act(g[:, (s % 2) * 256:(s % 2 + 1) * 256], ps[:, 256:512], AF.Silu, [ps], [g])

    def ret_out(self, l, s, Qb, Rb, pso, B):
        c = self.c
        QK, KK, QC, VTM, GS, SRB = B["QK"], B["KK"], B["QC"], B["VTM"], B["GS"], B["SRB"]
        qr = QK[:].bitcast(BF16); kr = KK[:].bitcast(BF16); qc = QC[:].bitcast(BF16)
        vtm = VTM[:].bitcast(BF16); srb = SRB[:].bitcast(BF16)
        ptall = Rb[:].bitcast(BF16)
        intra = self.cs("intraT")
        rng = self.TMS[:, l * NTM:l * NTM + 256]
        po = (s % 2) * 256
        st = self.RST[s]
        sq, on = Qb[:, 0:256], Qb[:, 256:512]
        for h in range(4):
            hp, c2 = (h % 2) * 64, h // 2
            pst = self.ps()
            self.mm(pst, pst[:, 0:128], kr[hp:hp + 64, c2 * 512 + s * 128:c2 * 512 + (s + 1) * 128],
                    qr[hp:hp + 64, c2 * 512 + s * 128:c2 * 512 + (s + 1) * 128], [KK, QK])
            yield
            ptv = ptall[:, h * 128:(h + 1) * 128]
            self.tt("dve", ptv, pst[:, 0:128], intra[:, h * 128:(h + 1) * 128], ALU.mult, [pst, self.CST], [Rb])
            yield
            self.mm(pso, pso[:, po + h * 64:po + h * 64 + 64], ptv, vtm[:, s * 256 + h * 64:s * 256 + h * 64 + 64], [Rb, VTM], True, False)
            for cc in range(2):
                pc = slice(cc * 64, cc * 64 + 64)
                n = 2 * s + cc
                self.mm(pso, pso[pc, po + h * 64:po + h * 64 + 64],
                        qc[hp:hp + 64, c2 * 512 + s * 128 + cc * 64:c2 * 512 + s * 128 + cc * 64 + 64],
                        srb[hp:hp + 64, n * 128 + c2 * 64:n * 128 + c2 * 64 + 64], [QC, SRB], False, cc == 1)
            yield
        po_ap = pso[:, po:po + 256]
        c.op("dve", lambda e: e.reduce_sum(out=st[:, 0:4], in_=po_ap.rearrange("p (h e) -> p h e", h=4), axis=AX.X), [pso], [st])
        yield
        self.act(sq, po_ap, AF.Square, [pso], [Qb])
        yield
        c.op("dve", lambda e: e.reduce_sum(out=st[:, 4:8], in_=sq.rearrange("p (h e) -> p h e", h=4), axis=AX.X), [Qb], [st])
        yield
        self.ts("dve", st[:, 8:12], st[:, 0:4], 1.0 / 64, None, ALU.mult, None, [st], [st])
        yield
        self.tt("dve", st[:, 12:16], st[:, 8:12], st[:, 8:12], ALU.mult, [st], [st])
        yield
        self.stt(st[:, 16:20], st[:, 4:8], 1.0 / 64, st[:, 12:16], ALU.mult, ALU.subtract, [st], [st])
        yield
        self.act(st[:, 20:24], st[:, 16:20], AF.Sqrt, [st], [st], bias=1e-5)
        yield
        c.op("dve", lambda e: e.reciprocal(out=st[:, 24:28], in_=st[:, 20:24]), [st], [st])
        yield
        for h in range(4):
            self.ts("dve", on[:, h * 64:(h + 1) * 64], pso[:, po + h * 64:po + (h + 1) * 64], st[:, 8 + h:9 + h], st[:, 24 + h:25 + h],
                    ALU.subtract, ALU.mult, [pso, st], [Qb])
        yield
        self.tt("pool", on, on, rng, ALU.mult, [Qb, self.TMS], [Qb])
        yield
        g = GS[s // 2]
        self.tt("pool", on, on, g[:, (s % 2) * 256:(s % 2 + 1) * 256], ALU.mult, [Qb, g], [Qb])
        yield
        pst = self.ps()
        for k in range(2):
            self.tr(pst, pst[:, k * 128:(k + 1) * 128], on[:, k * 128:(k + 1) * 128], [Qb])
        yield
        for k in range(2):
            self.cp("act", self.MIXT[k][:, s * 128:(s + 1) * 128], pst[:, k * 128:(k + 1) * 128], [pst], [self.MIXT[k]])

    def conv_chunk(self, l, ps, hist, hcol, wcol, bias_col, out_b):
        CB = self.cbuf[self.cbi % 2]
        self.cbi += 1
        self.cp("pool", CB[:, 0:3], hist[:, hcol:hcol + 3], [hist], [CB])
        self.cp("act", CB[:, 3:515], ps[:], [ps], [CB])
        self.cp("pool", hist[:, hcol:hcol + 3], CB[:, 512:515], [CB], [hist])
        w = self.FMS
        if bias_col is None:
            self.ts("dve", out_b[:, 0:512], CB[:, 3:515], w[:, wcol + 3:wcol + 4], None, ALU.mult, None, [CB, w], [out_b])
        else:
            self.ts("dve", out_b[:, 0:512], CB[:, 3:515], w[:, wcol + 3:wcol + 4], w[:, bias_col:bias_col + 1], ALU.mult, ALU.add, [CB, w], [out_b])
        for k in range(3):
            self.stt(out_b[:, 0:512], CB[:, k:k + 512], w[:, wcol + k:wcol + k + 1], out_b[:, 0:512], ALU.mult, ALU.add, [CB, w, out_b], [out_b])

    def lru(self, l):
        P = self.POOL
        gens = [self.lru_chunk(l, ch, [P[7 * ch + i] for i in range(7)], self.cbuf[ch]) for ch in range(3)]
        run_interleaved(gens)

    def lru_chunk(self, l, ch, B, CB):
        c = self.c
        XC, XCB, RA, IG, B2, HB, G1 = B
        xcb = XCB[:].bitcast(BF16)[:, 0:512]
        lw, der, fo, w = self.LRUW[l], self.DER, l * 60, self.FMS
        hist, hcol, wcol, bcol = self.HL[l], ch * 3, fo + ch * 8, fo + ch * 8 + 4
        hs = self.HS[l]
        ps = self.ps()
        self.fm_mm(l, 8 + ch, ps)
        self.cp("pool", CB[:, 0:3], hist[:, hcol:hcol + 3], [hist], [CB])
        yield
        self.cp("act", CB[:, 3:515], ps[:], [ps], [CB])
        psg = self.ps()
        self.fm_mm(l, 11 + ch, psg)
        yield
        self.cp("pool", hist[:, hcol:hcol + 3], CB[:, 512:515], [CB], [hist])
        self.ts("dve", XC[:], CB[:, 3:515], w[:, wcol + 3:wcol + 4], w[:, bcol:bcol + 1], ALU.mult, ALU.add, [CB, w], [XC])
        self.act(G1[:], psg[:], AF.Square, [psg], [G1])
        yield
        self.stt(XC[:], CB[:, 0:512], w[:, wcol:wcol + 1], XC[:], ALU.mult, ALU.add, [CB, w, XC], [XC])
        self.ts("dve", G1[:], G1[:], 0.044715, 1.0, ALU.mult, ALU.add, [G1], [G1])
        yield
        self.stt(XC[:], CB[:, 1:513], w[:, wcol + 1:wcol + 2], XC[:], ALU.mult, ALU.add, [CB, w, XC], [XC])
        self.tt("dve", G1[:], G1[:], psg[:], ALU.mult, [G1, psg], [G1])
        yield
        self.stt(XC[:], CB[:, 2:514], w[:, wcol + 2:wcol + 3], XC[:], ALU.mult, ALU.add, [CB, w, XC], [XC])
        self.act(G1[:], G1[:], AF.Sigmoid, [G1], [G1], scale=1.5957691216057308)
        yield
        self.cp("pool", xcb, XC[:], [XC], [XCB])
        self.tt("dve", G1[:], G1[:], psg[:], ALU.mult, [G1, psg], [G1])
        yield
        psr, psi = self.ps(), self.ps()
        self.mm(psr, psr[:], lw[:, (ch * 2) * 128:(ch * 2 + 1) * 128], xcb, [lw, XCB])
        self.mm(psi, psi[:], lw[:, (ch * 2 + 1) * 128:(ch * 2 + 2) * 128], xcb, [lw, XCB])
        yield
        self.act(RA[:], psr[:], AF.Sigmoid, [psr, w], [RA], bias=w[:, fo + ch * 8 + 5:fo + ch * 8 + 6])
        self.act(IG[:], psi[:], AF.Sigmoid, [psi, w], [IG], bias=w[:, fo + ch * 8 + 6:fo + ch * 8 + 7])
        yield
        self.act(RA[:], RA[:], AF.Exp, [RA, der], [RA], scale=der[:, l * 16 + 6 + ch:l * 16 + 7 + ch])
        self.tt("pool", IG[:], IG[:], XC[:], ALU.mult, [IG, XC], [IG])
        yield
        self.tt("dve", B2[:], RA[:], RA[:], ALU.mult, [RA], [B2])
        yield
        self.ts("dve", B2[:], B2[:], -1.0, 1.0, ALU.mult, ALU.add, [B2], [B2])
        yield
        self.act(B2[:], B2[:], AF.Sqrt, [B2], [B2])
        yield
        self.tt("dve", B2[:], B2[:], IG[:], ALU.mult, [B2, IG], [B2])
        yield
        c.op("dve", lambda e: e.tensor_tensor_scan(out=HB[:], data0=RA[:], data1=B2[:], initial=hs[:, ch:ch + 1],
                                                   op0=ALU.mult, op1=ALU.add), [RA, B2, hs], [HB])
        yield
        self.cp("dve", hs[:, ch:ch + 1], HB[:, 511:512], [HB], [hs])
        self.tt("pool", self.MIXT[2 + ch][:], G1[:], HB[:], ALU.mult, [G1, HB], [self.MIXT[2 + ch]])

    def gdn(self, l):
        c = self.c
        P = self.POOL
        fo = l * 60
        to = l * NTM
        QT = [P[0], P[1], P[2]]
        KT = QT
        KBT = [P[3], P[4], P[5]]
        qt = [b[:].bitcast(BF16)[:, 0:512] for b in QT]
        kt = [b[:].bitcast(BF16)[:, 512:1024] for b in KT]
        kbt = [b[:].bitcast(BF16)[:, 0:512] for b in KBT]
        KTM = [P[6], P[7], P[8], P[9]]
        VTM = [P[10], P[11], P[12], P[13]]
        ZS = [P[14], P[15], P[16], P[17]]
        SGT = P[22]
        AB = P[23]
        for grp in ((0, 1, 2), (3, 4, 5), (6, 7, 8)):
            gens = [self.gdn_front_chunk(l, ch, [P[14 + 3 * (ch % 3) + i] for i in range(3)], self.cbuf[ch % 3],
                                         qt, kt, QT, KT, KTM, VTM) for ch in grp]
            run_interleaved(gens)
        wb, wv = self.load_wa(l, "ba")
        ps = self.ps()
        for kc in range(8):
            self.mm(ps, ps[0:12, :], wv[:, kc, 0:12], self.XT[kc][:], [wb, self.XT[kc]], kc == 0, kc == 7)
        self.act(SGT[0:12, :], ps[0:12, :], AF.Sigmoid, [ps], [SGT])
        selb = self.cs("selb", slice(0, 12))
        for c3 in range(3):
            ps = self.ps()
            self.mm(ps, ps[:], selb[:, c3 * 128:(c3 + 1) * 128], SGT[0:12, :], [self.CST, SGT])
            self.tt("dve", kbt[c3], kt[c3], ps[:], ALU.mult, [KT[c3], ps], [KBT[c3]])
        wb, wv = self.load_wa(l, "tmB")
        ab = AB[:, 0:48].rearrange("p (s k) -> p s k", s=NS)
        for s in range(NS):
            ps = self.ps()
            for kc in range(8):
                self.mm(ps, ps[:, 0:396], self.XT[kc][:, s * 128:(s + 1) * 128], wv[:, kc, :], [wb, self.XT[kc]], kc == 0, kc == 7)
            self.act(ZS[s][:, 0:384], ps[:, 0:384], AF.Silu, [ps], [ZS[s]])
            self.cp("dve", AB[:, s * 12:(s + 1) * 12], ps[:, 384:396], [ps], [AB])
        def v46(o):
            return AB[:, o:o + 24].rearrange("p (s k) -> p s k", s=NS)
        a_v, b_v = ab[:, :, 0:6], ab[:, :, 6:12]
        Z1, AZ, EL, G, BETA = v46(48), v46(72), v46(96), v46(120), v46(144)
        dtb = self.TMS[:, to + 646:to + 652].unsqueeze(1).to_broadcast([128, NS, 6])
        nea = self.DER[:, l * 16:l * 16 + 6].unsqueeze(1).to_broadcast([128, NS, 6])
        self.tt("dve", Z1, a_v, dtb, ALU.add, [AB, self.TMS], [AB])
        self.stt(AZ, Z1, -1.0, Z1, ALU.mult, ALU.max, [AB], [AB])
        self.act(EL, AZ, AF.Exp, [AB], [AB], scale=-1.0)
        self.act(EL, EL, AF.Ln, [AB], [AB], bias=1.0)
        self.stt(G, Z1, 0.0, EL, ALU.max, ALU.add, [AB], [AB])
        self.tt("dve", G, G, nea, ALU.mult, [AB, self.DER], [AB])
        self.act(BETA, b_v, AF.Sigmoid, [AB], [AB])
        gng = self.TMS[:, to + 256:to + 640]
        for s in range(NS):
            self.gdn_pre(l, s, AB)
        prev_post = None
        for s in range(NS):
            gens = self.gdn_subtile_gens(l, s, qt, kt, kbt, QT, KT, KBT, KTM[s], VTM[s], AB)
            if prev_post is not None:
                gens.append(prev_post)
            run_interleaved(gens)
            prev_post = self.gdn_post(l, s, ZS[s], gng)
        run_interleaved([prev_post])

    def gdn_front_chunk(self, l, ch, B, CB, qt, kt, QT, KT, KTM, VTM):
        c = self.c
        CV, SQ, RN = B
        sqb = SQ[:].bitcast(BF16)[:, 0:512]
        fo, w = l * 60, self.FMS
        hist, hcol, wcol = self.HG[l], ch * 3, fo + 24 + ch * 4
        kind, c3 = ch // 3, ch % 3
        ps = self.ps()
        self.fm_mm(l, 14 + ch, ps)
        self.cp("pool", CB[:, 0:3], hist[:, hcol:hcol + 3], [hist], [CB])
        yield
        self.cp("act", CB[:, 3:515], ps[:], [ps], [CB])
        yield
        self.cp("pool", hist[:, hcol:hcol + 3], CB[:, 512:515], [CB], [hist])
        self.ts("dve", CV[:], CB[:, 3:515], w[:, wcol + 3:wcol + 4], None, ALU.mult, None, [CB, w], [CV])
        yield
        for k in range(3):
            self.stt(CV[:], CB[:, k:k + 512], w[:, wcol + k:wcol + k + 1], CV[:], ALU.mult, ALU.add, [CB, w, CV], [CV])
            yield
        self.act(CV[:], CV[:], AF.Silu, [CV], [CV])
        yield
        if kind == 2:
            pst = self.ps()
            for s in range(NS):
                self.tr(pst, pst[:, s * 128:(s + 1) * 128], CV[:, s * 128:(s + 1) * 128], [CV])
            yield
            for s in range(NS):
                self.cp("act" if s % 2 else "dve", VTM[s][:, c3 * 128:(c3 + 1) * 128], pst[:, s * 128:(s + 1) * 128], [pst], [VTM[s]])
            return
        self.tt("pool", sqb, CV[:], CV[:], ALU.mult, [CV], [SQ])
        yield
        pss = self.ps()
        self.mm(pss, pss[:], self.BONES[:], sqb, [self.BONES, SQ])
        yield
        self.act(RN[:], pss[:], AF.Sqrt, [pss], [RN], bias=1e-6)
        yield
        c.op("dve", lambda e: e.reciprocal(out=RN[:], in_=RN[:]), [RN], [RN])
        yield
        if kind == 0:
            self.stt(qt[c3], CV[:], 0.125, RN[:], ALU.mult, ALU.mult, [CV, RN], [QT[c3]])
            return
        self.tt("dve", CV[:], CV[:], RN[:], ALU.mult, [CV, RN], [CV])
        yield
        self.cp("pool", kt[c3], CV[:], [CV], [KT[c3]])
        pst = self.ps()
        for s in range(NS):
            self.tr(pst, pst[:, s * 128:(s + 1) * 128], CV[:, s * 128:(s + 1) * 128], [CV])
        yield
        for s in range(NS):
            self.cp("act" if s % 2 else "dve", KTM[s][:, c3 * 128:(c3 + 1) * 128], pst[:, s * 128:(s + 1) * 128], [pst], [KTM[s]])

    def gdn_pre(self, l, s, AB):
        SC = self.GSC[s]
        GCT = self.GCTB[s]
        g_s = AB[:, 120 + s * 6:120 + s * 6 + 6]
        beta_s = AB[:, 144 + s * 6:144 + s * 6 + 6]
        GC, GLB, EKT, EGC, BG, EGL0, EGL1 = (SC[:, i * 6:(i + 1) * 6] for i in range(7))
        ps = self.ps()
        self.mm(ps, ps[:, 0:6], self.cs("triU"), g_s, [self.CST, AB])
        self.mm(ps, ps[0:6, 128:256], g_s, self.cs("triU"), [self.CST, AB])
        self.cp("dve", GC, ps[:, 0:6], [ps], [SC])
        self.cp("dve", GCT[0:6, :], ps[0:6, 128:256], [ps], [GCT])
        ps = self.ps()
        self.mm(ps, ps[:, 0:6], self.cs("selB"), GC, [self.CST, SC])
        self.mm(ps, ps[:, 8:14], self.cs("sel63"), GC, [self.CST, SC])
        self.mm(ps, ps[:, 16:22], self.cs("sel127"), GC, [self.CST, SC])
        self.tt("dve", EKT, ps[:, 0:6], GC, ALU.subtract, [ps, SC], [SC])
        self.cp("dve", GLB, ps[:, 8:14], [ps], [SC])
        self.cp("dve", SC[:, 48:54], ps[:, 16:22], [ps], [SC])
        self.act(EKT, EKT, AF.Exp, [SC], [SC])
        self.act(EGC, GC, AF.Exp, [SC], [SC])
        self.act(EGL0, GLB, AF.Exp, [SC], [SC])
        self.act(EGL1, SC[:, 48:54], AF.Exp, [SC], [SC])
        self.tt("dve", BG, beta_s, EGC, ALU.mult, [AB, SC], [SC])

    def gdn_subtile_gens(self, l, s, qt, kt, kbt, QT, KT, KBT, KTM, VTM, AB):
        SC = self.GSC[s]
        GCT = self.GCTB[s]
        GV = self.GVB[s % 2]
        beta_s = AB[:, 144 + s * 6:144 + s * 6 + 6]
        GC, GLB, EKT, EGC, BG, EGL0, EGL1 = (SC[:, i * 6:(i + 1) * 6] for i in range(7))

        def bc(v):
            return v.unsqueeze(2).to_broadcast([128, 6, 64])

        def v3(ap):
            return ap.rearrange("p (h e) -> p h e", h=6)
        self.tt("dve", v3(GV[:, 0:384]), v3(VTM[:, 0:384]), bc(beta_s), ALU.mult, [VTM, AB], [GV])
        self.tt("dve", v3(GV[:, 384:768]), v3(KTM[:, 0:384]), bc(BG), ALU.mult, [KTM, SC], [GV])
        self.tt("dve", v3(GV[:, 768:1152]), v3(KTM[:, 0:384]), bc(EKT), ALU.mult, [KTM, SC], [GV])
        OG = self.GOG[s % 2]
        return [self.gdn_head(l, s, h, qt, kt, kbt, QT, KT, KBT, GV, AB, SC, GCT, OG, (EGL0, EGL1)) for h in range(6)]

    def gdn_post(self, l, s, ZS, gng):
        c = self.c
        OG = self.GOG[s % 2]
        st = self.sm()
        SQ = self.POOL[19]
        self.act(SQ[:, 0:384], OG[:, 0:384], AF.Square, [OG], [SQ])
        yield
        c.op("dve", lambda e: e.reduce_sum(out=st[:, 0:6], in_=SQ[:, 0:384].rearrange("p (h e) -> p h e", h=6), axis=AX.X), [SQ], [st])
        yield
        self.act(st[:, 8:14], st[:, 0:6], AF.Sqrt, [st], [st], bias=1e-6, scale=1.0 / 64)
        yield
        c.op("dve", lambda e: e.reciprocal(out=st[:, 16:22], in_=st[:, 8:14]), [st], [st])
        yield
        ogv = OG[:, 0:384].rearrange("p (h e) -> p h e", h=6)
        self.tt("dve", ogv, ogv, st[:, 16:22].unsqueeze(2).to_broadcast([128, 6, 64]), ALU.mult, [OG, st], [OG])
        yield
        self.tt("pool", OG[:, 0:384], OG[:, 0:384], gng, ALU.mult, [OG, self.TMS], [OG])
        yield
        self.tt("pool", OG[:, 0:384], OG[:, 0:384], ZS[:, 0:384], ALU.mult, [OG, ZS], [OG])
        yield
        pst = self.ps()
        for k in range(3):
            self.tr(pst, pst[:, k * 128:(k + 1) * 128], OG[:, k * 128:(k + 1) * 128], [OG])
        yield
        for k in range(3):
            self.cp("act" if k % 2 else "dve", self.MIXT[5 + k][:, s * 128:(s + 1) * 128], pst[:, k * 128:(k + 1) * 128], [pst], [self.MIXT[5 + k]])

    def gdn_head(self, l, s, h, qt, kt, kbt, QT, KT, KBT, GV, AB, SC, GCT, OG, EGL):
        cols = slice(s * 128, (s + 1) * 128)
        hp, c3 = (h % 2) * 64, h // 2
        hs = slice(h * 64, (h + 1) * 64)
        beta_s = AB[:, 144 + s * 6:144 + s * 6 + 6]
        GC, GLB, EKT, EGC, BG, EGL0, EGL1 = (SC[:, i * 6:(i + 1) * 6] for i in range(7))
        F, E, M, TTb, NPa, NPb, HV, HU = (self.GH[k][h] for k in ("F", "E", "M", "T", "Na", "Nb", "V", "U"))
        X1, X2 = F[:, 0:128], F[:, 128:256]
        E1s, E2s, E1i = E[:, 0:128], E[:, 128:256], E[:, 256:384]
        P0, N0, QKm = M[:, 0:128], M[:, 128:256], M[:, 256:384]
        Tt = TTb[:, 0:128]
        vb, kbg, ktl = GV[:, h * 64:(h + 1) * 64], GV[:, 384 + h * 64:384 + (h + 1) * 64], GV[:, 768 + h * 64:768 + (h + 1) * 64]
        vnew, wt = HV[:, 0:64], HV[:, 64:192]
        U = HU[:, 0:64]
        SG, SGB = self.SG[l], self.SGB[l]
        psB = self.ps()
        psk = psB
        oh = self.cs("oh6", slice(0, 6))
        self.mm(psB, psB[:, 0:128], oh[:, h * 128:(h + 1) * 128], GCT[0:6, :], [self.CST, GCT])
        ktc, kbc, qtc = kt[c3][hp:hp + 64, cols], kbt[c3][hp:hp + 64, cols], qt[c3][hp:hp + 64, cols]
        self.mm(psk, psk[:, 128:256], ktc, kbc, [KBT[c3], KT[c3]])
        self.mm(psk, psk[:, 256:384], kbc, ktc, [KBT[c3], KT[c3]])
        self.mm(psk, psk[:, 384:512], ktc, qtc, [QT[c3], KT[c3]])
        yield
        gcp = GC[:, h:h + 1]
        self.ts("dve", X1, psB[:, 0:128], gcp, 0.0, ALU.subtract, ALU.min, [psB, SC], [F])
        self.ts("dve", X2, psB[:, 0:128], gcp, 0.0, ALU.subtract, ALU.max, [psB, SC], [F])
        yield
        self.act(X1, X1, AF.Exp, [F], [F])
        self.act(X2, X2, AF.Exp, [F], [F], scale=-1.0)
        yield
        mo = CST["MSu"][0]
        self.tt("pool", E[:, 0:256], F[:, 0:256], self.CST[:, mo:mo + 256], ALU.mult, [F, self.CST], [E])
        self.tt("pool", E1i, X1, self.cs("MI"), ALU.mult, [F, self.CST], [E])
        yield
        self.tt("dve", M[:, 0:384], psk[:, 128:512], E[:, 0:384], ALU.mult, [psk, E], [M])
        yield
        self.tt("pool", Tt, self.cs("ident"), P0, ALU.subtract, [self.CST, M], [TTb])
        Nk, Pk, NB = N0, P0, M
        for j in range(0, 6):
            DB = NPa if j % 2 == 0 else NPb
            Nn, Pn = DB[:, 0:128], DB[:, 128:256]
            psn = self.ps()
            if j <= 4:
                self.mm(psn, psn[:, 0:128], Pk, Nk, [NB])
                if j <= 3:
                    self.mm(psn, psn[:, 128:256], Nk, Pk, [NB])
            if j >= 1:
                self.mm(psn, psn[:, 256:384], Nk, Tt, [NB, TTb])
            yield
            if j <= 4:
                self.cp("act", Nn, psn[:, 0:128], [psn], [DB])
                if j <= 3:
                    self.cp("act", Pn, psn[:, 128:256], [psn], [DB])
            if j >= 1:
                self.tt("dve", Tt, psn[:, 256:384], Tt, ALU.add, [psn, TTb], [TTb])
            if j <= 4:
                Nk, Pk, NB = Nn, Pn, DB
            yield
        psu = self.ps()
        self.mm(psu, psu[:, 0:64], Tt, vb, [TTb, GV])
        self.mm(psu, psu[hp:hp + 64, 128:256], kbg, Tt, [TTb, GV])
        yield
        self.cp("act", U, psu[:, 0:64], [psu], [HU])
        self.cp("act", wt[hp:hp + 64, :], psu[hp:hp + 64, 128:256], [psu], [HV])
        yield
        for cc in range(2):
            pc = slice(cc * 64, cc * 64 + 64)
            sgb = SGB[hp:hp + 64, c3 * 64:(c3 + 1) * 64]
            vn = vnew[pc, :]
            psw = self.ps()
            self.mm(psw, psw[pc, 0:64], wt[hp:hp + 64, cc * 64:cc * 64 + 64], sgb, [HV, SGB])
            self.mm(psw, psw[pc, 64:128], qt[c3][hp:hp + 64, s * 128 + cc * 64:s * 128 + cc * 64 + 64], sgb, [QT[c3], SGB])
            yield
            self.tt("dve", vn, U[pc, :], psw[pc, 0:64], ALU.subtract, [HU, psw], [HV])
            self.act(HU[pc, 64:128], psw[pc, 64:128], AF.Copy, [psw, SC], [HU], scale=EGC[pc, h:h + 1])
            yield
            pso = self.ps()
            self.mm(pso, pso[pc, 0:64], QKm[pc, cc * 64:cc * 64 + 64], vn, [M, HV])
            self.mm(pso, pso[hp:hp + 64, 64:128], ktl[pc, :], vn, [HV, GV])
            yield
            self.tt("dve", OG[pc, hs], HU[pc, 64:128], pso[pc, 0:64], ALU.add, [HU, pso], [OG])
            sgs = SG[hp:hp + 64, c3 * 64:(c3 + 1) * 64]
            self.stt(sgb, sgs, EGL[cc][hp:hp + 64, h:h + 1], pso[hp:hp + 64, 64:128], ALU.mult, ALU.add, [SG, SC, pso], [SGB])
            self.stt(sgs, sgs, EGL[cc][hp:hp + 64, h:h + 1], pso[hp:hp + 64, 64:128], ALU.mult, ALU.add, [SG, SC, pso], [SG])
            yield

    def ple(self, l, t):
        c = self.c
        PTB = [self.POOL[13], self.POOL[14]]
        PTTB = self.POOL[15]
        ptt = PTTB[:].bitcast(BF16)
        for sp in range(2):
            c.dma(PTB[sp][:].rearrange("p (s k) -> p s k", s=2),
                  self.p_d[l, t * T + sp * 256:t * T + (sp + 1) * 256, :].rearrange("(s p) k -> p s k", p=128), [], [PTB[sp]], PTB[sp])
        for k in range(2):
            ps = self.ps()
            for s in range(NS):
                pb = PTB[s // 2]
                self.tr(ps, ps[:, s * 128:(s + 1) * 128], pb[:, (s % 2) * 256 + k * 128:(s % 2) * 256 + (k + 1) * 128], [pb])
            self.cp("act" if k else "dve", ptt[:, k * T:(k + 1) * T], ps[:], [ps], [PTTB])
        wpb, wpv = self.load_wa(l, "pp")
        sgb, tb = self.POOL[11], self.POOL[12]
        for j in range(2):
            wb, wv = self.load_wa(l, f"pg{j}")
            for s in range(NS):
                psg, psp = self.ps(), self.ps()
                for kc in range(8):
                    self.mm(psg, psg[:], self.XT[kc][:, s * 128:(s + 1) * 128], wv[:, kc, :], [wb, self.XT[kc]], kc == 0, kc == 7)
                for k in range(2):
                    self.mm(psp, psp[:], ptt[:, k * T + s * 128:k * T + (s + 1) * 128], wpv[:, k, j * 512:(j + 1) * 512],
                            [wpb, PTTB], k == 0, k == 1)
                self.act(sgb[:], psg[:], AF.Sigmoid, [psg], [sgb])
                self.tt("dve", tb[:], sgb[:], psp[:], ALU.mult, [sgb, psp], [tb])
                xs = self.X[s][:, j * 512:(j + 1) * 512]
                self.stt(xs, xs, ALPHA, tb[:], ALU.mult, ALU.add, [self.X[s], tb], [self.X[s]])

    def build(self):
        c = self.c
        self.alloc()
        self.setup()
        self.alloc_main()
        self.cbuf = [c.sbuf(f"cb{i}", [128, 516], F32) for i in range(3)]
        self.cbi = 0
        self.sbuf_left = self.nc.sbuf_bytes_remaining
        for t in range(self.NT):
            for s in range(NS):
                c.dma(self.X[s][:], self.x_d[t * T + s * 128:t * T + (s + 1) * 128, :], [], [self.X[s]], self.X[s])
            if self.upto("rot"):
                self.rotary_tables(t)
            for l in range(L):
                if not self.upto("ffn1"):
                    break
                if l == 0:
                    self.make_xT()
                self.ffn(l, "f1", ALPHA)
                self.ln(l, 0)
                if l == 0:
                    self.dump("x1", t)
                if not self.upto("ret"):
                    break
                self.mixer(l, t)
                if l == 0 and "mix" in self.dbg_d:
                    for k in range(8):
                        self.cp("dve", self.POOL[k][:], self.MIXT[k][:], [self.MIXT[k]], [self.POOL[k]])
                        c.dma(self.dbg_d["mix"][k * 128:(k + 1) * 128, t * T:(t + 1) * T], self.POOL[k][:],
                              [self.POOL[k]], [], self.POOL[k])
                if not self.upto("wout"):
                    break
                self.ln(l, 1)
                if l == 0:
                    self.dump("x2", t)
                if not self.upto("ple"):
                    break
                self.ple(l, t)
                if not self.upto("all"):
                    break
                self.ffn(l, "f2", 1.0)
                self.ln(l, 2, with_xT=(l + 1 < L))
                if l == 0:
                    self.dump("x3", t)
            for s in range(NS):
                c.dma(self.out_d[t * T + s * 128:t * T + (s + 1) * 128, :], self.X[s][:], [self.X[s]], [], self.X[s])
        c.wait_all("sp", self.X)
        for e in ("pe", "act", "dve", "pool"):
            pass
        self.stats = dict(ops=c.nops, waits=c.nwaits, sems=c.nsem, sbuf_left=self.sbuf_left)
        c.close()
        return self.nc


NPOOL = 24


def build_program(NT, dbg=(), stop="all"):
    nc = bass.Bass("TRN2", target_bir_lowering=False)
    kb = KB(nc, NT, dbg, stop)
    kb.build()
    return nc, kb


def make_inputs(inp, S):
    wblob = _build_wblob(inp)
    cst = _build_consts()
    fms, tms, lnp = _build_small(inp)
    maps = []
    B = inp["x"].shape[0]
    for b in range(B):
        maps.append({
            "x": np.ascontiguousarray(inp["x"][b, :S]),
            "p": np.ascontiguousarray(inp["p"][:, b, :S]),
            "pos": np.ascontiguousarray(inp["positions"][b:b + 1, :S]),
            "wblob": wblob, "cst": cst, "fms": fms, "tms": tms, "lnp": lnp,
        })
    return maps


def kernel(**inputs):
    inp = {k: np.asarray(v) for k, v in inputs.items()}
    B, S = inp["x"].shape[0], inp["x"].shape[1]
    nc, kb = build_program(S // T)
    maps = make_inputs(inp, S)
    res = run_bass_kernel_spmd(nc, maps, core_ids=list(range(B)))
    out = np.stack([np.asarray(r["out"]) for r in res.results], axis=0)
    return out.astype(np.float32)
```

```python
import contextlib
import math
import os
import numpy as np
import concourse.bass as bass
import concourse.mybir as mybir
from concourse.bass_utils import run_bass_kernel_spmd

F32 = mybir.dt.float32
BF16 = mybir.dt.bfloat16
I32 = mybir.dt.int32
AF = mybir.ActivationFunctionType
ALU = mybir.AluOpType
AX = mybir.AxisListType

D = 1024
DFF = 2816
NFC = DFF // 128
T = 512
NS = 4
L = 2
SEQ = 8192
NCORES = 8
ALPHA = 4.0 ** 0.25
LN_EPS = 1e-5
SLAB = 4096
TWO_PI = 2.0 * math.pi


class Tok:
    __slots__ = ("sem", "val", "eng", "sid")

    def __init__(self, sem, val, eng, sid):
        self.sem, self.val, self.eng, self.sid = sem, val, eng, sid


class Buf:
    __slots__ = ("name", "t", "last_w", "readers", "dsem", "dcount", "dsid", "psum", "rg", "pe_tok")

    def __init__(self, name, t=None):
        self.name, self.t = name, t
        self.psum = False
        self.rg = None
        self.pe_tok = None
        self.last_w = None
        self.readers = []
        self.dsem = None
        self.dcount = 0
        self.dsid = None

    def __getitem__(self, k):
        return self.t[k]


class Ctx:
    SELF_SYNC = True

    def __init__(self, nc):
        self.nc = nc
        self.es = contextlib.ExitStack()
        self.eng = {"pe": nc.tensor, "act": nc.scalar, "dve": nc.vector,
                    "pool": nc.gpsimd, "sp": nc.sync}
        self.sem = {}
        self.cnt = {}
        self.known = {e: {} for e in self.eng}
        self.nsem = 0
        for e in ("pe", "act", "dve", "pool"):
            self.sem[e] = self.new_sem("s_" + e)
            self.cnt[e] = 0
        self.nops = 0
        self.nwaits = 0

    def new_sem(self, name):
        s = self.es.enter_context(self.nc.semaphore(name))
        self.nsem += 1
        return (s, self.nsem)

    def sbuf(self, name, shape, dt, es=None):
        t = (es or self.es).enter_context(self.nc.sbuf_tensor(name, list(shape), dt))
        return Buf(name, t)

    def psum(self, name, shape, dt=F32):
        t = self.es.enter_context(self.nc.psum_tensor(name, list(shape), dt))
        b = Buf(name, t)
        b.psum = True
        return b

    def _waits(self, e, reads, writes, rg=None):
        toks = []
        for b in reads:
            if b.last_w is not None:
                toks.append(b.last_w)
            if b.psum:
                toks.extend(r for r in b.readers if r.eng != e)
        for b in writes:
            if b.psum and e == "pe" and rg is not None:
                if b.rg is not None and b.rg != rg and b.pe_tok is not None:
                    t = b.pe_tok
                    if self.known[e].get(t.sid, 0) < t.val:
                        self.eng[e].wait_ge(t.sem, t.val)
                        self.known[e][t.sid] = t.val
                        self.nwaits += 1
                b.rg = rg
            if b.last_w is not None:
                toks.append(b.last_w)
            toks.extend(b.readers)
        E = self.eng[e]
        kn = self.known[e]
        raw = set()
        for b in reads:
            if b.last_w is not None:
                raw.add(id(b.last_w))
        for t in toks:
            if t.eng == e and (e == "pe" or not self.SELF_SYNC or id(t) not in raw):
                continue
            if kn.get(t.sid, 0) >= t.val:
                continue
            E.wait_ge(t.sem, t.val)
            kn[t.sid] = t.val
            self.nwaits += 1

    def _commit(self, tok, reads, writes):
        for b in reads:
            b.readers = [r for r in b.readers if r.sid != tok.sid]
            b.readers.append(tok)
        for b in writes:
            b.last_w = tok
            b.readers = []

    def op(self, e, fn, reads=(), writes=(), rg=None):
        self._waits(e, reads, writes, rg)
        ins = fn(self.eng[e])
        self.cnt[e] += 1
        s, sid = self.sem[e]
        ins.then_inc(s, 1)
        tok = Tok(s, self.cnt[e], e, sid)
        self._commit(tok, reads, writes)
        if e == "pe":
            for b in writes:
                b.pe_tok = tok
        self.nops += 1
        return tok

    def dma(self, out_ap, in_ap, reads, writes, sb, q="sp"):
        self._waits(q, reads, writes)
        if sb.dsem is None:
            sb.dsem, sb.dsid = self.new_sem("d_" + sb.name)
        ins = self.eng[q].dma_start(out=out_ap, in_=in_ap)
        sb.dcount += 16
        ins.then_inc(sb.dsem, 16)
        tok = Tok(sb.dsem, sb.dcount, "dma", sb.dsid)
        self._commit(tok, reads, writes)
        self.nops += 1
        return tok

    def wait_all(self, e, bufs):
        self._waits(e, bufs, bufs)

    def close(self):
        self.es.close()


def _swap_half(cols):
    c = np.asarray(cols).reshape(-1, 64)
    return np.concatenate([c[:, 32:], c[:, :32]], axis=1).reshape(-1)


def _fm_chunks():
    rq = np.arange(0, 256)
    rk = np.arange(256, 512)
    rqs, rks = _swap_half(rq), _swap_half(rk)
    ch = [rq[:128], rq[128:], rqs[:128], rqs[128:], rk[:128], rk[128:], rks[:128], rks[128:]]
    for base in (1024, 1408, 1792, 2176, 2560):
        for c in range(3):
            ch.append(np.arange(base + c * 128, base + (c + 1) * 128))
    return ch


def _unit_table():
    tab = {}
    off = 0

    def add(name, kc, ncols):
        nonlocal off
        tab[name] = (off, kc, ncols)
        off += kc * ncols

    for f in ("f1", "f2"):
        for u in range(11):
            add(f"{f}_gu{u}", 8, 512)
        for j in range(4):
            add(f"{f}_d{j}", NFC, 256)
    for u in range(6):
        add(f"fm{u}", 8, 512 if u < 5 else 384)
    add("ba", 8, 12)
    add("tmA", 8, 512)
    add("tmB", 8, 396)
    for j in range(2):
        add(f"wo{j}", 8, 512)
    for j in range(2):
        add(f"pg{j}", 8, 512)
    add("pp", 2, 1024)
    add("lruw", 6, 128)
    return tab, off


UNITS, WC = _unit_table()
NSLAB = (WC + SLAB - 1) // SLAB
WCP = NSLAB * SLAB


def _unit(W, cols):
    K = W.shape[0]
    kc = K // 128
    cols = np.asarray(cols)
    return W[:, cols].reshape(kc, 128, len(cols)).transpose(1, 0, 2)


def _build_wblob(inp):
    blob = np.zeros((L, 128, WCP), np.float32)
    fmch = _fm_chunks()
    for l in range(L):
        def put(name, arr):
            off, kc, ncols = UNITS[name]
            assert arr.shape == (128, kc, ncols), (name, arr.shape)
            blob[l, :, off:off + kc * ncols] = arr.reshape(128, kc * ncols)
        for f, pre in (("f1", "ffn1"), ("f2", "ffn2")):
            Wg, Wu, Wd = inp[pre + "_w_gate"][l], inp[pre + "_w_up"][l], inp[pre + "_w_down"][l]
            for u in range(11):
                parts = []
                for cc in range(2):
                    cols = np.arange((2 * u + cc) * 128, (2 * u + cc + 1) * 128)
                    parts += [_unit(Wg, cols), _unit(Wu, cols)]
                put(f"{f}_gu{u}", np.concatenate(parts, axis=2))
            for j in range(4):
                put(f"{f}_d{j}", _unit(Wd, np.arange(j * 256, (j + 1) * 256)))
        Win = inp["w_in"][l]
        for u in range(6):
            cols = np.concatenate(fmch[4 * u:4 * u + 4])
            put(f"fm{u}", _unit(Win, cols))
        put("ba", _unit(Win, np.concatenate([np.arange(3334, 3340), np.arange(3328, 3334)])))
        put("tmA", _unit(Win, np.arange(512, 1024)))
        put("tmB", _unit(Win, np.arange(2944, 3340)))
        for j in range(2):
            put(f"wo{j}", _unit(inp["w_out"][l], np.arange(j * 512, (j + 1) * 512)))
            put(f"pg{j}", _unit(inp["ple_w_gate"][l], np.arange(j * 512, (j + 1) * 512)))
        put("pp", _unit(inp["ple_w_proj"][l], np.arange(1024)))
        bd = np.zeros((128, 6, 128), np.float32)
        for c in range(3):
            for gi, nm in enumerate(("lru_w_a", "lru_w_x")):
                w = inp[nm][l]
                bd[0:64, c * 2 + gi, 0:64] = w[2 * c]
                bd[64:128, c * 2 + gi, 64:128] = w[2 * c + 1]
        put("lruw", bd)
    return blob


CST = {}
_coff = 0
for _n, _w in (("ident", 128), ("triU", 128), ("ones", 128), ("MI", 128), ("MSu", 128), ("MSl", 128),
               ("sel63", 128), ("sel127", 128), ("selB", 128), ("bones", 128), ("intraT", 512),
               ("cross", 1024), ("tailf", 256), ("cdcol", 2), ("invf", 1), ("sgn", 1), ("selb", 384), ("oh6", 768)):
    CST[_n] = (_coff, _w)
    _coff += _w
NCST = _coff


def _build_consts():
    c = np.zeros((128, NCST), np.float32)

    def put(name, arr):
        off, w = CST[name]
        c[:, off:off + w] = arr

    p = np.arange(128)
    same = (p[:, None] // 64) == (p[None, :] // 64)
    put("ident", np.eye(128))
    put("triU", (same & (p[:, None] <= p[None, :])).astype(np.float32))
    put("ones", np.ones((128, 128)))
    put("MI", (same & (p[:, None] <= p[None, :])).astype(np.float32))
    put("MSu", (same & (p[:, None] < p[None, :])).astype(np.float32))
    put("MSl", (same & (p[:, None] > p[None, :])).astype(np.float32))
    s63 = np.zeros((128, 128)); s63[63, :] = 1
    s127 = np.zeros((128, 128)); s127[127, :] = 1
    sB = np.zeros((128, 128)); sB[63, :64] = 1; sB[127, 64:] = 1
    put("sel63", s63); put("sel127", s127); put("selB", sB)
    put("bones", same.astype(np.float32))
    lg = np.log1p(-np.exp2(-5.0 - np.arange(4, dtype=np.float32))).astype(np.float32)
    idx = (p % 64).astype(np.float32)
    intra = np.zeros((128, 4, 128), np.float32)
    for h in range(4):
        m = np.exp(np.abs(idx[:, None] - idx[None, :]) * lg[h]).astype(np.float32)
        intra[:, h, :] = np.where(same, m, 0.0)
    put("intraT", intra.reshape(128, 512))
    cross = np.zeros((128, 2, 512), np.float32)
    ii = (np.arange(512) % 64).astype(np.float32)
    for c2 in range(2):
        for hh in range(2):
            cross[hh * 64:(hh + 1) * 64, c2, :] = np.exp((ii + 1.0) * lg[2 * c2 + hh])[None, :]
    put("cross", cross.reshape(128, 1024))
    tail = np.zeros((128, 4, 64), np.float32)
    for h in range(4):
        tail[:, h, :] = np.exp((63.0 - idx) * lg[h])[:, None]
    put("tailf", tail.reshape(128, 256))
    cd = np.zeros((128, 2), np.float32)
    for c2 in range(2):
        for hh in range(2):
            cd[hh * 64:(hh + 1) * 64, c2] = np.exp(np.float32(64.0) * lg[2 * c2 + hh])
    put("cdcol", cd)
    half = 32
    invf = (10000.0 ** (-np.arange(half, dtype=np.float32) / half)).astype(np.float32)
    put("invf", invf[p % 32][:, None])
    put("sgn", np.where((p % 64) < 32, -1.0, 1.0)[:, None])
    selb = np.zeros((128, 384), np.float32)
    for h in range(6):
        selb[h, h * 64:(h + 1) * 64] = 1.0
    put("selb", selb)
    oh = np.zeros((128, 768), np.float32)
    for h in range(6):
        oh[h, h * 128:(h + 1) * 128] = 1.0
    put("oh6", oh)
    return c


NTM = 256 + 384 + 6 + 6


def _build_small(inp):
    fms = np.zeros((128, L, 60), np.float32)
    tms = np.zeros((1, L * NTM), np.float32)
    lnp = np.zeros((L * 3, 1, 2048), np.float32)
    for l in range(L):
        for c in range(3):
            sl = slice(c * 128, (c + 1) * 128)
            fms[:, l, c * 8 + 0:c * 8 + 4] = inp["lru_conv_w"][l][:, sl].T
            fms[:, l, c * 8 + 4] = inp["lru_conv_b"][l][sl]
            fms[:, l, c * 8 + 5] = inp["lru_b_a"][l][sl]
            fms[:, l, c * 8 + 6] = inp["lru_b_x"][l][sl]
            fms[:, l, c * 8 + 7] = inp["lru_lambda"][l][sl]
        for ch in range(9):
            sl = slice(ch * 128, (ch + 1) * 128)
            fms[:, l, 24 + ch * 4:24 + ch * 4 + 4] = inp["gdn_conv_w"][l][:, sl].T
        o = l * NTM
        tms[0, o:o + 256] = inp["ret_norm_g"][l]
        tms[0, o + 256:o + 640] = np.tile(inp["gdn_norm_g"][l], 6)
        tms[0, o + 640:o + 646] = inp["gdn_a_log"][l]
        tms[0, o + 646:o + 652] = inp["gdn_dt_bias"][l]
        for w, (g, b) in enumerate((("ln_ffn1_g", "ln_ffn1_b"), ("ln_mix_g", "ln_mix_b"), ("ln_ffn2_g", "ln_ffn2_b"))):
            lnp[l * 3 + w, 0, :1024] = inp[g][l]
            lnp[l * 3 + w, 0, 1024:] = inp[b][l]
    return fms.reshape(128, L * 60), tms, lnp


def run_interleaved(gens):
    gens = list(gens)
    while gens:
        for g in list(gens):
            try:
                next(g)
            except StopIteration:
                gens.remove(g)


class KB:
    STAGES = ["setup", "rot", "ffn1", "ret", "lru", "gdn", "wout", "ple", "all"]

    def upto(self, name):
        return self.STAGES.index(self.stop) >= self.STAGES.index(name)

    def __init__(self, nc, NT, dbg=(), stop="all"):
        self.nc = nc
        self.NT = NT
        self.stop = stop
        self.c = Ctx(nc)
        self.dbg = set(dbg)
        S = NT * T
        dt = nc.dram_tensor
        self.x_d = dt("x", [S, D], F32, kind="ExternalInput").ap()
        self.p_d = dt("p", [L, S, 256], F32, kind="ExternalInput").ap()
        self.pos_d = dt("pos", [1, S], I32, kind="ExternalInput").ap()
        self.wb_d = dt("wblob", [L, 128, WCP], F32, kind="ExternalInput").ap()
        self.cst_d = dt("cst", [128, NCST], F32, kind="ExternalInput").ap()
        self.fms_d = dt("fms", [128, L * 60], F32, kind="ExternalInput").ap()
        self.tms_d = dt("tms", [1, L * NTM], F32, kind="ExternalInput").ap()
        self.lnp_d = dt("lnp", [L * 3, 1, 2048], F32, kind="ExternalInput").ap()
        self.out_d = dt("out", [S, D], F32, kind="ExternalOutput").ap()
        self.wsc = Buf("wscr", dt("wscr", [L, 128, WCP], BF16, kind="Internal").ap())
        self.dbg_d = {}
        for n, shp in (("x1", [S, D]), ("mix", [D, S]), ("x2", [S, D]), ("x3", [S, D])):
            if n in self.dbg:
                self.dbg_d[n] = dt("dbg_" + n, shp, F32, kind="ExternalOutput").ap()

    def mm(self, psb, out, lhsT, rhs, reads, start=True, stop=True):
        rg = (lhsT.base_partition(), lhsT.partition_size())
        return self.c.op("pe", lambda e: e.matmul(out, lhsT=lhsT, rhs=rhs, start=start, stop=stop), reads, [psb], rg)

    def tr(self, psb, out, in_, reads):
        idn = self.cs("ident")
        rg = (in_.base_partition(), in_.partition_size())
        return self.c.op("pe", lambda e: e.transpose(out, in_, idn), list(reads) + [self.CST], [psb], rg)

    def act(self, out, in_, func, reads, writes, bias=None, scale=None):
        kw = {}
        if bias is not None:
            kw["bias"] = bias
        if scale is not None:
            kw["scale"] = scale
        return self.c.op("act", lambda e: e.activation(out=out, in_=in_, func=func, **kw), reads, writes)

    def tt(self, eng, out, in0, in1, op, reads, writes):
        return self.c.op(eng, lambda e: e.tensor_tensor(out=out, in0=in0, in1=in1, op=op), reads, writes)

    def ts(self, eng, out, in0, s1, s2, op0, op1, reads, writes):
        if s2 is None:
            return self.c.op(eng, lambda e: e.tensor_scalar(out=out, in0=in0, scalar1=s1, scalar2=None, op0=op0), reads, writes)
        return self.c.op(eng, lambda e: e.tensor_scalar(out=out, in0=in0, scalar1=s1, scalar2=s2, op0=op0, op1=op1), reads, writes)

    def stt(self, out, in0, scalar, in1, op0, op1, reads, writes):
        return self.c.op("dve", lambda e: e.scalar_tensor_tensor(out=out, in0=in0, scalar=scalar, in1=in1, op0=op0, op1=op1), reads, writes)

    def cp(self, eng, out, in_, reads, writes):
        if eng == "act":
            return self.c.op("act", lambda e: e.copy(out=out, in_=in_), reads, writes)
        return self.c.op(eng, lambda e: e.tensor_copy(out=out, in_=in_), reads, writes)

    def ps(self):
        b = self.PS[self.psi % self.ps_n]
        self.psi += 1
        return b

    def cs(self, name, rows=slice(0, 128)):
        off, w = CST[name]
        return self.CST[rows, off:off + w]

    def alloc(self):
        c = self.c
        self.PS = [c.psum(f"ps{i}", [128, 512]) for i in range(8)]
        self.psi = 0
        self.ps_n = 8
        self.CST = c.sbuf("cst_sb", [128, NCST], F32)
        self.FMS = c.sbuf("fms_sb", [128, L * 60], F32)
        self.TMS = c.sbuf("tms_sb", [128, L * NTM], F32)
        self.DER = c.sbuf("der_sb", [128, L * 16], F32)
        self.BONES = c.sbuf("bones_bf", [128, 128], BF16)
        self.LRUW = [c.sbuf(f"lruw{l}", [128, 768], BF16) for l in range(L)]
        self.SR = [c.sbuf(f"sr{l}", [128, 128], F32) for l in range(L)]
        self.SG = [c.sbuf(f"sg{l}", [128, 192], F32) for l in range(L)]
        self.SGB = [c.sbuf(f"sgb{l}", [128, 192], BF16) for l in range(L)]
        self.HS = [c.sbuf(f"hs{l}", [128, 4], F32) for l in range(L)]
        self.HL = [c.sbuf(f"hl{l}", [128, 3 * 3], F32) for l in range(L)]
        self.HG = [c.sbuf(f"hg{l}", [128, 9 * 3], F32) for l in range(L)]

    def alloc_main(self):
        c = self.c
        self.X = [c.sbuf(f"x{s}", [128, D], F32) for s in range(NS)]
        self.XT = [c.sbuf(f"xT{k}", [128, T], BF16) for k in range(8)]
        self.WA = [c.sbuf(f"wA{i}", [128, 4096], BF16) for i in range(3)]
        self.wai = 0
        self.WD = [c.sbuf(f"wD{i}", [128, NFC * 256], BF16) for i in range(2)]
        self.wdi = 0
        self.MIXT = [c.sbuf(f"mixT{k}", [128, T], BF16) for k in range(8)]
        self.COS = c.sbuf("cosT", [128, T], F32)
        self.SIN = c.sbuf("sinT", [128, T], F32)
        self.SM = [c.sbuf(f"sm{i}", [128, 64], F32) for i in range(8)]
        self.LNS = [c.sbuf(f"lns{i}", [128, 20], F32) for i in range(NS)]
        self.RST = [c.sbuf(f"rst{i}", [128, 32], F32) for i in range(NS)]
        self.smi = 0
        self.POOL = [c.sbuf(f"scr{i}", [128, 512], F32) for i in range(NPOOL)]
        self.GH = {}
        for k, (w, dtp) in {"F": (256, F32), "E": (384, BF16), "M": (384, BF16), "T": (128, BF16), "Na": (256, BF16),
                            "Nb": (256, BF16), "V": (192, BF16), "U": (128, F32)}.items():
            self.GH[k] = [c.sbuf(f"gh{k}{h}", [128, w], dtp) for h in range(6)]
        self.GSC = [c.sbuf(f"gsc{i}", [128, 64], F32) for i in range(4)]
        self.GCTB = [c.sbuf(f"gct{i}", [128, 128], F32) for i in range(4)]
        self.GVB = [c.sbuf(f"gv{i}", [128, 1152], BF16) for i in range(2)]
        self.GOG = [c.sbuf(f"gog{i}", [128, 384], F32) for i in range(2)]

    def sm(self):
        b = self.SM[self.smi % len(self.SM)]
        self.smi += 1
        return b

    def setup(self):
        c = self.c
        c.dma(self.CST[:], self.cst_d, [], [self.CST], self.CST)
        c.dma(self.FMS[:], self.fms_d, [], [self.FMS], self.FMS)
        c.dma(self.TMS[:], self.tms_d.partition_broadcast(128), [], [self.TMS], self.TMS)
        self.cp("dve", self.BONES[:], self.cs("bones"), [self.CST], [self.BONES])
        for l in range(L):
            o = l * 16
            self.act(self.DER[:, o:o + 6], self.TMS[:, l * NTM + 640:l * NTM + 646], AF.Exp, [self.TMS], [self.DER])
            self.ts("dve", self.DER[:, o:o + 6], self.DER[:, o:o + 6], -1.0, None, ALU.mult, None, [self.DER], [self.DER])
            for ch in range(3):
                lam = self.FMS[:, l * 60 + ch * 8 + 7:l * 60 + ch * 8 + 8]
                self.act(self.DER[:, o + 6 + ch:o + 7 + ch], lam, AF.Exp, [self.FMS], [self.DER], scale=-1.0)
            self.act(self.DER[:, o + 6:o + 9], self.DER[:, o + 6:o + 9], AF.Ln, [self.DER], [self.DER], bias=1.0)
            self.ts("dve", self.DER[:, o + 9:o + 12], self.DER[:, o + 6:o + 9], -16.0, None, ALU.mult, None, [self.DER], [self.DER])
            self.ts("dve", self.DER[:, o + 6:o + 9], self.DER[:, o + 6:o + 9], -8.0, None, ALU.mult, None, [self.DER], [self.DER])
            for b in (self.SR[l], self.SG[l], self.HS[l], self.HL[l], self.HG[l]):
                c.op("pool", lambda e, b=b: e.memset(b[:], 0.0), [], [b])
            c.op("pool", lambda e, b=self.SGB[l]: e.memset(b[:], 0.0), [], [self.SGB[l]])
        with contextlib.ExitStack() as es:
            st_f = [c.sbuf(f"cvf{i}", [128, SLAB], F32, es) for i in range(2)]
            st_b = [c.sbuf(f"cvb{i}", [128, SLAB], BF16, es) for i in range(2)]
            engs = ("dve", "pool", "act")
            k = 0
            for l in range(L):
                for s in range(NSLAB):
                    f, b = st_f[k % 2], st_b[k % 2]
                    c.dma(f[:], self.wb_d[l, :, s * SLAB:(s + 1) * SLAB], [], [f], f)
                    self.cp(engs[k % 3], b[:], f[:], [f], [b])
                    c.dma(self.wsc[l, :, s * SLAB:(s + 1) * SLAB], b[:], [b], [self.wsc], b)
                    k += 1
            for e in ("sp", "pe", "act", "dve", "pool"):
                c.wait_all(e, st_f + st_b)
        for l in range(L):
            off, kc, ncols = UNITS["lruw"]
            c.dma(self.LRUW[l][:], self.wsc[l, :, off:off + 768], [self.wsc], [self.LRUW[l]], self.LRUW[l])

    def load_wa(self, l, name):
        off, kc, ncols = UNITS[name]
        b = self.WA[self.wai % len(self.WA)]
        self.wai += 1
        n = kc * ncols
        self.c.dma(b[:, 0:n], self.wsc[l, :, off:off + n], [self.wsc], [b], b)
        return b, b[:, 0:n].rearrange("p (k n) -> p k n", k=kc)

    def load_wd(self, l, name):
        off, kc, ncols = UNITS[name]
        b = self.WD[self.wdi % len(self.WD)]
        self.wdi += 1
        n = kc * ncols
        self.c.dma(b[:, 0:n], self.wsc[l, :, off:off + n], [self.wsc], [b], b)
        return b, b[:, 0:n].rearrange("p (k n) -> p k n", k=kc)

    def make_xT(self):
        for kc in range(8):
            ps = self.ps()
            for s in range(NS):
                self.tr(ps, ps[:, s * 128:(s + 1) * 128], self.X[s][:, kc * 128:(kc + 1) * 128], [self.X[s]])
            self.cp("act" if kc % 2 else "dve", self.XT[kc][:], ps[:], [ps], [self.XT[kc]])

    def hT(self, fc):
        b = self.POOL[fc // 2]
        return b, b[:].bitcast(BF16)[:, (fc % 2) * 512:(fc % 2 + 1) * 512]

    def ffn(self, l, f, alpha):
        sg = [self.POOL[11], self.POOL[12]]
        for u in range(11):
            wb, wv = self.load_wa(l, f"{f}_gu{u}")
            for cc in range(2):
                fc = 2 * u + cc
                psg, psu = self.ps(), self.ps()
                for kc in range(8):
                    self.mm(psg, psg[:], wv[:, kc, cc * 256:cc * 256 + 128], self.XT[kc][:], [wb, self.XT[kc]], kc == 0, kc == 7)
                for kc in range(8):
                    self.mm(psu, psu[:], wv[:, kc, cc * 256 + 128:cc * 256 + 256], self.XT[kc][:], [wb, self.XT[kc]], kc == 0, kc == 7)
                sb = sg[fc % 2]
                self.act(sb[:], psg[:], AF.Silu, [psg], [sb])
                hb, hv = self.hT(fc)
                self.stt(hv, psu[:], 0.5, sb[:], ALU.mult, ALU.mult, [psu, sb], [hb])
        for j in range(4):
            wb, wv = self.load_wd(l, f"{f}_d{j}")
            for s in range(NS):
                ps = self.ps()
                for fc in range(NFC):
                    hb, hv = self.hT(fc)
                    self.mm(ps, ps[:, 0:256], hv[:, s * 128:(s + 1) * 128], wv[:, fc, :], [wb, hb], fc == 0, fc == NFC - 1)
                xs = self.X[s][:, j * 256:(j + 1) * 256]
                self.stt(xs, xs, alpha, ps[:, 0:256], ALU.mult, ALU.add, [self.X[s], ps], [self.X[s]])

    def ln(self, l, w, with_xT=True, OUT=None):
        c = self.c
        LB = [self.POOL[16 + k] for k in range(4)]
        for k in range(4):
            c.dma(LB[k][:], self.lnp_d[l * 3 + w][:, k * 512:(k + 1) * 512].partition_broadcast(128), [], [LB[k]], LB[k])
        run_interleaved([self.ln_chain(s, LB, with_xT, OUT) for s in range(NS)])

    def ln_chain(self, s, LB, with_xT, OUT=None):
        c = self.c
        x = self.X[s]
        st = self.LNS[s]
        c.op("dve", lambda e: e.bn_stats(out=st[:, 0:6], in_=x[:, 0:512]), [x], [st])
        c.op("dve", lambda e: e.bn_stats(out=st[:, 6:12], in_=x[:, 512:1024]), [x], [st])
        yield
        c.op("dve", lambda e: e.bn_aggr(out=st[:, 12:14], in_=st[:, 0:12]), [st], [st])
        yield
        self.act(st[:, 14:15], st[:, 13:14], AF.Sqrt, [st], [st], bias=LN_EPS)
        yield
        c.op("dve", lambda e: e.reciprocal(out=st[:, 15:16], in_=st[:, 14:15]), [st], [st])
        yield
        self.stt(st[:, 16:17], st[:, 12:13], -1.0, st[:, 15:16], ALU.mult, ALU.mult, [st], [st])
        yield
        self.act(x[:], x[:], AF.Identity, [x, st], [x], bias=st[:, 16:17], scale=st[:, 15:16])
        yield
        self.tt("dve", x[:, 0:512], x[:, 0:512], LB[0][:], ALU.mult, [x, LB[0]], [x])
        self.tt("pool", x[:, 512:1024], x[:, 512:1024], LB[1][:], ALU.mult, [x, LB[1]], [x])
        yield
        if OUT is None:
            self.tt("dve", x[:, 0:512], x[:, 0:512], LB[2][:], ALU.add, [x, LB[2]], [x])
            self.tt("pool", x[:, 512:1024], x[:, 512:1024], LB[3][:], ALU.add, [x, LB[3]], [x])
        else:
            self.tt("dve", OUT[2 * s][:], x[:, 0:512], LB[2][:], ALU.add, [x, LB[2]], [OUT[2 * s]])
            self.tt("pool", OUT[2 * s + 1][:], x[:, 512:1024], LB[3][:], ALU.add, [x, LB[3]], [OUT[2 * s + 1]])
        yield
        if not with_xT:
            return
        for j in range(2):
            ps = self.ps()
            for k in range(4):
                kc = 4 * j + k
                self.tr(ps, ps[:, k * 128:(k + 1) * 128], x[:, kc * 128:(kc + 1) * 128], [x])
            yield
            for k in range(4):
                kc = 4 * j + k
                self.cp("act" if k % 2 else "dve", self.XT[kc][:, s * 128:(s + 1) * 128], ps[:, k * 128:(k + 1) * 128], [ps], [self.XT[kc]])
            yield

    def dump(self, name, t, src=None):
        if name not in self.dbg_d:
            return
        src = src or self.X
        dd = self.dbg_d[name]
        for s in range(NS):
            self.c.dma(dd[t * T + s * 128:t * T + (s + 1) * 128, :], src[s][:], [src[s]], [], src[s])

    def rotary_tables(self, t):
        c = self.c
        pi_b = self.POOL[0]
        pf, ang, kf, r = self.POOL[1], self.POOL[2], self.POOL[3], self.POOL[4]
        m = self.POOL[5]
        piv = pi_b[:].bitcast(I32)
        c.dma(piv, self.pos_d[:, t * T:(t + 1) * T].partition_broadcast(128), [], [pi_b], pi_b)
        self.cp("dve", pf[:], piv, [pi_b], [pf])
        self.ts("dve", ang[:], pf[:], self.cs("invf"), None, ALU.mult, None, [pf, self.CST], [ang])
        kiv = pi_b[:].bitcast(I32)
        self.ts("dve", kiv, ang[:], 1.0 / TWO_PI, None, ALU.mult, None, [ang], [pi_b])
        self.cp("dve", kf[:], kiv, [pi_b], [kf])
        C1 = float(np.float32(6.28125))
        C2 = float(TWO_PI - 6.28125)
        self.stt(r[:], kf[:], -C1, ang[:], ALU.mult, ALU.add, [kf, ang], [r])
        self.stt(r[:], kf[:], -C2, r[:], ALU.mult, ALU.add, [kf, r], [r])
        for dst, shift, sgn in ((self.SIN, 0.0, True), (self.COS, math.pi / 2, False)):
            rr = dst
            self.ts("dve", rr[:], r[:], shift, None, ALU.add, None, [r], [rr])
            self.ts("dve", m[:], rr[:], math.pi, None, ALU.is_gt, None, [rr], [m])
            self.stt(rr[:], m[:], -TWO_PI, rr[:], ALU.mult, ALU.add, [m, rr], [rr])
            self.ts("dve", m[:], rr[:], -math.pi, None, ALU.is_lt, None, [rr], [m])
            self.stt(rr[:], m[:], TWO_PI, rr[:], ALU.mult, ALU.add, [m, rr], [rr])
            self.ts("dve", rr[:], rr[:], math.pi, -math.pi, ALU.min, ALU.max, [rr], [rr])
            self.act(rr[:], rr[:], AF.Sin, [rr], [rr])
            if sgn:
                self.ts("dve", rr[:], rr[:], self.cs("sgn"), None, ALU.mult, None, [rr, self.CST], [rr])

    def fm_mm(self, l, i, ps):
        u = i // 4
        if self.fm_cur is None or self.fm_cur[0] != (l, u):
            wb, wv = self.load_wa(l, f"fm{u}")
            self.fm_cur = ((l, u), wb, wv)
        _, wb, wv = self.fm_cur
        o = (i % 4) * 128
        for kc in range(8):
            self.mm(ps, ps[:], wv[:, kc, o:o + 128], self.XT[kc][:], [wb, self.XT[kc]], kc == 0, kc == 7)

    def mixer(self, l, t):
        self.fm_cur = None
        self.retention(l)
        if self.upto("lru"):
            self.lru(l)
        if self.upto("gdn"):
            self.gdn(l)
        if not self.upto("wout"):
            return
        for j in range(2):
            wb, wv = self.load_wa(l, f"wo{j}")
            for s in range(NS):
                ps = self.ps()
                for k in range(8):
                    self.mm(ps, ps[:], self.MIXT[k][:, s * 128:(s + 1) * 128], wv[:, k, :], [wb, self.MIXT[k]], k == 0, k == 7)
                xs = self.X[s][:, j * 512:(j + 1) * 512]
                self.stt(xs, xs, ALPHA, ps[:], ALU.mult, ALU.add, [self.X[s], ps], [self.X[s]])

    def retention(self, l):
        c = self.c
        P = self.POOL
        QK, KK, QC, KTM, VTM, VTT = P[2], P[3], P[4], P[5], P[6], P[7]
        GS = [P[8], P[9]]
        SRB = P[10]
        bufs = dict(QK=QK, KK=KK, QC=QC, KTM=KTM, VTM=VTM, VTT=VTT, GS=GS, SRB=SRB)
        run_interleaved([self.ret_q(l, c2, P[0 + c2], P[11 + c2], bufs) for c2 in range(2)])
        run_interleaved([self.ret_k(l, c2, P[13 + 3 * c2], P[14 + 3 * c2], P[15 + 3 * c2], bufs) for c2 in range(2)])
        wb, wv = self.load_wa(l, "tmA")
        run_interleaved([self.ret_vg(l, s, wb, wv, bufs) for s in range(NS)])
        ktm = KTM[:].bitcast(BF16); vtt = VTT[:].bitcast(BF16); srb = SRB[:].bitcast(BF16)
        SR = self.SR[l]
        pk = [self.ps(), self.ps()]
        for n in range(8):
            s, cc = n // 2, n % 2
            pc = slice(cc * 64, cc * 64 + 64)
            pb = pk[n // 4]
            o = (n % 4) * 128
            for h in range(4):
                hp, c2 = (h % 2) * 64, h // 2
                self.mm(pb, pb[hp:hp + 64, o + c2 * 64:o + c2 * 64 + 64], ktm[pc, s * 256 + h * 64:s * 256 + h * 64 + 64],
                        vtt[pc, s * 256 + h * 64:s * 256 + h * 64 + 64], [KTM, VTT])
        for n in range(8):
            pb = pk[n // 4]
            o = (n % 4) * 128
            self.cp("dve", srb[:, n * 128:(n + 1) * 128], SR[:], [SR], [SRB])
            for c2 in range(2):
                cd = self.cs("cdcol")[:, c2:c2 + 1]
                self.stt(SR[:, c2 * 64:c2 * 64 + 64], SR[:, c2 * 64:c2 * 64 + 64], cd, pb[:, o + c2 * 64:o + c2 * 64 + 64],
                         ALU.mult, ALU.add, [SR, self.CST, pb], [SR])
        self.psi = 0
        self.ps_n = 6
        Q = [P[0], P[1], P[11], P[12]]
        R = [P[13], P[14], P[15], P[16]]
        run_interleaved([self.ret_out(l, s, Q[s], R[s], self.PS[6 + s // 2], bufs) for s in range(NS)])
        self.ps_n = 8

    def ret_q(self, l, c2, ta, tb, B):
        QK, QC = B["QK"], B["QC"]
        qr = QK[:].bitcast(BF16); qc = QC[:].bitcast(BF16)
        cross = self.cs("cross")
        psq, psqs = self.ps(), self.ps()
        self.fm_mm(l, 0 + c2, psq)
        self.fm_mm(l, 2 + c2, psqs)
        yield
        self.stt(ta[:], psq[:], 0.125, self.COS[:], ALU.mult, ALU.mult, [psq, self.COS], [ta])
        self.stt(tb[:], psqs[:], 0.125, self.SIN[:], ALU.mult, ALU.mult, [psqs, self.SIN], [tb])
        yield
        self.tt("pool", qr[:, c2 * 512:(c2 + 1) * 512], ta[:], tb[:], ALU.add, [ta, tb], [QK])
        yield
        self.tt("pool", qc[:, c2 * 512:(c2 + 1) * 512], qr[:, c2 * 512:(c2 + 1) * 512], cross[:, c2 * 512:(c2 + 1) * 512],
                ALU.mult, [QK, self.CST], [QC])

    def ret_k(self, l, c2, ta, tb, KF, B):
        KK, KTM = B["KK"], B["KTM"]
        kr = KK[:].bitcast(BF16); ktm = KTM[:].bitcast(BF16)
        psk, psks = self.ps(), self.ps()
        self.fm_mm(l, 4 + c2, psk)
        self.fm_mm(l, 6 + c2, psks)
        yield
        self.tt("dve", ta[:], psk[:], self.COS[:], ALU.mult, [psk, self.COS], [ta])
        self.tt("dve", tb[:], psks[:], self.SIN[:], ALU.mult, [psks, self.SIN], [tb])
        yield
        self.tt("pool", KF[:], ta[:], tb[:], ALU.add, [ta, tb], [KF])
        yield
        self.cp("pool", kr[:, c2 * 512:(c2 + 1) * 512], KF[:], [KF], [KK])
        ps = self.ps()
        for s in range(NS):
            self.tr(ps, ps[:, s * 128:(s + 1) * 128], KF[:, s * 128:(s + 1) * 128], [KF])
        yield
        for s in range(NS):
            self.cp("act", ktm[:, s * 256 + c2 * 128:s * 256 + (c2 + 1) * 128], ps[:, s * 128:(s + 1) * 128], [ps], [KTM])

    def ret_vg(self, l, s, wb, wv, B):
        VTM, VTT, GS = B["VTM"], B["VTT"], B["GS"]
        vtm = VTM[:].bitcast(BF16); vtt = VTT[:].bitcast(BF16)
        ps = self.ps()
        for kc in range(8):
            self.mm(ps, ps[:], self.XT[kc][:, s * 128:(s + 1) * 128], wv[:, kc, :], [wb, self.XT[kc]], kc == 0, kc == 7)
        yield
        self.cp("dve", vtm[:, s * 256:(s + 1) * 256], ps[:, 0:256], [ps], [VTM])
        self.tt("dve", vtt[:, s * 256:(s + 1) * 256], ps[:, 0:256], self.cs("tailf"), ALU.mult, [ps, self.CST], [VTT])
        yield
        g = GS[s // 2]
        self.act(g[:, (s % 2) * 256:(s % 2 + 1) * 256], ps[:, 256:512], AF.Silu, [ps], [g])

    def ret_out(self, l, s, Qb, Rb, pso, B):
        c = self.c
        QK, KK, QC, VTM, GS, SRB = B["QK"], B["KK"], B["QC"], B["VTM"], B["GS"], B["SRB"]
        qr = QK[:].bitcast(BF16); kr = KK[:].bitcast(BF16); qc = QC[:].bitcast(BF16)
        vtm = VTM[:].bitcast(BF16); srb = SRB[:].bitcast(BF16)
        ptall = Rb[:].bitcast(BF16)
        intra = self.cs("intraT")
        rng = self.TMS[:, l * NTM:l * NTM + 256]
        po = (s % 2) * 256
        st = self.RST[s]
        sq, on = Qb[:, 0:256], Qb[:, 256:512]
        for h in range(4):
            hp, c2 = (h % 2) * 64, h // 2
            pst = self.ps()
            self.mm(pst, pst[:, 0:128], kr[hp:hp + 64, c2 * 512 + s * 128:c2 * 512 + (s + 1) * 128],
                    qr[hp:hp + 64, c2 * 512 + s * 128:c2 * 512 + (s + 1) * 128], [KK, QK])
            yield
            ptv = ptall[:, h * 128:(h + 1) * 128]
            self.tt("dve", ptv, pst[:, 0:128], intra[:, h * 128:(h + 1) * 128], ALU.mult, [pst, self.CST], [Rb])
            yield
            self.mm(pso, pso[:, po + h * 64:po + h * 64 + 64], ptv, vtm[:, s * 256 + h * 64:s * 256 + h * 64 + 64], [Rb, VTM], True, False)
            for cc in range(2):
                pc = slice(cc * 64, cc * 64 + 64)
                n = 2 * s + cc
                self.mm(pso, pso[pc, po + h * 64:po + h * 64 + 64],
                        qc[hp:hp + 64, c2 * 512 + s * 128 + cc * 64:c2 * 512 + s * 128 + cc * 64 + 64],
                        srb[hp:hp + 64, n * 128 + c2 * 64:n * 128 + c2 * 64 + 64], [QC, SRB], False, cc == 1)
            yield
        po_ap = pso[:, po:po + 256]
        c.op("dve", lambda e: e.reduce_sum(out=st[:, 0:4], in_=po_ap.rearrange("p (h e) -> p h e", h=4), axis=AX.X), [pso], [st])
        yield
        self.act(sq, po_ap, AF.Square, [pso], [Qb])
        yield
        c.op("dve", lambda e: e.reduce_sum(out=st[:, 4:8], in_=sq.rearrange("p (h e) -> p h e", h=4), axis=AX.X), [Qb], [st])
        yield
        self.ts("dve", st[:, 8:12], st[:, 0:4], 1.0 / 64, None, ALU.mult, None, [st], [st])
        yield
        self.tt("dve", st[:, 12:16], st[:, 8:12], st[:, 8:12], ALU.mult, [st], [st])
        yield
        self.stt(st[:, 16:20], st[:, 4:8], 1.0 / 64, st[:, 12:16], ALU.mult, ALU.subtract, [st], [st])
        yield
        self.act(st[:, 20:24], st[:, 16:20], AF.Sqrt, [st], [st], bias=1e-5)
        yield
        c.op("dve", lambda e: e.reciprocal(out=st[:, 24:28], in_=st[:, 20:24]), [st], [st])
        yield
        for h in range(4):
            self.ts("dve", on[:, h * 64:(h + 1) * 64], pso[:, po + h * 64:po + (h + 1) * 64], st[:, 8 + h:9 + h], st[:, 24 + h:25 + h],
                    ALU.subtract, ALU.mult, [pso, st], [Qb])
        yield
        self.tt("pool", on, on, rng, ALU.mult, [Qb, self.TMS], [Qb])
        yield
        g = GS[s // 2]
        self.tt("pool", on, on, g[:, (s % 2) * 256:(s % 2 + 1) * 256], ALU.mult, [Qb, g], [Qb])
        yield
        pst = self.ps()
        for k in range(2):
            self.tr(pst, pst[:, k * 128:(k + 1) * 128], on[:, k * 128:(k + 1) * 128], [Qb])
        yield
        for k in range(2):
            self.cp("act", self.MIXT[k][:, s * 128:(s + 1) * 128], pst[:, k * 128:(k + 1) * 128], [pst], [self.MIXT[k]])

    def conv_chunk(self, l, ps, hist, hcol, wcol, bias_col, out_b):
        CB = self.cbuf[self.cbi % 2]
        self.cbi += 1
        self.cp("pool", CB[:, 0:3], hist[:, hcol:hcol + 3], [hist], [CB])
        self.cp("act", CB[:, 3:515], ps[:], [ps], [CB])
        self.cp("pool", hist[:, hcol:hcol + 3], CB[:, 512:515], [CB], [hist])
        w = self.FMS
        if bias_col is None:
            self.ts("dve", out_b[:, 0:512], CB[:, 3:515], w[:, wcol + 3:wcol + 4], None, ALU.mult, None, [CB, w], [out_b])
        else:
            self.ts("dve", out_b[:, 0:512], CB[:, 3:515], w[:, wcol + 3:wcol + 4], w[:, bias_col:bias_col + 1], ALU.mult, ALU.add, [CB, w], [out_b])
        for k in range(3):
            self.stt(out_b[:, 0:512], CB[:, k:k + 512], w[:, wcol + k:wcol + k + 1], out_b[:, 0:512], ALU.mult, ALU.add, [CB, w, out_b], [out_b])

    def lru(self, l):
        P = self.POOL
        gens = [self.lru_chunk(l, ch, [P[7 * ch + i] for i in range(7)], self.cbuf[ch]) for ch in range(3)]
        run_interleaved(gens)

    def lru_chunk(self, l, ch, B, CB):
        c = self.c
        XC, XCB, RA, IG, B2, HB, G1 = B
        xcb = XCB[:].bitcast(BF16)[:, 0:512]
        lw, der, fo, w = self.LRUW[l], self.DER, l * 60, self.FMS
        hist, hcol, wcol, bcol = self.HL[l], ch * 3, fo + ch * 8, fo + ch * 8 + 4
        hs = self.HS[l]
        ps = self.ps()
        self.fm_mm(l, 8 + ch, ps)
        self.cp("pool", CB[:, 0:3], hist[:, hcol:hcol + 3], [hist], [CB])
        yield
        self.cp("act", CB[:, 3:515], ps[:], [ps], [CB])
        psg = self.ps()
        self.fm_mm(l, 11 + ch, psg)
        yield
        self.cp("pool", hist[:, hcol:hcol + 3], CB[:, 512:515], [CB], [hist])
        self.ts("dve", XC[:], CB[:, 3:515], w[:, wcol + 3:wcol + 4], w[:, bcol:bcol + 1], ALU.mult, ALU.add, [CB, w], [XC])
        self.act(G1[:], psg[:], AF.Square, [psg], [G1])
        yield
        self.stt(XC[:], CB[:, 0:512], w[:, wcol:wcol + 1], XC[:], ALU.mult, ALU.add, [CB, w, XC], [XC])
        self.ts("dve", G1[:], G1[:], 0.044715, 1.0, ALU.mult, ALU.add, [G1], [G1])
        yield
        self.stt(XC[:], CB[:, 1:513], w[:, wcol + 1:wcol + 2], XC[:], ALU.mult, ALU.add, [CB, w, XC], [XC])
        self.tt("dve", G1[:], G1[:], psg[:], ALU.mult, [G1, psg], [G1])
        yield
        self.stt(XC[:], CB[:, 2:514], w[:, wcol + 2:wcol + 3], XC[:], ALU.mult, ALU.add, [CB, w, XC], [XC])
        self.act(G1[:], G1[:], AF.Sigmoid, [G1], [G1], scale=1.5957691216057308)
        yield
        self.cp("pool", xcb, XC[:], [XC], [XCB])
        self.tt("dve", G1[:], G1[:], psg[:], ALU.mult, [G1, psg], [G1])
        yield
        psr, psi = self.ps(), self.ps()
        self.mm(psr, psr[:], lw[:, (ch * 2) * 128:(ch * 2 + 1) * 128], xcb, [lw, XCB])
        self.mm(psi, psi[:], lw[:, (ch * 2 + 1) * 128:(ch * 2 + 2) * 128], xcb, [lw, XCB])
        yield
        self.act(RA[:], psr[:], AF.Sigmoid, [psr, w], [RA], bias=w[:, fo + ch * 8 + 5:fo + ch * 8 + 6])
        self.act(IG[:], psi[:], AF.Sigmoid, [psi, w], [IG], bias=w[:, fo + ch * 8 + 6:fo + ch * 8 + 7])
        yield
        self.act(RA[:], RA[:], AF.Exp, [RA, der], [RA], scale=der[:, l * 16 + 6 + ch:l * 16 + 7 + ch])
        self.tt("pool", IG[:], IG[:], XC[:], ALU.mult, [IG, XC], [IG])
        yield
        self.tt("dve", B2[:], RA[:], RA[:], ALU.mult, [RA], [B2])
        yield
        self.ts("dve", B2[:], B2[:], -1.0, 1.0, ALU.mult, ALU.add, [B2], [B2])
        yield
        self.act(B2[:], B2[:], AF.Sqrt, [B2], [B2])
        yield
        self.tt("dve", B2[:], B2[:], IG[:], ALU.mult, [B2, IG], [B2])
        yield
        c.op("dve", lambda e: e.tensor_tensor_scan(out=HB[:], data0=RA[:], data1=B2[:], initial=hs[:, ch:ch + 1],
                                                   op0=ALU.mult, op1=ALU.add), [RA, B2, hs], [HB])
        yield
        self.cp("dve", hs[:, ch:ch + 1], HB[:, 511:512], [HB], [hs])
        self.tt("pool", self.MIXT[2 + ch][:], G1[:], HB[:], ALU.mult, [G1, HB], [self.MIXT[2 + ch]])

    def gdn(self, l):
        c = self.c
        P = self.POOL
        fo = l * 60
        to = l * NTM
        QT = [P[0], P[1], P[2]]
        KT = QT
        KBT = [P[3], P[4], P[5]]
        qt = [b[:].bitcast(BF16)[:, 0:512] for b in QT]
        kt = [b[:].bitcast(BF16)[:, 512:1024] for b in KT]
        kbt = [b[:].bitcast(BF16)[:, 0:512] for b in KBT]
        KTM = [P[6], P[7], P[8], P[9]]
        VTM = [P[10], P[11], P[12], P[13]]
        ZS = [P[14], P[15], P[16], P[17]]
        SGT = P[22]
        AB = P[23]
        for grp in ((0, 1, 2), (3, 4, 5), (6, 7, 8)):
            gens = [self.gdn_front_chunk(l, ch, [P[14 + 3 * (ch % 3) + i] for i in range(3)], self.cbuf[ch % 3],
                                         qt, kt, QT, KT, KTM, VTM) for ch in grp]
            run_interleaved(gens)
        wb, wv = self.load_wa(l, "ba")
        ps = self.ps()
        for kc in range(8):
            self.mm(ps, ps[0:12, :], wv[:, kc, 0:12], self.XT[kc][:], [wb, self.XT[kc]], kc == 0, kc == 7)
        self.act(SGT[0:12, :], ps[0:12, :], AF.Sigmoid, [ps], [SGT])
        selb = self.cs("selb", slice(0, 12))
        for c3 in range(3):
            ps = self.ps()
            self.mm(ps, ps[:], selb[:, c3 * 128:(c3 + 1) * 128], SGT[0:12, :], [self.CST, SGT])
            self.tt("dve", kbt[c3], kt[c3], ps[:], ALU.mult, [KT[c3], ps], [KBT[c3]])
        wb, wv = self.load_wa(l, "tmB")
        ab = AB[:, 0:48].rearrange("p (s k) -> p s k", s=NS)
        for s in range(NS):
            ps = self.ps()
            for kc in range(8):
                self.mm(ps, ps[:, 0:396], self.XT[kc][:, s * 128:(s + 1) * 128], wv[:, kc, :], [wb, self.XT[kc]], kc == 0, kc == 7)
            self.act(ZS[s][:, 0:384], ps[:, 0:384], AF.Silu, [ps], [ZS[s]])
            self.cp("dve", AB[:, s * 12:(s + 1) * 12], ps[:, 384:396], [ps], [AB])
        def v46(o):
            return AB[:, o:o + 24].rearrange("p (s k) -> p s k", s=NS)
        a_v, b_v = ab[:, :, 0:6], ab[:, :, 6:12]
        Z1, AZ, EL, G, BETA = v46(48), v46(72), v46(96), v46(120), v46(144)
        dtb = self.TMS[:, to + 646:to + 652].unsqueeze(1).to_broadcast([128, NS, 6])
        nea = self.DER[:, l * 16:l * 16 + 6].unsqueeze(1).to_broadcast([128, NS, 6])
        self.tt("dve", Z1, a_v, dtb, ALU.add, [AB, self.TMS], [AB])
        self.stt(AZ, Z1, -1.0, Z1, ALU.mult, ALU.max, [AB], [AB])
        self.act(EL, AZ, AF.Exp, [AB], [AB], scale=-1.0)
        self.act(EL, EL, AF.Ln, [AB], [AB], bias=1.0)
        self.stt(G, Z1, 0.0, EL, ALU.max, ALU.add, [AB], [AB])
        self.tt("dve", G, G, nea, ALU.mult, [AB, self.DER], [AB])
        self.act(BETA, b_v, AF.Sigmoid, [AB], [AB])
        gng = self.TMS[:, to + 256:to + 640]
        for s in range(NS):
            self.gdn_pre(l, s, AB)
        prev_post = None
        for s in range(NS):
            gens = self.gdn_subtile_gens(l, s, qt, kt, kbt, QT, KT, KBT, KTM[s], VTM[s], AB)
            if prev_post is not None:
                gens.append(prev_post)
            run_interleaved(gens)
            prev_post = self.gdn_post(l, s, ZS[s], gng)
        run_interleaved([prev_post])

    def gdn_front_chunk(self, l, ch, B, CB, qt, kt, QT, KT, KTM, VTM):
        c = self.c
        CV, SQ, RN = B
        sqb = SQ[:].bitcast(BF16)[:, 0:512]
        fo, w = l * 60, self.FMS
        hist, hcol, wcol = self.HG[l], ch * 3, fo + 24 + ch * 4
        kind, c3 = ch // 3, ch % 3
        ps = self.ps()
        self.fm_mm(l, 14 + ch, ps)
        self.cp("pool", CB[:, 0:3], hist[:, hcol:hcol + 3], [hist], [CB])
        yield
        self.cp("act", CB[:, 3:515], ps[:], [ps], [CB])
        yield
        self.cp("pool", hist[:, hcol:hcol + 3], CB[:, 512:515], [CB], [hist])
        self.ts("dve", CV[:], CB[:, 3:515], w[:, wcol + 3:wcol + 4], None, ALU.mult, None, [CB, w], [CV])
        yield
        for k in range(3):
            self.stt(CV[:], CB[:, k:k + 512], w[:, wcol + k:wcol + k + 1], CV[:], ALU.mult, ALU.add, [CB, w, CV], [CV])
            yield
        self.act(CV[:], CV[:], AF.Silu, [CV], [CV])
        yield
        if kind == 2:
            pst = self.ps()
            for s in range(NS):
                self.tr(pst, pst[:, s * 128:(s + 1) * 128], CV[:, s * 128:(s + 1) * 128], [CV])
            yield
            for s in range(NS):
                self.cp("act" if s % 2 else "dve", VTM[s][:, c3 * 128:(c3 + 1) * 128], pst[:, s * 128:(s + 1) * 128], [pst], [VTM[s]])
            return
        self.tt("pool", sqb, CV[:], CV[:], ALU.mult, [CV], [SQ])
        yield
        pss = self.ps()
        self.mm(pss, pss[:], self.BONES[:], sqb, [self.BONES, SQ])
        yield
        self.act(RN[:], pss[:], AF.Sqrt, [pss], [RN], bias=1e-6)
        yield
        c.op("dve", lambda e: e.reciprocal(out=RN[:], in_=RN[:]), [RN], [RN])
        yield
        if kind == 0:
            self.stt(qt[c3], CV[:], 0.125, RN[:], ALU.mult, ALU.mult, [CV, RN], [QT[c3]])
            return
        self.tt("dve", CV[:], CV[:], RN[:], ALU.mult, [CV, RN], [CV])
        yield
        self.cp("pool", kt[c3], CV[:], [CV], [KT[c3]])
        pst = self.ps()
        for s in range(NS):
            self.tr(pst, pst[:, s * 128:(s + 1) * 128], CV[:, s * 128:(s + 1) * 128], [CV])
        yield
        for s in range(NS):
            self.cp("act" if s % 2 else "dve", KTM[s][:, c3 * 128:(c3 + 1) * 128], pst[:, s * 128:(s + 1) * 128], [pst], [KTM[s]])

    def gdn_pre(self, l, s, AB):
        SC = self.GSC[s]
        GCT = self.GCTB[s]
        g_s = AB[:, 120 + s * 6:120 + s * 6 + 6]
        beta_s = AB[:, 144 + s * 6:144 + s * 6 + 6]
        GC, GLB, EKT, EGC, BG, EGL0, EGL1 = (SC[:, i * 6:(i + 1) * 6] for i in range(7))
        ps = self.ps()
        self.mm(ps, ps[:, 0:6], self.cs("triU"), g_s, [self.CST, AB])
        self.mm(ps, ps[0:6, 128:256], g_s, self.cs("triU"), [self.CST, AB])
        self.cp("dve", GC, ps[:, 0:6], [ps], [SC])
        self.cp("dve", GCT[0:6, :], ps[0:6, 128:256], [ps], [GCT])
        ps = self.ps()
        self.mm(ps, ps[:, 0:6], self.cs("selB"), GC, [self.CST, SC])
        self.mm(ps, ps[:, 8:14], self.cs("sel63"), GC, [self.CST, SC])
        self.mm(ps, ps[:, 16:22], self.cs("sel127"), GC, [self.CST, SC])
        self.tt("dve", EKT, ps[:, 0:6], GC, ALU.subtract, [ps, SC], [SC])
        self.cp("dve", GLB, ps[:, 8:14], [ps], [SC])
        self.cp("dve", SC[:, 48:54], ps[:, 16:22], [ps], [SC])
        self.act(EKT, EKT, AF.Exp, [SC], [SC])
        self.act(EGC, GC, AF.Exp, [SC], [SC])
        self.act(EGL0, GLB, AF.Exp, [SC], [SC])
        self.act(EGL1, SC[:, 48:54], AF.Exp, [SC], [SC])
        self.tt("dve", BG, beta_s, EGC, ALU.mult, [AB, SC], [SC])

    def gdn_subtile_gens(self, l, s, qt, kt, kbt, QT, KT, KBT, KTM, VTM, AB):
        SC = self.GSC[s]
        GCT = self.GCTB[s]
        GV = self.GVB[s % 2]
        beta_s = AB[:, 144 + s * 6:144 + s * 6 + 6]
        GC, GLB, EKT, EGC, BG, EGL0, EGL1 = (SC[:, i * 6:(i + 1) * 6] for i in range(7))

        def bc(v):
            return v.unsqueeze(2).to_broadcast([128, 6, 64])

        def v3(ap):
            return ap.rearrange("p (h e) -> p h e", h=6)
        self.tt("dve", v3(GV[:, 0:384]), v3(VTM[:, 0:384]), bc(beta_s), ALU.mult, [VTM, AB], [GV])
        self.tt("dve", v3(GV[:, 384:768]), v3(KTM[:, 0:384]), bc(BG), ALU.mult, [KTM, SC], [GV])
        self.tt("dve", v3(GV[:, 768:1152]), v3(KTM[:, 0:384]), bc(EKT), ALU.mult, [KTM, SC], [GV])
        OG = self.GOG[s % 2]
        return [self.gdn_head(l, s, h, qt, kt, kbt, QT, KT, KBT, GV, AB, SC, GCT, OG, (EGL0, EGL1)) for h in range(6)]

    def gdn_post(self, l, s, ZS, gng):
        c = self.c
        OG = self.GOG[s % 2]
        st = self.sm()
        SQ = self.POOL[19]
        self.act(SQ[:, 0:384], OG[:, 0:384], AF.Square, [OG], [SQ])
        yield
        c.op("dve", lambda e: e.reduce_sum(out=st[:, 0:6], in_=SQ[:, 0:384].rearrange("p (h e) -> p h e", h=6), axis=AX.X), [SQ], [st])
        yield
        self.act(st[:, 8:14], st[:, 0:6], AF.Sqrt, [st], [st], bias=1e-6, scale=1.0 / 64)
        yield
        c.op("dve", lambda e: e.reciprocal(out=st[:, 16:22], in_=st[:, 8:14]), [st], [st])
        yield
        ogv = OG[:, 0:384].rearrange("p (h e) -> p h e", h=6)
        self.tt("dve", ogv, ogv, st[:, 16:22].unsqueeze(2).to_broadcast([128, 6, 64]), ALU.mult, [OG, st], [OG])
        yield
        self.tt("pool", OG[:, 0:384], OG[:, 0:384], gng, ALU.mult, [OG, self.TMS], [OG])
        yield
        self.tt("pool", OG[:, 0:384], OG[:, 0:384], ZS[:, 0:384], ALU.mult, [OG, ZS], [OG])
        yield
        pst = self.ps()
        for k in range(3):
            self.tr(pst, pst[:, k * 128:(k + 1) * 128], OG[:, k * 128:(k + 1) * 128], [OG])
        yield
        for k in range(3):
            self.cp("act" if k % 2 else "dve", self.MIXT[5 + k][:, s * 128:(s + 1) * 128], pst[:, k * 128:(k + 1) * 128], [pst], [self.MIXT[5 + k]])

    def gdn_head(self, l, s, h, qt, kt, kbt, QT, KT, KBT, GV, AB, SC, GCT, OG, EGL):
        cols = slice(s * 128, (s + 1) * 128)
        hp, c3 = (h % 2) * 64, h // 2
        hs = slice(h * 64, (h + 1) * 64)
        beta_s = AB[:, 144 + s * 6:144 + s * 6 + 6]
        GC, GLB, EKT, EGC, BG, EGL0, EGL1 = (SC[:, i * 6:(i + 1) * 6] for i in range(7))
        F, E, M, TTb, NPa, NPb, HV, HU = (self.GH[k][h] for k in ("F", "E", "M", "T", "Na", "Nb", "V", "U"))
        X1, X2 = F[:, 0:128], F[:, 128:256]
        E1s, E2s, E1i = E[:, 0:128], E[:, 128:256], E[:, 256:384]
        P0, N0, QKm = M[:, 0:128], M[:, 128:256], M[:, 256:384]
        Tt = TTb[:, 0:128]
        vb, kbg, ktl = GV[:, h * 64:(h + 1) * 64], GV[:, 384 + h * 64:384 + (h + 1) * 64], GV[:, 768 + h * 64:768 + (h + 1) * 64]
        vnew, wt = HV[:, 0:64], HV[:, 64:192]
        U = HU[:, 0:64]
        SG, SGB = self.SG[l], self.SGB[l]
        psB = self.ps()
        psk = psB
        oh = self.cs("oh6", slice(0, 6))
        self.mm(psB, psB[:, 0:128], oh[:, h * 128:(h + 1) * 128], GCT[0:6, :], [self.CST, GCT])
        ktc, kbc, qtc = kt[c3][hp:hp + 64, cols], kbt[c3][hp:hp + 64, cols], qt[c3][hp:hp + 64, cols]
        self.mm(psk, psk[:, 128:256], ktc, kbc, [KBT[c3], KT[c3]])
        self.mm(psk, psk[:, 256:384], kbc, ktc, [KBT[c3], KT[c3]])
        self.mm(psk, psk[:, 384:512], ktc, qtc, [QT[c3], KT[c3]])
        yield
        gcp = GC[:, h:h + 1]
        self.ts("dve", X1, psB[:, 0:128], gcp, 0.0, ALU.subtract, ALU.min, [psB, SC], [F])
        self.ts("dve", X2, psB[:, 0:128], gcp, 0.0, ALU.subtract, ALU.max, [psB, SC], [F])
        yield
        self.act(X1, X1, AF.Exp, [F], [F])
        self.act(X2, X2, AF.Exp, [F], [F], scale=-1.0)
        yield
        mo = CST["MSu"][0]
        self.tt("pool", E[:, 0:256], F[:, 0:256], self.CST[:, mo:mo + 256], ALU.mult, [F, self.CST], [E])
        self.tt("pool", E1i, X1, self.cs("MI"), ALU.mult, [F, self.CST], [E])
        yield
        self.tt("dve", M[:, 0:384], psk[:, 128:512], E[:, 0:384], ALU.mult, [psk, E], [M])
        yield
        self.tt("pool", Tt, self.cs("ident"), P0, ALU.subtract, [self.CST, M], [TTb])
        Nk, Pk, NB = N0, P0, M
        for j in range(0, 6):
            DB = NPa if j % 2 == 0 else NPb
            Nn, Pn = DB[:, 0:128], DB[:, 128:256]
            psn = self.ps()
            if j <= 4:
                self.mm(psn, psn[:, 0:128], Pk, Nk, [NB])
                if j <= 3:
                    self.mm(psn, psn[:, 128:256], Nk, Pk, [NB])
            if j >= 1:
                self.mm(psn, psn[:, 256:384], Nk, Tt, [NB, TTb])
            yield
            if j <= 4:
                self.cp("act", Nn, psn[:, 0:128], [psn], [DB])
                if j <= 3:
                    self.cp("act", Pn, psn[:, 128:256], [psn], [DB])
            if j >= 1:
                self.tt("dve", Tt, psn[:, 256:384], Tt, ALU.add, [psn, TTb], [TTb])
            if j <= 4:
                Nk, Pk, NB = Nn, Pn, DB
            yield
        psu = self.ps()
        self.mm(psu, psu[:, 0:64], Tt, vb, [TTb, GV])
        self.mm(psu, psu[hp:hp + 64, 128:256], kbg, Tt, [TTb, GV])
        yield
        self.cp("act", U, psu[:, 0:64], [psu], [HU])
        self.cp("act", wt[hp:hp + 64, :], psu[hp:hp + 64, 128:256], [psu], [HV])
        yield
        for cc in range(2):
            pc = slice(cc * 64, cc * 64 + 64)
            sgb = SGB[hp:hp + 64, c3 * 64:(c3 + 1) * 64]
            vn = vnew[pc, :]
            psw = self.ps()
            self.mm(psw, psw[pc, 0:64], wt[hp:hp + 64, cc * 64:cc * 64 + 64], sgb, [HV, SGB])
            self.mm(psw, psw[pc, 64:128], qt[c3][hp:hp + 64, s * 128 + cc * 64:s * 128 + cc * 64 + 64], sgb, [QT[c3], SGB])
            yield
            self.tt("dve", vn, U[pc, :], psw[pc, 0:64], ALU.subtract, [HU, psw], [HV])
            self.act(HU[pc, 64:128], psw[pc, 64:128], AF.Copy, [psw, SC], [HU], scale=EGC[pc, h:h + 1])
            yield
            pso = self.ps()
            self.mm(pso, pso[pc, 0:64], QKm[pc, cc * 64:cc * 64 + 64], vn, [M, HV])
            self.mm(pso, pso[hp:hp + 64, 64:128], ktl[pc, :], vn, [HV, GV])
            yield
            self.tt("dve", OG[pc, hs], HU[pc, 64:128], pso[pc, 0:64], ALU.add, [HU, pso], [OG])
            sgs = SG[hp:hp + 64, c3 * 64:(c3 + 1) * 64]
            self.stt(sgb, sgs, EGL[cc][hp:hp + 64, h:h + 1], pso[hp:hp + 64, 64:128], ALU.mult, ALU.add, [SG, SC, pso], [SGB])
            self.stt(sgs, sgs, EGL[cc][hp:hp + 64, h:h + 1], pso[hp:hp + 64, 64:128], ALU.mult, ALU.add, [SG, SC, pso], [SG])
            yield

    def ple(self, l, t):
        c = self.c
        PTB = [self.POOL[13], self.POOL[14]]
        PTTB = self.POOL[15]
        ptt = PTTB[:].bitcast(BF16)
        for sp in range(2):
            c.dma(PTB[sp][:].rearrange("p (s k) -> p s k", s=2),
                  self.p_d[l, t * T + sp * 256:t * T + (sp + 1) * 256, :].rearrange("(s p) k -> p s k", p=128), [], [PTB[sp]], PTB[sp])
        for k in range(2):
            ps = self.ps()
            for s in range(NS):
                pb = PTB[s // 2]
                self.tr(ps, ps[:, s * 128:(s + 1) * 128], pb[:, (s % 2) * 256 + k * 128:(s % 2) * 256 + (k + 1) * 128], [pb])
            self.cp("act" if k else "dve", ptt[:, k * T:(k + 1) * T], ps[:], [ps], [PTTB])
        wpb, wpv = self.load_wa(l, "pp")
        sgb, tb = self.POOL[11], self.POOL[12]
        for j in range(2):
            wb, wv = self.load_wa(l, f"pg{j}")
            for s in range(NS):
                psg, psp = self.ps(), self.ps()
                for kc in range(8):
                    self.mm(psg, psg[:], self.XT[kc][:, s * 128:(s + 1) * 128], wv[:, kc, :], [wb, self.XT[kc]], kc == 0, kc == 7)
                for k in range(2):
                    self.mm(psp, psp[:], ptt[:, k * T + s * 128:k * T + (s + 1) * 128], wpv[:, k, j * 512:(j + 1) * 512],
                            [wpb, PTTB], k == 0, k == 1)
                self.act(sgb[:], psg[:], AF.Sigmoid, [psg], [sgb])
                self.tt("dve", tb[:], sgb[:], psp[:], ALU.mult, [sgb, psp], [tb])
                xs = self.X[s][:, j * 512:(j + 1) * 512]
                self.stt(xs, xs, ALPHA, tb[:], ALU.mult, ALU.add, [self.X[s], tb], [self.X[s]])

    def build(self):
        c = self.c
        self.alloc()
        self.setup()
        self.alloc_main()
        self.cbuf = [c.sbuf(f"cb{i}", [128, 516], F32) for i in range(3)]
        self.cbi = 0
        self.sbuf_left = self.nc.sbuf_bytes_remaining
        for t in range(self.NT):
            for s in range(NS):
                c.dma(self.X[s][:], self.x_d[t * T + s * 128:t * T + (s + 1) * 128, :], [], [self.X[s]], self.X[s])
            if self.upto("rot"):
                self.rotary_tables(t)
            for l in range(L):
                if not self.upto("ffn1"):
                    break
                if l == 0:
                    self.make_xT()
                self.ffn(l, "f1", ALPHA)
                self.ln(l, 0)
                if l == 0:
                    self.dump("x1", t)
                if not self.upto("ret"):
                    break
                self.mixer(l, t)
                if l == 0 and "mix" in self.dbg_d:
                    for k in range(8):
                        self.cp("dve", self.POOL[k][:], self.MIXT[k][:], [self.MIXT[k]], [self.POOL[k]])
                        c.dma(self.dbg_d["mix"][k * 128:(k + 1) * 128, t * T:(t + 1) * T], self.POOL[k][:],
                              [self.POOL[k]], [], self.POOL[k])
                if not self.upto("wout"):
                    break
                self.ln(l, 1)
                if l == 0:
                    self.dump("x2", t)
                if not self.upto("ple"):
                    break
                self.ple(l, t)
                if not self.upto("all"):
                    break
                self.ffn(l, "f2", 1.0)
                OUT = None
                if l == L - 1:
                    P = self.POOL
                    OUT = [P[12], P[13], P[14], P[15], P[20], P[21], P[22], P[23]]
                self.ln(l, 2, with_xT=(l + 1 < L), OUT=OUT)
                if l == 0:
                    self.dump("x3", t)
            if self.upto("all"):
                for s in range(NS):
                    for j in range(2):
                        ob = OUT[2 * s + j]
                        c.dma(self.out_d[t * T + s * 128:t * T + (s + 1) * 128, j * 512:(j + 1) * 512], ob[:], [ob], [], ob)
            else:
                for s in range(NS):
                    c.dma(self.out_d[t * T + s * 128:t * T + (s + 1) * 128, :], self.X[s][:], [self.X[s]], [], self.X[s])
        c.wait_all("sp", self.X)
        if self.upto("all"):
            c.wait_all("sp", [self.POOL[i] for i in (12, 13, 14, 15, 20, 21, 22, 23)])
        for e in ("pe", "act", "dve", "pool"):
            pass
        self.stats = dict(ops=c.nops, waits=c.nwaits, sems=c.nsem, sbuf_left=self.sbuf_left)
        c.close()
        return self.nc


NPOOL = 24


def build_program(NT, dbg=(), stop="all"):
    nc = bass.Bass("TRN2", target_bir_lowering=False)
    kb = KB(nc, NT, dbg, stop)
    kb.build()
    return nc, kb


def make_inputs(inp, S):
    wblob = _build_wblob(inp)
    cst = _build_consts()
    fms, tms, lnp = _build_small(inp)
    maps = []
    B = inp["x"].shape[0]
    for b in range(B):
        maps.append({
            "x": np.ascontiguousarray(inp["x"][b, :S]),
            "p": np.ascontiguousarray(inp["p"][:, b, :S]),
            "pos": np.ascontiguousarray(inp["positions"][b:b + 1, :S]),
            "wblob": wblob, "cst": cst, "fms": fms, "tms": tms, "lnp": lnp,
        })
    return maps


def kernel(**inputs):
    inp = {k: np.asarray(v) for k, v in inputs.items()}
    B, S = inp["x"].shape[0], inp["x"].shape[1]
    nc, kb = build_program(S // T)
    maps = make_inputs(inp, S)
    res = run_bass_kernel_spmd(nc, maps, core_ids=list(range(B)))
    out = np.stack([np.asarray(r["out"]) for r in res.results], axis=0)
    return out.astype(np.float32)
```

```python
import contextlib
import math
import os
import numpy as np
import concourse.bass as bass
import concourse.mybir as mybir
from concourse.bass_utils import run_bass_kernel_spmd

F32 = mybir.dt.float32
BF16 = mybir.dt.bfloat16
I32 = mybir.dt.int32
AF = mybir.ActivationFunctionType
ALU = mybir.AluOpType
AX = mybir.AxisListType

D = 1024
DFF = 2816
NFC = DFF // 128
T = 512
NS = 4
L = 2
SEQ = 8192
NCORES = 8
ALPHA = 4.0 ** 0.25
LN_EPS = 1e-5
SLAB = 4096
TWO_PI = 2.0 * math.pi


class Tok:
    __slots__ = ("sem", "val", "eng", "sid")

    def __init__(self, sem, val, eng, sid):
        self.sem, self.val, self.eng, self.sid = sem, val, eng, sid


class Buf:
    __slots__ = ("name", "t", "last_w", "readers", "dsem", "dcount", "dsid", "psum", "rg", "pe_tok")

    def __init__(self, name, t=None):
        self.name, self.t = name, t
        self.psum = False
        self.rg = None
        self.pe_tok = None
        self.last_w = None
        self.readers = []
        self.dsem = None
        self.dcount = 0
        self.dsid = None

    def __getitem__(self, k):
        return self.t[k]


class Ctx:
    SELF_SYNC = True

    def __init__(self, nc):
        self.nc = nc
        self.es = contextlib.ExitStack()
        self.eng = {"pe": nc.tensor, "act": nc.scalar, "dve": nc.vector,
                    "pool": nc.gpsimd, "sp": nc.sync}
        self.sem = {}
        self.cnt = {}
        self.known = {e: {} for e in self.eng}
        self.nsem = 0
        for e in ("pe", "act", "dve", "pool"):
            self.sem[e] = self.new_sem("s_" + e)
            self.cnt[e] = 0
        self.nops = 0
        self.nwaits = 0

    def new_sem(self, name):
        s = self.es.enter_context(self.nc.semaphore(name))
        self.nsem += 1
        return (s, self.nsem)

    def sbuf(self, name, shape, dt, es=None):
        t = (es or self.es).enter_context(self.nc.sbuf_tensor(name, list(shape), dt))
        return Buf(name, t)

    def psum(self, name, shape, dt=F32):
        t = self.es.enter_context(self.nc.psum_tensor(name, list(shape), dt))
        b = Buf(name, t)
        b.psum = True
        return b

    def _waits(self, e, reads, writes, rg=None):
        toks = []
        for b in reads:
            if b.last_w is not None:
                toks.append(b.last_w)
            if b.psum:
                toks.extend(r for r in b.readers if r.eng != e)
        for b in writes:
            if b.psum and e == "pe" and rg is not None:
                if b.rg is not None and b.rg != rg and b.pe_tok is not None:
                    t = b.pe_tok
                    if self.known[e].get(t.sid, 0) < t.val:
                        self.eng[e].wait_ge(t.sem, t.val)
                        self.known[e][t.sid] = t.val
                        self.nwaits += 1
                b.rg = rg
            if b.last_w is not None:
                toks.append(b.last_w)
            toks.extend(b.readers)
        E = self.eng[e]
        kn = self.known[e]
        raw = set()
        for b in reads:
            if b.last_w is not None:
                raw.add(id(b.last_w))
        for t in toks:
            if t.eng == e and (e == "pe" or not self.SELF_SYNC or id(t) not in raw):
                continue
            if kn.get(t.sid, 0) >= t.val:
                continue
            E.wait_ge(t.sem, t.val)
            kn[t.sid] = t.val
            self.nwaits += 1

    def _commit(self, tok, reads, writes):
        for b in reads:
            b.readers = [r for r in b.readers if r.sid != tok.sid]
            b.readers.append(tok)
        for b in writes:
            b.last_w = tok
            b.readers = []

    def op(self, e, fn, reads=(), writes=(), rg=None):
        self._waits(e, reads, writes, rg)
        ins = fn(self.eng[e])
        self.cnt[e] += 1
        s, sid = self.sem[e]
        ins.then_inc(s, 1)
        tok = Tok(s, self.cnt[e], e, sid)
        self._commit(tok, reads, writes)
        if e == "pe":
            for b in writes:
                b.pe_tok = tok
        self.nops += 1
        return tok

    def dma(self, out_ap, in_ap, reads, writes, sb, q="sp"):
        self._waits(q, reads, writes)
        if sb.dsem is None:
            sb.dsem, sb.dsid = self.new_sem("d_" + sb.name)
        ins = self.eng[q].dma_start(out=out_ap, in_=in_ap)
        sb.dcount += 16
        ins.then_inc(sb.dsem, 16)
        tok = Tok(sb.dsem, sb.dcount, "dma", sb.dsid)
        self._commit(tok, reads, writes)
        self.nops += 1
        return tok

    def wait_all(self, e, bufs):
        self._waits(e, bufs, bufs)

    def close(self):
        self.es.close()


def _swap_half(cols):
    c = np.asarray(cols).reshape(-1, 64)
    return np.concatenate([c[:, 32:], c[:, :32]], axis=1).reshape(-1)


def _fm_chunks():
    rq = np.arange(0, 256)
    rk = np.arange(256, 512)
    rqs, rks = _swap_half(rq), _swap_half(rk)
    ch = [rq[:128], rq[128:], rqs[:128], rqs[128:], rk[:128], rk[128:], rks[:128], rks[128:]]
    for base in (1024, 1408, 1792, 2176, 2560):
        for c in range(3):
            ch.append(np.arange(base + c * 128, base + (c + 1) * 128))
    return ch


def _unit_table():
    tab = {}
    off = 0

    def add(name, kc, ncols):
        nonlocal off
        tab[name] = (off, kc, ncols)
        off += kc * ncols

    for f in ("f1", "f2"):
        for u in range(11):
            add(f"{f}_gu{u}", 8, 512)
        for j in range(4):
            add(f"{f}_d{j}", NFC, 256)
    for u in range(6):
        add(f"fm{u}", 8, 512 if u < 5 else 384)
    add("ba", 8, 12)
    add("tmA", 8, 512)
    add("tmB", 8, 396)
    for j in range(2):
        add(f"wo{j}", 8, 512)
    for j in range(2):
        add(f"pg{j}", 8, 512)
    add("pp", 2, 1024)
    add("lruw", 6, 128)
    return tab, off


UNITS, WC = _unit_table()
NSLAB = (WC + SLAB - 1) // SLAB
WCP = NSLAB * SLAB


def _unit(W, cols):
    K = W.shape[0]
    kc = K // 128
    cols = np.asarray(cols)
    return W[:, cols].reshape(kc, 128, len(cols)).transpose(1, 0, 2)


def _build_wblob(inp):
    blob = np.zeros((L, 128, WCP), np.float32)
    fmch = _fm_chunks()
    for l in range(L):
        def put(name, arr):
            off, kc, ncols = UNITS[name]
            assert arr.shape == (128, kc, ncols), (name, arr.shape)
            blob[l, :, off:off + kc * ncols] = arr.reshape(128, kc * ncols)
        for f, pre in (("f1", "ffn1"), ("f2", "ffn2")):
            Wg, Wu, Wd = inp[pre + "_w_gate"][l], inp[pre + "_w_up"][l], inp[pre + "_w_down"][l]
            for u in range(11):
                parts = []
                for cc in range(2):
                    cols = np.arange((2 * u + cc) * 128, (2 * u + cc + 1) * 128)
                    parts += [_unit(Wg, cols), _unit(Wu, cols)]
                put(f"{f}_gu{u}", np.concatenate(parts, axis=2))
            for j in range(4):
                put(f"{f}_d{j}", _unit(Wd, np.arange(j * 256, (j + 1) * 256)))
        Win = inp["w_in"][l]
        for u in range(6):
            cols = np.concatenate(fmch[4 * u:4 * u + 4])
            put(f"fm{u}", _unit(Win, cols))
        put("ba", _unit(Win, np.concatenate([np.arange(3334, 3340), np.arange(3328, 3334)])))
        put("tmA", _unit(Win, np.arange(512, 1024)))
        put("tmB", _unit(Win, np.arange(2944, 3340)))
        for j in range(2):
            put(f"wo{j}", _unit(inp["w_out"][l], np.arange(j * 512, (j + 1) * 512)))
            put(f"pg{j}", _unit(inp["ple_w_gate"][l], np.arange(j * 512, (j + 1) * 512)))
        put("pp", _unit(inp["ple_w_proj"][l], np.arange(1024)))
        bd = np.zeros((128, 6, 128), np.float32)
        for c in range(3):
            for gi, nm in enumerate(("lru_w_a", "lru_w_x")):
                w = inp[nm][l]
                bd[0:64, c * 2 + gi, 0:64] = w[2 * c]
                bd[64:128, c * 2 + gi, 64:128] = w[2 * c + 1]
        put("lruw", bd)
    return blob


CST = {}
_coff = 0
for _n, _w in (("ident", 128), ("triU", 128), ("ones", 128), ("MI", 128), ("MSu", 128), ("MSl", 128),
               ("sel63", 128), ("sel127", 128), ("selB", 128), ("bones", 128), ("intraT", 512),
               ("cross", 1024), ("tailf", 256), ("cdcol", 2), ("invf", 1), ("sgn", 1), ("selb", 384), ("oh6", 768)):
    CST[_n] = (_coff, _w)
    _coff += _w
NCST = _coff


def _build_consts():
    c = np.zeros((128, NCST), np.float32)

    def put(name, arr):
        off, w = CST[name]
        c[:, off:off + w] = arr

    p = np.arange(128)
    same = (p[:, None] // 64) == (p[None, :] // 64)
    put("ident", np.eye(128))
    put("triU", (same & (p[:, None] <= p[None, :])).astype(np.float32))
    put("ones", np.ones((128, 128)))
    put("MI", (same & (p[:, None] <= p[None, :])).astype(np.float32))
    put("MSu", (same & (p[:, None] < p[None, :])).astype(np.float32))
    put("MSl", (same & (p[:, None] > p[None, :])).astype(np.float32))
    s63 = np.zeros((128, 128)); s63[63, :] = 1
    s127 = np.zeros((128, 128)); s127[127, :] = 1
    sB = np.zeros((128, 128)); sB[63, :64] = 1; sB[127, 64:] = 1
    put("sel63", s63); put("sel127", s127); put("selB", sB)
    put("bones", same.astype(np.float32))
    lg = np.log1p(-np.exp2(-5.0 - np.arange(4, dtype=np.float32))).astype(np.float32)
    idx = (p % 64).astype(np.float32)
    intra = np.zeros((128, 4, 128), np.float32)
    for h in range(4):
        m = np.exp(np.abs(idx[:, None] - idx[None, :]) * lg[h]).astype(np.float32)
        intra[:, h, :] = np.where(same, m, 0.0)
    put("intraT", intra.reshape(128, 512))
    cross = np.zeros((128, 2, 512), np.float32)
    ii = (np.arange(512) % 64).astype(np.float32)
    for c2 in range(2):
        for hh in range(2):
            cross[hh * 64:(hh + 1) * 64, c2, :] = np.exp((ii + 1.0) * lg[2 * c2 + hh])[None, :]
    put("cross", cross.reshape(128, 1024))
    tail = np.zeros((128, 4, 64), np.float32)
    for h in range(4):
        tail[:, h, :] = np.exp((63.0 - idx) * lg[h])[:, None]
    put("tailf", tail.reshape(128, 256))
    cd = np.zeros((128, 2), np.float32)
    for c2 in range(2):
        for hh in range(2):
            cd[hh * 64:(hh + 1) * 64, c2] = np.exp(np.float32(64.0) * lg[2 * c2 + hh])
    put("cdcol", cd)
    half = 32
    invf = (10000.0 ** (-np.arange(half, dtype=np.float32) / half)).astype(np.float32)
    put("invf", invf[p % 32][:, None])
    put("sgn", np.where((p % 64) < 32, -1.0, 1.0)[:, None])
    selb = np.zeros((128, 384), np.float32)
    for h in range(6):
        selb[h, h * 64:(h + 1) * 64] = 1.0
    put("selb", selb)
    oh = np.zeros((128, 768), np.float32)
    for h in range(6):
        oh[h, h * 128:(h + 1) * 128] = 1.0
    put("oh6", oh)
    return c


NTM = 256 + 384 + 6 + 6


def _build_small(inp):
    fms = np.zeros((128, L, 60), np.float32)
    tms = np.zeros((1, L * NTM), np.float32)
    lnp = np.zeros((L * 3, 1, 2048), np.float32)
    for l in range(L):
        for c in range(3):
            sl = slice(c * 128, (c + 1) * 128)
            fms[:, l, c * 8 + 0:c * 8 + 4] = inp["lru_conv_w"][l][:, sl].T
            fms[:, l, c * 8 + 4] = inp["lru_conv_b"][l][sl]
            fms[:, l, c * 8 + 5] = inp["lru_b_a"][l][sl]
            fms[:, l, c * 8 + 6] = inp["lru_b_x"][l][sl]
            fms[:, l, c * 8 + 7] = inp["lru_lambda"][l][sl]
        for ch in range(9):
            sl = slice(ch * 128, (ch + 1) * 128)
            fms[:, l, 24 + ch * 4:24 + ch * 4 + 4] = inp["gdn_conv_w"][l][:, sl].T
        o = l * NTM
        tms[0, o:o + 256] = inp["ret_norm_g"][l]
        tms[0, o + 256:o + 640] = np.tile(inp["gdn_norm_g"][l], 6)
        tms[0, o + 640:o + 646] = inp["gdn_a_log"][l]
        tms[0, o + 646:o + 652] = inp["gdn_dt_bias"][l]
        for w, (g, b) in enumerate((("ln_ffn1_g", "ln_ffn1_b"), ("ln_mix_g", "ln_mix_b"), ("ln_ffn2_g", "ln_ffn2_b"))):
            lnp[l * 3 + w, 0, :1024] = inp[g][l]
            lnp[l * 3 + w, 0, 1024:] = inp[b][l]
    return fms.reshape(128, L * 60), tms, lnp


def run_interleaved(gens):
    gens = list(gens)
    while gens:
        for g in list(gens):
            try:
                next(g)
            except StopIteration:
                gens.remove(g)


class KB:
    STAGES = ["setup", "rot", "ffn1", "ret", "lru", "gdn", "wout", "ple", "all"]

    def upto(self, name):
        return self.STAGES.index(self.stop) >= self.STAGES.index(name)

    def __init__(self, nc, NT, dbg=(), stop="all"):
        self.nc = nc
        self.NT = NT
        self.stop = stop
        self.c = Ctx(nc)
        self.dbg = set(dbg)
        S = NT * T
        dt = nc.dram_tensor
        self.x_d = dt("x", [S, D], F32, kind="ExternalInput").ap()
        self.p_d = dt("p", [L, S, 256], F32, kind="ExternalInput").ap()
        self.pos_d = dt("pos", [1, S], I32, kind="ExternalInput").ap()
        self.wb_d = dt("wblob", [L, 128, WCP], F32, kind="ExternalInput").ap()
        self.cst_d = dt("cst", [128, NCST], F32, kind="ExternalInput").ap()
        self.fms_d = dt("fms", [128, L * 60], F32, kind="ExternalInput").ap()
        self.tms_d = dt("tms", [1, L * NTM], F32, kind="ExternalInput").ap()
        self.lnp_d = dt("lnp", [L * 3, 1, 2048], F32, kind="ExternalInput").ap()
        self.out_d = dt("out", [S, D], F32, kind="ExternalOutput").ap()
        self.wsc = Buf("wscr", dt("wscr", [L, 128, WCP], BF16, kind="Internal").ap())
        self.dbg_d = {}
        for n, shp in (("x1", [S, D]), ("mix", [D, S]), ("x2", [S, D]), ("x3", [S, D])):
            if n in self.dbg:
                self.dbg_d[n] = dt("dbg_" + n, shp, F32, kind="ExternalOutput").ap()

    def mm(self, psb, out, lhsT, rhs, reads, start=True, stop=True):
        rg = (lhsT.base_partition(), lhsT.partition_size())
        return self.c.op("pe", lambda e: e.matmul(out, lhsT=lhsT, rhs=rhs, start=start, stop=stop), reads, [psb], rg)

    def tr(self, psb, out, in_, reads):
        idn = self.cs("ident")
        rg = (in_.base_partition(), in_.partition_size())
        return self.c.op("pe", lambda e: e.transpose(out, in_, idn), list(reads) + [self.CST], [psb], rg)

    def act(self, out, in_, func, reads, writes, bias=None, scale=None):
        kw = {}
        if bias is not None:
            kw["bias"] = bias
        if scale is not None:
            kw["scale"] = scale
        return self.c.op("act", lambda e: e.activation(out=out, in_=in_, func=func, **kw), reads, writes)

    def tt(self, eng, out, in0, in1, op, reads, writes):
        return self.c.op(eng, lambda e: e.tensor_tensor(out=out, in0=in0, in1=in1, op=op), reads, writes)

    def ts(self, eng, out, in0, s1, s2, op0, op1, reads, writes):
        if s2 is None:
            return self.c.op(eng, lambda e: e.tensor_scalar(out=out, in0=in0, scalar1=s1, scalar2=None, op0=op0), reads, writes)
        return self.c.op(eng, lambda e: e.tensor_scalar(out=out, in0=in0, scalar1=s1, scalar2=s2, op0=op0, op1=op1), reads, writes)

    def stt(self, out, in0, scalar, in1, op0, op1, reads, writes):
        return self.c.op("dve", lambda e: e.scalar_tensor_tensor(out=out, in0=in0, scalar=scalar, in1=in1, op0=op0, op1=op1), reads, writes)

    def cp(self, eng, out, in_, reads, writes):
        if eng == "act":
            return self.c.op("act", lambda e: e.copy(out=out, in_=in_), reads, writes)
        return self.c.op(eng, lambda e: e.tensor_copy(out=out, in_=in_), reads, writes)

    def ps(self):
        b = self.PS[self.psi % self.ps_n]
        self.psi += 1
        return b

    def cs(self, name, rows=slice(0, 128)):
        off, w = CST[name]
        return self.CST[rows, off:off + w]

    def alloc(self):
        c = self.c
        self.PS = [c.psum(f"ps{i}", [128, 512]) for i in range(8)]
        self.psi = 0
        self.ps_n = 8
        self.CST = c.sbuf("cst_sb", [128, NCST], F32)
        self.FMS = c.sbuf("fms_sb", [128, L * 60], F32)
        self.TMS = c.sbuf("tms_sb", [128, L * NTM], F32)
        self.DER = c.sbuf("der_sb", [128, L * 16], F32)
        self.BONES = c.sbuf("bones_bf", [128, 128], BF16)
        self.LRUW = [c.sbuf(f"lruw{l}", [128, 768], BF16) for l in range(L)]
        self.SR = [c.sbuf(f"sr{l}", [128, 128], F32) for l in range(L)]
        self.SG = [c.sbuf(f"sg{l}", [128, 192], F32) for l in range(L)]
        self.SGB = [c.sbuf(f"sgb{l}", [128, 192], BF16) for l in range(L)]
        self.HS = [c.sbuf(f"hs{l}", [128, 4], F32) for l in range(L)]
        self.HL = [c.sbuf(f"hl{l}", [128, 3 * 3], F32) for l in range(L)]
        self.HG = [c.sbuf(f"hg{l}", [128, 9 * 3], F32) for l in range(L)]

    def alloc_main(self):
        c = self.c
        self.X = [c.sbuf(f"x{s}", [128, D], F32) for s in range(NS)]
        self.XT = [c.sbuf(f"xT{k}", [128, T], BF16) for k in range(8)]
        self.WA = [c.sbuf(f"wA{i}", [128, 4096], BF16) for i in range(3)]
        self.wai = 0
        self.WD = [c.sbuf(f"wD{i}", [128, NFC * 256], BF16) for i in range(2)]
        self.wdi = 0
        self.MIXT = [c.sbuf(f"mixT{k}", [128, T], BF16) for k in range(8)]
        self.COS = c.sbuf("cosT", [128, T], F32)
        self.SIN = c.sbuf("sinT", [128, T], F32)
        self.SM = [c.sbuf(f"sm{i}", [128, 64], F32) for i in range(8)]
        self.LNS = [c.sbuf(f"lns{i}", [128, 20], F32) for i in range(NS)]
        self.RST = [c.sbuf(f"rst{i}", [128, 32], F32) for i in range(NS)]
        self.smi = 0
        self.POOL = [c.sbuf(f"scr{i}", [128, 512], F32) for i in range(NPOOL)]
        self.GH = {}
        for k, (w, dtp) in {"F": (256, F32), "E": (384, BF16), "M": (384, BF16), "T": (128, BF16), "Na": (256, BF16),
                            "Nb": (256, BF16), "V": (192, BF16), "U": (128, F32)}.items():
            self.GH[k] = [c.sbuf(f"gh{k}{h}", [128, w], dtp) for h in range(6)]
        self.GSC = [c.sbuf(f"gsc{i}", [128, 64], F32) for i in range(4)]
        self.GCTB = [c.sbuf(f"gct{i}", [128, 128], F32) for i in range(4)]
        self.GVB = [c.sbuf(f"gv{i}", [128, 1152], BF16) for i in range(2)]
        self.GOG = [c.sbuf(f"gog{i}", [128, 384], F32) for i in range(2)]

    def sm(self):
        b = self.SM[self.smi % len(self.SM)]
        self.smi += 1
        return b

    def setup(self):
        c = self.c
        c.dma(self.CST[:], self.cst_d, [], [self.CST], self.CST)
        c.dma(self.FMS[:], self.fms_d, [], [self.FMS], self.FMS)
        c.dma(self.TMS[:], self.tms_d.partition_broadcast(128), [], [self.TMS], self.TMS)
        self.cp("dve", self.BONES[:], self.cs("bones"), [self.CST], [self.BONES])
        c.op("pool", lambda e: e.memset(self.DER[:, 12:13], -0.5), [], [self.DER])
        for l in range(L):
            o = l * 16
            self.act(self.DER[:, o:o + 6], self.TMS[:, l * NTM + 640:l * NTM + 646], AF.Exp, [self.TMS], [self.DER])
            self.ts("dve", self.DER[:, o:o + 6], self.DER[:, o:o + 6], -1.0, None, ALU.mult, None, [self.DER], [self.DER])
            for ch in range(3):
                lam = self.FMS[:, l * 60 + ch * 8 + 7:l * 60 + ch * 8 + 8]
                self.act(self.DER[:, o + 6 + ch:o + 7 + ch], lam, AF.Exp, [self.FMS], [self.DER], scale=-1.0)
            self.act(self.DER[:, o + 6:o + 9], self.DER[:, o + 6:o + 9], AF.Ln, [self.DER], [self.DER], bias=1.0)
            self.ts("dve", self.DER[:, o + 9:o + 12], self.DER[:, o + 6:o + 9], -16.0, None, ALU.mult, None, [self.DER], [self.DER])
            self.ts("dve", self.DER[:, o + 6:o + 9], self.DER[:, o + 6:o + 9], -8.0, None, ALU.mult, None, [self.DER], [self.DER])
            for b in (self.SR[l], self.SG[l], self.HS[l], self.HL[l], self.HG[l]):
                c.op("pool", lambda e, b=b: e.memset(b[:], 0.0), [], [b])
            c.op("pool", lambda e, b=self.SGB[l]: e.memset(b[:], 0.0), [], [self.SGB[l]])
        with contextlib.ExitStack() as es:
            st_f = [c.sbuf(f"cvf{i}", [128, SLAB], F32, es) for i in range(2)]
            st_b = [c.sbuf(f"cvb{i}", [128, SLAB], BF16, es) for i in range(2)]
            engs = ("dve", "pool", "act")
            k = 0
            for l in range(L):
                for s in range(NSLAB):
                    f, b = st_f[k % 2], st_b[k % 2]
                    c.dma(f[:], self.wb_d[l, :, s * SLAB:(s + 1) * SLAB], [], [f], f)
                    self.cp(engs[k % 3], b[:], f[:], [f], [b])
                    c.dma(self.wsc[l, :, s * SLAB:(s + 1) * SLAB], b[:], [b], [self.wsc], b)
                    k += 1
            for e in ("sp", "pe", "act", "dve", "pool"):
                c.wait_all(e, st_f + st_b)
        for l in range(L):
            off, kc, ncols = UNITS["lruw"]
            c.dma(self.LRUW[l][:], self.wsc[l, :, off:off + 768], [self.wsc], [self.LRUW[l]], self.LRUW[l])

    def load_wa(self, l, name):
        off, kc, ncols = UNITS[name]
        b = self.WA[self.wai % len(self.WA)]
        self.wai += 1
        n = kc * ncols
        self.c.dma(b[:, 0:n], self.wsc[l, :, off:off + n], [self.wsc], [b], b)
        return b, b[:, 0:n].rearrange("p (k n) -> p k n", k=kc)

    def load_wd(self, l, name):
        off, kc, ncols = UNITS[name]
        b = self.WD[self.wdi % len(self.WD)]
        self.wdi += 1
        n = kc * ncols
        self.c.dma(b[:, 0:n], self.wsc[l, :, off:off + n], [self.wsc], [b], b)
        return b, b[:, 0:n].rearrange("p (k n) -> p k n", k=kc)

    def make_xT(self):
        for kc in range(8):
            ps = self.ps()
            for s in range(NS):
                self.tr(ps, ps[:, s * 128:(s + 1) * 128], self.X[s][:, kc * 128:(kc + 1) * 128], [self.X[s]])
            self.cp("act" if kc % 2 else "dve", self.XT[kc][:], ps[:], [ps], [self.XT[kc]])

    def hT(self, fc):
        b = self.POOL[fc // 2]
        return b, b[:].bitcast(BF16)[:, (fc % 2) * 512:(fc % 2 + 1) * 512]

    def ffn(self, l, f, alpha):
        sg = [self.POOL[11], self.POOL[12]]
        for u in range(11):
            wb, wv = self.load_wa(l, f"{f}_gu{u}")
            for cc in range(2):
                fc = 2 * u + cc
                psg, psu = self.ps(), self.ps()
                for kc in range(8):
                    self.mm(psg, psg[:], wv[:, kc, cc * 256:cc * 256 + 128], self.XT[kc][:], [wb, self.XT[kc]], kc == 0, kc == 7)
                for kc in range(8):
                    self.mm(psu, psu[:], wv[:, kc, cc * 256 + 128:cc * 256 + 256], self.XT[kc][:], [wb, self.XT[kc]], kc == 0, kc == 7)
                sb = sg[fc % 2]
                self.act(sb[:], psg[:], AF.Silu, [psg], [sb])
                hb, hv = self.hT(fc)
                self.stt(hv, psu[:], 0.5, sb[:], ALU.mult, ALU.mult, [psu, sb], [hb])
        for j in range(4):
            wb, wv = self.load_wd(l, f"{f}_d{j}")
            for s in range(NS):
                ps = self.ps()
                for fc in range(NFC):
                    hb, hv = self.hT(fc)
                    self.mm(ps, ps[:, 0:256], hv[:, s * 128:(s + 1) * 128], wv[:, fc, :], [wb, hb], fc == 0, fc == NFC - 1)
                xs = self.X[s][:, j * 256:(j + 1) * 256]
                self.stt(xs, xs, alpha, ps[:, 0:256], ALU.mult, ALU.add, [self.X[s], ps], [self.X[s]])

    def ln(self, l, w, with_xT=True, OUT=None):
        c = self.c
        LB = [self.POOL[16 + k] for k in range(4)]
        for k in range(4):
            c.dma(LB[k][:], self.lnp_d[l * 3 + w][:, k * 512:(k + 1) * 512].partition_broadcast(128), [], [LB[k]], LB[k])
        run_interleaved([self.ln_chain(s, LB, with_xT, OUT) for s in range(NS)])

    def ln_chain(self, s, LB, with_xT, OUT=None):
        c = self.c
        x = self.X[s]
        st = self.LNS[s]
        c.op("dve", lambda e: e.bn_stats(out=st[:, 0:6], in_=x[:, 0:512]), [x], [st])
        c.op("dve", lambda e: e.bn_stats(out=st[:, 6:12], in_=x[:, 512:1024]), [x], [st])
        yield
        c.op("dve", lambda e: e.bn_aggr(out=st[:, 12:14], in_=st[:, 0:12]), [st], [st])
        yield
        self.ts("dve", st[:, 14:15], st[:, 13:14], LN_EPS, None, ALU.add, None, [st], [st])
        yield
        self.tt("pool", st[:, 15:16], st[:, 14:15], self.DER[:, 12:13], ALU.pow, [st, self.DER], [st])
        yield
        self.stt(st[:, 16:17], st[:, 12:13], -1.0, st[:, 15:16], ALU.mult, ALU.mult, [st], [st])
        yield
        self.act(x[:], x[:], AF.Identity, [x, st], [x], bias=st[:, 16:17], scale=st[:, 15:16])
        yield
        self.tt("dve", x[:, 0:512], x[:, 0:512], LB[0][:], ALU.mult, [x, LB[0]], [x])
        self.tt("pool", x[:, 512:1024], x[:, 512:1024], LB[1][:], ALU.mult, [x, LB[1]], [x])
        yield
        if OUT is None:
            self.tt("dve", x[:, 0:512], x[:, 0:512], LB[2][:], ALU.add, [x, LB[2]], [x])
            self.tt("pool", x[:, 512:1024], x[:, 512:1024], LB[3][:], ALU.add, [x, LB[3]], [x])
        else:
            self.tt("dve", OUT[2 * s][:], x[:, 0:512], LB[2][:], ALU.add, [x, LB[2]], [OUT[2 * s]])
            self.tt("pool", OUT[2 * s + 1][:], x[:, 512:1024], LB[3][:], ALU.add, [x, LB[3]], [OUT[2 * s + 1]])
        yield
        if not with_xT:
            return
        for j in range(2):
            ps = self.ps()
            for k in range(4):
                kc = 4 * j + k
                self.tr(ps, ps[:, k * 128:(k + 1) * 128], x[:, kc * 128:(kc + 1) * 128], [x])
            yield
            for k in range(4):
                kc = 4 * j + k
                self.cp("act" if k % 2 else "dve", self.XT[kc][:, s * 128:(s + 1) * 128], ps[:, k * 128:(k + 1) * 128], [ps], [self.XT[kc]])
            yield

    def dump(self, name, t, src=None):
        if name not in self.dbg_d:
            return
        src = src or self.X
        dd = self.dbg_d[name]
        for s in range(NS):
            self.c.dma(dd[t * T + s * 128:t * T + (s + 1) * 128, :], src[s][:], [src[s]], [], src[s])

    def rotary_tables(self, t):
        c = self.c
        pi_b = self.POOL[0]
        pf, ang, kf, r = self.POOL[1], self.POOL[2], self.POOL[3], self.POOL[4]
        m = self.POOL[5]
        piv = pi_b[:].bitcast(I32)
        c.dma(piv, self.pos_d[:, t * T:(t + 1) * T].partition_broadcast(128), [], [pi_b], pi_b)
        self.cp("dve", pf[:], piv, [pi_b], [pf])
        self.ts("dve", ang[:], pf[:], self.cs("invf"), None, ALU.mult, None, [pf, self.CST], [ang])
        kiv = pi_b[:].bitcast(I32)
        self.ts("dve", kiv, ang[:], 1.0 / TWO_PI, None, ALU.mult, None, [ang], [pi_b])
        self.cp("dve", kf[:], kiv, [pi_b], [kf])
        C1 = float(np.float32(6.28125))
        C2 = float(TWO_PI - 6.28125)
        self.stt(r[:], kf[:], -C1, ang[:], ALU.mult, ALU.add, [kf, ang], [r])
        self.stt(r[:], kf[:], -C2, r[:], ALU.mult, ALU.add, [kf, r], [r])
        for dst, shift, sgn in ((self.SIN, 0.0, True), (self.COS, math.pi / 2, False)):
            rr = dst
            self.ts("dve", rr[:], r[:], shift, None, ALU.add, None, [r], [rr])
            self.ts("dve", m[:], rr[:], math.pi, None, ALU.is_gt, None, [rr], [m])
            self.stt(rr[:], m[:], -TWO_PI, rr[:], ALU.mult, ALU.add, [m, rr], [rr])
            self.ts("dve", m[:], rr[:], -math.pi, None, ALU.is_lt, None, [rr], [m])
            self.stt(rr[:], m[:], TWO_PI, rr[:], ALU.mult, ALU.add, [m, rr], [rr])
            self.ts("dve", rr[:], rr[:], math.pi, -math.pi, ALU.min, ALU.max, [rr], [rr])
            self.act(rr[:], rr[:], AF.Sin, [rr], [rr])
            if sgn:
                self.ts("dve", rr[:], rr[:], self.cs("sgn"), None, ALU.mult, None, [rr, self.CST], [rr])

    def fm_mm(self, l, i, ps):
        u = i // 4
        if self.fm_cur is None or self.fm_cur[0] != (l, u):
            wb, wv = self.load_wa(l, f"fm{u}")
            self.fm_cur = ((l, u), wb, wv)
        _, wb, wv = self.fm_cur
        o = (i % 4) * 128
        for kc in range(8):
            self.mm(ps, ps[:], wv[:, kc, o:o + 128], self.XT[kc][:], [wb, self.XT[kc]], kc == 0, kc == 7)

    def mixer(self, l, t):
        self.fm_cur = None
        self.retention(l)
        if self.upto("lru"):
            self.lru(l)
        if self.upto("gdn"):
            self.gdn(l)
        if not self.upto("wout"):
            return
        for j in range(2):
            wb, wv = self.load_wa(l, f"wo{j}")
            for s in range(NS):
                ps = self.ps()
                for k in range(8):
                    self.mm(ps, ps[:], self.MIXT[k][:, s * 128:(s + 1) * 128], wv[:, k, :], [wb, self.MIXT[k]], k == 0, k == 7)
                xs = self.X[s][:, j * 512:(j + 1) * 512]
                self.stt(xs, xs, ALPHA, ps[:], ALU.mult, ALU.add, [self.X[s], ps], [self.X[s]])

    def retention(self, l):
        c = self.c
        P = self.POOL
        QK, KK, QC, KTM, VTM, VTT = P[2], P[3], P[4], P[5], P[6], P[7]
        GS = [P[8], P[9]]
        SRB = P[10]
        bufs = dict(QK=QK, KK=KK, QC=QC, KTM=KTM, VTM=VTM, VTT=VTT, GS=GS, SRB=SRB)
        run_interleaved([self.ret_q(l, c2, P[0 + c2], P[11 + c2], bufs) for c2 in range(2)])
        run_interleaved([self.ret_k(l, c2, P[13 + 3 * c2], P[14 + 3 * c2], P[15 + 3 * c2], bufs) for c2 in range(2)])
        wb, wv = self.load_wa(l, "tmA")
        run_interleaved([self.ret_vg(l, s, wb, wv, bufs) for s in range(NS)])
        ktm = KTM[:].bitcast(BF16); vtt = VTT[:].bitcast(BF16); srb = SRB[:].bitcast(BF16)
        SR = self.SR[l]
        pk = [self.ps(), self.ps()]
        for n in range(8):
            s, cc = n // 2, n % 2
            pc = slice(cc * 64, cc * 64 + 64)
            pb = pk[n // 4]
            o = (n % 4) * 128
            for h in range(4):
                hp, c2 = (h % 2) * 64, h // 2
                self.mm(pb, pb[hp:hp + 64, o + c2 * 64:o + c2 * 64 + 64], ktm[pc, s * 256 + h * 64:s * 256 + h * 64 + 64],
                        vtt[pc, s * 256 + h * 64:s * 256 + h * 64 + 64], [KTM, VTT])
        for n in range(8):
            pb = pk[n // 4]
            o = (n % 4) * 128
            self.cp("dve", srb[:, n * 128:(n + 1) * 128], SR[:], [SR], [SRB])
            for c2 in range(2):
                cd = self.cs("cdcol")[:, c2:c2 + 1]
                self.stt(SR[:, c2 * 64:c2 * 64 + 64], SR[:, c2 * 64:c2 * 64 + 64], cd, pb[:, o + c2 * 64:o + c2 * 64 + 64],
                         ALU.mult, ALU.add, [SR, self.CST, pb], [SR])
        self.psi = 0
        self.ps_n = 6
        Q = [P[0], P[1], P[11], P[12]]
        R = [P[13], P[14], P[15], P[16]]
        run_interleaved([self.ret_out(l, s, Q[s], R[s], self.PS[6 + s // 2], bufs) for s in range(NS)])
        self.ps_n = 8

    def ret_q(self, l, c2, ta, tb, B):
        QK, QC = B["QK"], B["QC"]
        qr = QK[:].bitcast(BF16); qc = QC[:].bitcast(BF16)
        cross = self.cs("cross")
        psq, psqs = self.ps(), self.ps()
        self.fm_mm(l, 0 + c2, psq)
        self.fm_mm(l, 2 + c2, psqs)
        yield
        self.stt(ta[:], psq[:], 0.125, self.COS[:], ALU.mult, ALU.mult, [psq, self.COS], [ta])
        self.stt(tb[:], psqs[:], 0.125, self.SIN[:], ALU.mult, ALU.mult, [psqs, self.SIN], [tb])
        yield
        self.tt("pool", qr[:, c2 * 512:(c2 + 1) * 512], ta[:], tb[:], ALU.add, [ta, tb], [QK])
        yield
        self.tt("pool", qc[:, c2 * 512:(c2 + 1) * 512], qr[:, c2 * 512:(c2 + 1) * 512], cross[:, c2 * 512:(c2 + 1) * 512],
                ALU.mult, [QK, self.CST], [QC])

    def ret_k(self, l, c2, ta, tb, KF, B):
        KK, KTM = B["KK"], B["KTM"]
        kr = KK[:].bitcast(BF16); ktm = KTM[:].bitcast(BF16)
        psk, psks = self.ps(), self.ps()
        self.fm_mm(l, 4 + c2, psk)
        self.fm_mm(l, 6 + c2, psks)
        yield
        self.tt("dve", ta[:], psk[:], self.COS[:], ALU.mult, [psk, self.COS], [ta])
        self.tt("dve", tb[:], psks[:], self.SIN[:], ALU.mult, [psks, self.SIN], [tb])
        yield
        self.tt("pool", KF[:], ta[:], tb[:], ALU.add, [ta, tb], [KF])
        yield
        self.cp("pool", kr[:, c2 * 512:(c2 + 1) * 512], KF[:], [KF], [KK])
        ps = self.ps()
        for s in range(NS):
            self.tr(ps, ps[:, s * 128:(s + 1) * 128], KF[:, s * 128:(s + 1) * 128], [KF])
        yield
        for s in range(NS):
            self.cp("act", ktm[:, s * 256 + c2 * 128:s * 256 + (c2 + 1) * 128], ps[:, s * 128:(s + 1) * 128], [ps], [KTM])

    def ret_vg(self, l, s, wb, wv, B):
        VTM, VTT, GS = B["VTM"], B["VTT"], B["GS"]
        vtm = VTM[:].bitcast(BF16); vtt = VTT[:].bitcast(BF16)
        ps = self.ps()
        for kc in range(8):
            self.mm(ps, ps[:], self.XT[kc][:, s * 128:(s + 1) * 128], wv[:, kc, :], [wb, self.XT[kc]], kc == 0, kc == 7)
        yield
        self.cp("dve", vtm[:, s * 256:(s + 1) * 256], ps[:, 0:256], [ps], [VTM])
        self.tt("dve", vtt[:, s * 256:(s + 1) * 256], ps[:, 0:256], self.cs("tailf"), ALU.mult, [ps, self.CST], [VTT])
        yield
        g = GS[s // 2]
        self.act(g[:, (s % 2) * 256:(s % 2 + 1) * 256], ps[:, 256:512], AF.Silu, [ps], [g])

    def ret_out(self, l, s, Qb, Rb, pso, B):
        c = self.c
        QK, KK, QC, VTM, GS, SRB = B["QK"], B["KK"], B["QC"], B["VTM"], B["GS"], B["SRB"]
        qr = QK[:].bitcast(BF16); kr = KK[:].bitcast(BF16); qc = QC[:].bitcast(BF16)
        vtm = VTM[:].bitcast(BF16); srb = SRB[:].bitcast(BF16)
        ptall = Rb[:].bitcast(BF16)
        intra = self.cs("intraT")
        rng = self.TMS[:, l * NTM:l * NTM + 256]
        po = (s % 2) * 256
        st = self.RST[s]
        sq, on = Qb[:, 0:256], Qb[:, 256:512]
        for h in range(4):
            hp, c2 = (h % 2) * 64, h // 2
            pst = self.ps()
            self.mm(pst, pst[:, 0:128], kr[hp:hp + 64, c2 * 512 + s * 128:c2 * 512 + (s + 1) * 128],
                    qr[hp:hp + 64, c2 * 512 + s * 128:c2 * 512 + (s + 1) * 128], [KK, QK])
            yield
            ptv = ptall[:, h * 128:(h + 1) * 128]
            self.tt("dve", ptv, pst[:, 0:128], intra[:, h * 128:(h + 1) * 128], ALU.mult, [pst, self.CST], [Rb])
            yield
            self.mm(pso, pso[:, po + h * 64:po + h * 64 + 64], ptv, vtm[:, s * 256 + h * 64:s * 256 + h * 64 + 64], [Rb, VTM], True, False)
            for cc in range(2):
                pc = slice(cc * 64, cc * 64 + 64)
                n = 2 * s + cc
                self.mm(pso, pso[pc, po + h * 64:po + h * 64 + 64],
                        qc[hp:hp + 64, c2 * 512 + s * 128 + cc * 64:c2 * 512 + s * 128 + cc * 64 + 64],
                        srb[hp:hp + 64, n * 128 + c2 * 64:n * 128 + c2 * 64 + 64], [QC, SRB], False, cc == 1)
            yield
        po_ap = pso[:, po:po + 256]
        c.op("dve", lambda e: e.reduce_sum(out=st[:, 0:4], in_=po_ap.rearrange("p (h e) -> p h e", h=4), axis=AX.X), [pso], [st])
        yield
        self.act(sq, po_ap, AF.Square, [pso], [Qb])
        yield
        c.op("dve", lambda e: e.reduce_sum(out=st[:, 4:8], in_=sq.rearrange("p (h e) -> p h e", h=4), axis=AX.X), [Qb], [st])
        yield
        self.ts("dve", st[:, 8:12], st[:, 0:4], 1.0 / 64, None, ALU.mult, None, [st], [st])
        yield
        self.tt("dve", st[:, 12:16], st[:, 8:12], st[:, 8:12], ALU.mult, [st], [st])
        yield
        self.stt(st[:, 16:20], st[:, 4:8], 1.0 / 64, st[:, 12:16], ALU.mult, ALU.subtract, [st], [st])
        yield
        self.act(st[:, 20:24], st[:, 16:20], AF.Sqrt, [st], [st], bias=1e-5)
        yield
        c.op("dve", lambda e: e.reciprocal(out=st[:, 24:28], in_=st[:, 20:24]), [st], [st])
        yield
        for h in range(4):
            self.ts("dve", on[:, h * 64:(h + 1) * 64], pso[:, po + h * 64:po + (h + 1) * 64], st[:, 8 + h:9 + h], st[:, 24 + h:25 + h],
                    ALU.subtract, ALU.mult, [pso, st], [Qb])
        yield
        self.tt("pool", on, on, rng, ALU.mult, [Qb, self.TMS], [Qb])
        yield
        g = GS[s // 2]
        self.tt("pool", on, on, g[:, (s % 2) * 256:(s % 2 + 1) * 256], ALU.mult, [Qb, g], [Qb])
        yield
        pst = self.ps()
        for k in range(2):
            self.tr(pst, pst[:, k * 128:(k + 1) * 128], on[:, k * 128:(k + 1) * 128], [Qb])
        yield
        for k in range(2):
            self.cp("act", self.MIXT[k][:, s * 128:(s + 1) * 128], pst[:, k * 128:(k + 1) * 128], [pst], [self.MIXT[k]])

    def conv_chunk(self, l, ps, hist, hcol, wcol, bias_col, out_b):
        CB = self.cbuf[self.cbi % 2]
        self.cbi += 1
        self.cp("pool", CB[:, 0:3], hist[:, hcol:hcol + 3], [hist], [CB])
        self.cp("act", CB[:, 3:515], ps[:], [ps], [CB])
        self.cp("pool", hist[:, hcol:hcol + 3], CB[:, 512:515], [CB], [hist])
        w = self.FMS
        if bias_col is None:
            self.ts("dve", out_b[:, 0:512], CB[:, 3:515], w[:, wcol + 3:wcol + 4], None, ALU.mult, None, [CB, w], [out_b])
        else:
            self.ts("dve", out_b[:, 0:512], CB[:, 3:515], w[:, wcol + 3:wcol + 4], w[:, bias_col:bias_col + 1], ALU.mult, ALU.add, [CB, w], [out_b])
        for k in range(3):
            self.stt(out_b[:, 0:512], CB[:, k:k + 512], w[:, wcol + k:wcol + k + 1], out_b[:, 0:512], ALU.mult, ALU.add, [CB, w, out_b], [out_b])

    def lru(self, l):
        P = self.POOL
        gens = [self.lru_chunk(l, ch, [P[7 * ch + i] for i in range(7)], self.cbuf[ch]) for ch in range(3)]
        run_interleaved(gens)

    def lru_chunk(self, l, ch, B, CB):
        c = self.c
        XC, XCB, RA, IG, B2, HB, G1 = B
        xcb = XCB[:].bitcast(BF16)[:, 0:512]
        lw, der, fo, w = self.LRUW[l], self.DER, l * 60, self.FMS
        hist, hcol, wcol, bcol = self.HL[l], ch * 3, fo + ch * 8, fo + ch * 8 + 4
        hs = self.HS[l]
        ps = self.ps()
        self.fm_mm(l, 8 + ch, ps)
        self.cp("pool", CB[:, 0:3], hist[:, hcol:hcol + 3], [hist], [CB])
        yield
        self.cp("act", CB[:, 3:515], ps[:], [ps], [CB])
        psg = self.ps()
        self.fm_mm(l, 11 + ch, psg)
        yield
        self.cp("pool", hist[:, hcol:hcol + 3], CB[:, 512:515], [CB], [hist])
        self.ts("dve", XC[:], CB[:, 3:515], w[:, wcol + 3:wcol + 4], w[:, bcol:bcol + 1], ALU.mult, ALU.add, [CB, w], [XC])
        self.act(G1[:], psg[:], AF.Square, [psg], [G1])
        yield
        self.stt(XC[:], CB[:, 0:512], w[:, wcol:wcol + 1], XC[:], ALU.mult, ALU.add, [CB, w, XC], [XC])
        self.ts("dve", G1[:], G1[:], 0.044715, 1.0, ALU.mult, ALU.add, [G1], [G1])
        yield
        self.stt(XC[:], CB[:, 1:513], w[:, wcol + 1:wcol + 2], XC[:], ALU.mult, ALU.add, [CB, w, XC], [XC])
        self.tt("dve", G1[:], G1[:], psg[:], ALU.mult, [G1, psg], [G1])
        yield
        self.stt(XC[:], CB[:, 2:514], w[:, wcol + 2:wcol + 3], XC[:], ALU.mult, ALU.add, [CB, w, XC], [XC])
        self.act(G1[:], G1[:], AF.Sigmoid, [G1], [G1], scale=1.5957691216057308)
        yield
        self.cp("pool", xcb, XC[:], [XC], [XCB])
        self.tt("dve", G1[:], G1[:], psg[:], ALU.mult, [G1, psg], [G1])
        yield
        psr, psi = self.ps(), self.ps()
        self.mm(psr, psr[:], lw[:, (ch * 2) * 128:(ch * 2 + 1) * 128], xcb, [lw, XCB])
        self.mm(psi, psi[:], lw[:, (ch * 2 + 1) * 128:(ch * 2 + 2) * 128], xcb, [lw, XCB])
        yield
        self.act(RA[:], psr[:], AF.Sigmoid, [psr, w], [RA], bias=w[:, fo + ch * 8 + 5:fo + ch * 8 + 6])
        self.act(IG[:], psi[:], AF.Sigmoid, [psi, w], [IG], bias=w[:, fo + ch * 8 + 6:fo + ch * 8 + 7])
        yield
        self.act(RA[:], RA[:], AF.Exp, [RA, der], [RA], scale=der[:, l * 16 + 6 + ch:l * 16 + 7 + ch])
        self.tt("pool", IG[:], IG[:], XC[:], ALU.mult, [IG, XC], [IG])
        yield
        self.tt("dve", B2[:], RA[:], RA[:], ALU.mult, [RA], [B2])
        yield
        self.ts("dve", B2[:], B2[:], -1.0, 1.0, ALU.mult, ALU.add, [B2], [B2])
        yield
        self.act(B2[:], B2[:], AF.Sqrt, [B2], [B2])
        yield
        self.tt("dve", B2[:], B2[:], IG[:], ALU.mult, [B2, IG], [B2])
        yield
        c.op("dve", lambda e: e.tensor_tensor_scan(out=HB[:], data0=RA[:], data1=B2[:], initial=hs[:, ch:ch + 1],
                                                   op0=ALU.mult, op1=ALU.add), [RA, B2, hs], [HB])
        yield
        self.cp("dve", hs[:, ch:ch + 1], HB[:, 511:512], [HB], [hs])
        self.tt("pool", self.MIXT[2 + ch][:], G1[:], HB[:], ALU.mult, [G1, HB], [self.MIXT[2 + ch]])

    def gdn(self, l):
        c = self.c
        P = self.POOL
        fo = l * 60
        to = l * NTM
        QT = [P[0], P[1], P[2]]
        KT = QT
        KBT = [P[3], P[4], P[5]]
        qt = [b[:].bitcast(BF16)[:, 0:512] for b in QT]
        kt = [b[:].bitcast(BF16)[:, 512:1024] for b in KT]
        kbt = [b[:].bitcast(BF16)[:, 0:512] for b in KBT]
        KTM = [P[6], P[7], P[8], P[9]]
        VTM = [P[10], P[11], P[12], P[13]]
        ZS = [P[14], P[15], P[16], P[17]]
        SGT = P[22]
        AB = P[23]
        for grp in ((0, 1, 2), (3, 4, 5), (6, 7, 8)):
            gens = [self.gdn_front_chunk(l, ch, [P[14 + 3 * (ch % 3) + i] for i in range(3)], self.cbuf[ch % 3],
                                         qt, kt, QT, KT, KTM, VTM) for ch in grp]
            run_interleaved(gens)
        wb, wv = self.load_wa(l, "ba")
        ps = self.ps()
        for kc in range(8):
            self.mm(ps, ps[0:12, :], wv[:, kc, 0:12], self.XT[kc][:], [wb, self.XT[kc]], kc == 0, kc == 7)
        self.act(SGT[0:12, :], ps[0:12, :], AF.Sigmoid, [ps], [SGT])
        selb = self.cs("selb", slice(0, 12))
        for c3 in range(3):
            ps = self.ps()
            self.mm(ps, ps[:], selb[:, c3 * 128:(c3 + 1) * 128], SGT[0:12, :], [self.CST, SGT])
            self.tt("dve", kbt[c3], kt[c3], ps[:], ALU.mult, [KT[c3], ps], [KBT[c3]])
        wb, wv = self.load_wa(l, "tmB")
        ab = AB[:, 0:48].rearrange("p (s k) -> p s k", s=NS)
        for s in range(NS):
            ps = self.ps()
            for kc in range(8):
                self.mm(ps, ps[:, 0:396], self.XT[kc][:, s * 128:(s + 1) * 128], wv[:, kc, :], [wb, self.XT[kc]], kc == 0, kc == 7)
            self.act(ZS[s][:, 0:384], ps[:, 0:384], AF.Silu, [ps], [ZS[s]])
            self.cp("dve", AB[:, s * 12:(s + 1) * 12], ps[:, 384:396], [ps], [AB])
        def v46(o):
            return AB[:, o:o + 24].rearrange("p (s k) -> p s k", s=NS)
        a_v, b_v = ab[:, :, 0:6], ab[:, :, 6:12]
        Z1, AZ, EL, G, BETA = v46(48), v46(72), v46(96), v46(120), v46(144)
        dtb = self.TMS[:, to + 646:to + 652].unsqueeze(1).to_broadcast([128, NS, 6])
        nea = self.DER[:, l * 16:l * 16 + 6].unsqueeze(1).to_broadcast([128, NS, 6])
        self.tt("dve", Z1, a_v, dtb, ALU.add, [AB, self.TMS], [AB])
        self.stt(AZ, Z1, -1.0, Z1, ALU.mult, ALU.max, [AB], [AB])
        self.act(EL, AZ, AF.Exp, [AB], [AB], scale=-1.0)
        self.act(EL, EL, AF.Ln, [AB], [AB], bias=1.0)
        self.stt(G, Z1, 0.0, EL, ALU.max, ALU.add, [AB], [AB])
        self.tt("dve", G, G, nea, ALU.mult, [AB, self.DER], [AB])
        self.act(BETA, b_v, AF.Sigmoid, [AB], [AB])
        gng = self.TMS[:, to + 256:to + 640]
        for s in range(NS):
            self.gdn_pre(l, s, AB)
        prev_post = None
        for s in range(NS):
            gens = self.gdn_subtile_gens(l, s, qt, kt, kbt, QT, KT, KBT, KTM[s], VTM[s], AB)
            if prev_post is not None:
                gens.append(prev_post)
            run_interleaved(gens)
            prev_post = self.gdn_post(l, s, ZS[s], gng)
        run_interleaved([prev_post])

    def gdn_front_chunk(self, l, ch, B, CB, qt, kt, QT, KT, KTM, VTM):
        c = self.c
        CV, SQ, RN = B
        sqb = SQ[:].bitcast(BF16)[:, 0:512]
        fo, w = l * 60, self.FMS
        hist, hcol, wcol = self.HG[l], ch * 3, fo + 24 + ch * 4
        kind, c3 = ch // 3, ch % 3
        ps = self.ps()
        self.fm_mm(l, 14 + ch, ps)
        self.cp("pool", CB[:, 0:3], hist[:, hcol:hcol + 3], [hist], [CB])
        yield
        self.cp("act", CB[:, 3:515], ps[:], [ps], [CB])
        yield
        self.cp("pool", hist[:, hcol:hcol + 3], CB[:, 512:515], [CB], [hist])
        self.ts("dve", CV[:], CB[:, 3:515], w[:, wcol + 3:wcol + 4], None, ALU.mult, None, [CB, w], [CV])
        yield
        for k in range(3):
            self.stt(CV[:], CB[:, k:k + 512], w[:, wcol + k:wcol + k + 1], CV[:], ALU.mult, ALU.add, [CB, w, CV], [CV])
            yield
        self.act(CV[:], CV[:], AF.Silu, [CV], [CV])
        yield
        if kind == 2:
            pst = self.ps()
            for s in range(NS):
                self.tr(pst, pst[:, s * 128:(s + 1) * 128], CV[:, s * 128:(s + 1) * 128], [CV])
            yield
            for s in range(NS):
                self.cp("act" if s % 2 else "dve", VTM[s][:, c3 * 128:(c3 + 1) * 128], pst[:, s * 128:(s + 1) * 128], [pst], [VTM[s]])
            return
        self.tt("pool", sqb, CV[:], CV[:], ALU.mult, [CV], [SQ])
        yield
        pss = self.ps()
        self.mm(pss, pss[:], self.BONES[:], sqb, [self.BONES, SQ])
        yield
        self.act(RN[:], pss[:], AF.Sqrt, [pss], [RN], bias=1e-6)
        yield
        c.op("dve", lambda e: e.reciprocal(out=RN[:], in_=RN[:]), [RN], [RN])
        yield
        if kind == 0:
            self.stt(qt[c3], CV[:], 0.125, RN[:], ALU.mult, ALU.mult, [CV, RN], [QT[c3]])
            return
        self.tt("dve", CV[:], CV[:], RN[:], ALU.mult, [CV, RN], [CV])
        yield
        self.cp("pool", kt[c3], CV[:], [CV], [KT[c3]])
        pst = self.ps()
        for s in range(NS):
            self.tr(pst, pst[:, s * 128:(s + 1) * 128], CV[:, s * 128:(s + 1) * 128], [CV])
        yield
        for s in range(NS):
            self.cp("act" if s % 2 else "dve", KTM[s][:, c3 * 128:(c3 + 1) * 128], pst[:, s * 128:(s + 1) * 128], [pst], [KTM[s]])

    def gdn_pre(self, l, s, AB):
        SC = self.GSC[s]
        GCT = self.GCTB[s]
        g_s = AB[:, 120 + s * 6:120 + s * 6 + 6]
        beta_s = AB[:, 144 + s * 6:144 + s * 6 + 6]
        GC, GLB, EKT, EGC, BG, EGL0, EGL1 = (SC[:, i * 6:(i + 1) * 6] for i in range(7))
        ps = self.ps()
        self.mm(ps, ps[:, 0:6], self.cs("triU"), g_s, [self.CST, AB])
        self.mm(ps, ps[0:6, 128:256], g_s, self.cs("triU"), [self.CST, AB])
        self.cp("dve", GC, ps[:, 0:6], [ps], [SC])
        self.cp("dve", GCT[0:6, :], ps[0:6, 128:256], [ps], [GCT])
        ps = self.ps()
        self.mm(ps, ps[:, 0:6], self.cs("selB"), GC, [self.CST, SC])
        self.mm(ps, ps[:, 8:14], self.cs("sel63"), GC, [self.CST, SC])
        self.mm(ps, ps[:, 16:22], self.cs("sel127"), GC, [self.CST, SC])
        self.tt("dve", EKT, ps[:, 0:6], GC, ALU.subtract, [ps, SC], [SC])
        self.cp("dve", GLB, ps[:, 8:14], [ps], [SC])
        self.cp("dve", SC[:, 48:54], ps[:, 16:22], [ps], [SC])
        self.act(EKT, EKT, AF.Exp, [SC], [SC])
        self.act(EGC, GC, AF.Exp, [SC], [SC])
        self.act(EGL0, GLB, AF.Exp, [SC], [SC])
        self.act(EGL1, SC[:, 48:54], AF.Exp, [SC], [SC])
        self.tt("dve", BG, beta_s, EGC, ALU.mult, [AB, SC], [SC])

    def gdn_subtile_gens(self, l, s, qt, kt, kbt, QT, KT, KBT, KTM, VTM, AB):
        SC = self.GSC[s]
        GCT = self.GCTB[s]
        GV = self.GVB[s % 2]
        beta_s = AB[:, 144 + s * 6:144 + s * 6 + 6]
        GC, GLB, EKT, EGC, BG, EGL0, EGL1 = (SC[:, i * 6:(i + 1) * 6] for i in range(7))

        def bc(v):
            return v.unsqueeze(2).to_broadcast([128, 6, 64])

        def v3(ap):
            return ap.rearrange("p (h e) -> p h e", h=6)
        self.tt("dve", v3(GV[:, 0:384]), v3(VTM[:, 0:384]), bc(beta_s), ALU.mult, [VTM, AB], [GV])
        self.tt("dve", v3(GV[:, 384:768]), v3(KTM[:, 0:384]), bc(BG), ALU.mult, [KTM, SC], [GV])
        self.tt("dve", v3(GV[:, 768:1152]), v3(KTM[:, 0:384]), bc(EKT), ALU.mult, [KTM, SC], [GV])
        OG = self.GOG[s % 2]
        return [self.gdn_head(l, s, h, qt, kt, kbt, QT, KT, KBT, GV, AB, SC, GCT, OG, (EGL0, EGL1)) for h in range(6)]

    def gdn_post(self, l, s, ZS, gng):
        c = self.c
        OG = self.GOG[s % 2]
        st = self.sm()
        SQ = self.POOL[19]
        self.act(SQ[:, 0:384], OG[:, 0:384], AF.Square, [OG], [SQ])
        yield
        c.op("dve", lambda e: e.reduce_sum(out=st[:, 0:6], in_=SQ[:, 0:384].rearrange("p (h e) -> p h e", h=6), axis=AX.X), [SQ], [st])
        yield
        self.act(st[:, 8:14], st[:, 0:6], AF.Sqrt, [st], [st], bias=1e-6, scale=1.0 / 64)
        yield
        c.op("dve", lambda e: e.reciprocal(out=st[:, 16:22], in_=st[:, 8:14]), [st], [st])
        yield
        ogv = OG[:, 0:384].rearrange("p (h e) -> p h e", h=6)
        self.tt("dve", ogv, ogv, st[:, 16:22].unsqueeze(2).to_broadcast([128, 6, 64]), ALU.mult, [OG, st], [OG])
        yield
        self.tt("pool", OG[:, 0:384], OG[:, 0:384], gng, ALU.mult, [OG, self.TMS], [OG])
        yield
        self.tt("pool", OG[:, 0:384], OG[:, 0:384], ZS[:, 0:384], ALU.mult, [OG, ZS], [OG])
        yield
        pst = self.ps()
        for k in range(3):
            self.tr(pst, pst[:, k * 128:(k + 1) * 128], OG[:, k * 128:(k + 1) * 128], [OG])
        yield
        for k in range(3):
            self.cp("act" if k % 2 else "dve", self.MIXT[5 + k][:, s * 128:(s + 1) * 128], pst[:, k * 128:(k + 1) * 128], [pst], [self.MIXT[5 + k]])

    def gdn_head(self, l, s, h, qt, kt, kbt, QT, KT, KBT, GV, AB, SC, GCT, OG, EGL):
        cols = slice(s * 128, (s + 1) * 128)
        hp, c3 = (h % 2) * 64, h // 2
        hs = slice(h * 64, (h + 1) * 64)
        beta_s = AB[:, 144 + s * 6:144 + s * 6 + 6]
        GC, GLB, EKT, EGC, BG, EGL0, EGL1 = (SC[:, i * 6:(i + 1) * 6] for i in range(7))
        F, E, M, TTb, NPa, NPb, HV, HU = (self.GH[k][h] for k in ("F", "E", "M", "T", "Na", "Nb", "V", "U"))
        X1, X2 = F[:, 0:128], F[:, 128:256]
        E1s, E2s, E1i = E[:, 0:128], E[:, 128:256], E[:, 256:384]
        P0, N0, QKm = M[:, 0:128], M[:, 128:256], M[:, 256:384]
        Tt = TTb[:, 0:128]
        vb, kbg, ktl = GV[:, h * 64:(h + 1) * 64], GV[:, 384 + h * 64:384 + (h + 1) * 64], GV[:, 768 + h * 64:768 + (h + 1) * 64]
        vnew, wt = HV[:, 0:64], HV[:, 64:192]
        U = HU[:, 0:64]
        SG, SGB = self.SG[l], self.SGB[l]
        psB = self.ps()
        psk = psB
        oh = self.cs("oh6", slice(0, 6))
        self.mm(psB, psB[:, 0:128], oh[:, h * 128:(h + 1) * 128], GCT[0:6, :], [self.CST, GCT])
        ktc, kbc, qtc = kt[c3][hp:hp + 64, cols], kbt[c3][hp:hp + 64, cols], qt[c3][hp:hp + 64, cols]
        self.mm(psk, psk[:, 128:256], ktc, kbc, [KBT[c3], KT[c3]])
        self.mm(psk, psk[:, 256:384], kbc, ktc, [KBT[c3], KT[c3]])
        self.mm(psk, psk[:, 384:512], ktc, qtc, [QT[c3], KT[c3]])
        yield
        gcp = GC[:, h:h + 1]
        self.ts("dve", X1, psB[:, 0:128], gcp, 0.0, ALU.subtract, ALU.min, [psB, SC], [F])
        self.ts("dve", X2, psB[:, 0:128], gcp, 0.0, ALU.subtract, ALU.max, [psB, SC], [F])
        yield
        self.act(X1, X1, AF.Exp, [F], [F])
        self.act(X2, X2, AF.Exp, [F], [F], scale=-1.0)
        yield
        mo = CST["MSu"][0]
        self.tt("pool", E[:, 0:256], F[:, 0:256], self.CST[:, mo:mo + 256], ALU.mult, [F, self.CST], [E])
        self.tt("pool", E1i, X1, self.cs("MI"), ALU.mult, [F, self.CST], [E])
        yield
        self.tt("dve", M[:, 0:384], psk[:, 128:512], E[:, 0:384], ALU.mult, [psk, E], [M])
        yield
        self.tt("pool", Tt, self.cs("ident"), P0, ALU.subtract, [self.CST, M], [TTb])
        Nk, Pk, NB = N0, P0, M
        for j in range(0, 6):
            DB = NPa if j % 2 == 0 else NPb
            Nn, Pn = DB[:, 0:128], DB[:, 128:256]
            psn = self.ps()
            if j <= 4:
                self.mm(psn, psn[:, 0:128], Pk, Nk, [NB])
                if j <= 3:
                    self.mm(psn, psn[:, 128:256], Nk, Pk, [NB])
            if j >= 1:
                self.mm(psn, psn[:, 256:384], Nk, Tt, [NB, TTb])
            yield
            if j <= 4:
                self.cp("act", Nn, psn[:, 0:128], [psn], [DB])
                if j <= 3:
                    self.cp("act", Pn, psn[:, 128:256], [psn], [DB])
            if j >= 1:
                self.tt("dve", Tt, psn[:, 256:384], Tt, ALU.add, [psn, TTb], [TTb])
            if j <= 4:
                Nk, Pk, NB = Nn, Pn, DB
            yield
        psu = self.ps()
        self.mm(psu, psu[:, 0:64], Tt, vb, [TTb, GV])
        self.mm(psu, psu[hp:hp + 64, 128:256], kbg, Tt, [TTb, GV])
        yield
        self.cp("act", U, psu[:, 0:64], [psu], [HU])
        self.cp("act", wt[hp:hp + 64, :], psu[hp:hp + 64, 128:256], [psu], [HV])
        yield
        for cc in range(2):
            pc = slice(cc * 64, cc * 64 + 64)
            sgb = SGB[hp:hp + 64, c3 * 64:(c3 + 1) * 64]
            vn = vnew[pc, :]
            psw = self.ps()
            self.mm(psw, psw[pc, 0:64], wt[hp:hp + 64, cc * 64:cc * 64 + 64], sgb, [HV, SGB])
            self.mm(psw, psw[pc, 64:128], qt[c3][hp:hp + 64, s * 128 + cc * 64:s * 128 + cc * 64 + 64], sgb, [QT[c3], SGB])
            yield
            self.tt("dve", vn, U[pc, :], psw[pc, 0:64], ALU.subtract, [HU, psw], [HV])
            self.act(HU[pc, 64:128], psw[pc, 64:128], AF.Copy, [psw, SC], [HU], scale=EGC[pc, h:h + 1])
            yield
            pso = self.ps()
            self.mm(pso, pso[pc, 0:64], QKm[pc, cc * 64:cc * 64 + 64], vn, [M, HV])
            self.mm(pso, pso[hp:hp + 64, 64:128], ktl[pc, :], vn, [HV, GV])
            yield
            self.tt("dve", OG[pc, hs], HU[pc, 64:128], pso[pc, 0:64], ALU.add, [HU, pso], [OG])
            sgs = SG[hp:hp + 64, c3 * 64:(c3 + 1) * 64]
            self.stt(sgb, sgs, EGL[cc][hp:hp + 64, h:h + 1], pso[hp:hp + 64, 64:128], ALU.mult, ALU.add, [SG, SC, pso], [SGB])
            self.stt(sgs, sgs, EGL[cc][hp:hp + 64, h:h + 1], pso[hp:hp + 64, 64:128], ALU.mult, ALU.add, [SG, SC, pso], [SG])
            yield

    def ple(self, l, t):
        c = self.c
        PTB = [self.POOL[13], self.POOL[14]]
        PTTB = self.POOL[15]
        ptt = PTTB[:].bitcast(BF16)
        for sp in range(2):
            c.dma(PTB[sp][:].rearrange("p (s k) -> p s k", s=2),
                  self.p_d[l, t * T + sp * 256:t * T + (sp + 1) * 256, :].rearrange("(s p) k -> p s k", p=128), [], [PTB[sp]], PTB[sp])
        for k in range(2):
            ps = self.ps()
            for s in range(NS):
                pb = PTB[s // 2]
                self.tr(ps, ps[:, s * 128:(s + 1) * 128], pb[:, (s % 2) * 256 + k * 128:(s % 2) * 256 + (k + 1) * 128], [pb])
            self.cp("act" if k else "dve", ptt[:, k * T:(k + 1) * T], ps[:], [ps], [PTTB])
        wpb, wpv = self.load_wa(l, "pp")
        sgb, tb = self.POOL[11], self.POOL[12]
        for j in range(2):
            wb, wv = self.load_wa(l, f"pg{j}")
            for s in range(NS):
                psg, psp = self.ps(), self.ps()
                for kc in range(8):
                    self.mm(psg, psg[:], self.XT[kc][:, s * 128:(s + 1) * 128], wv[:, kc, :], [wb, self.XT[kc]], kc == 0, kc == 7)
                for k in range(2):
                    self.mm(psp, psp[:], ptt[:, k * T + s * 128:k * T + (s + 1) * 128], wpv[:, k, j * 512:(j + 1) * 512],
                            [wpb, PTTB], k == 0, k == 1)
                self.act(sgb[:], psg[:], AF.Sigmoid, [psg], [sgb])
                self.tt("dve", tb[:], sgb[:], psp[:], ALU.mult, [sgb, psp], [tb])
                xs = self.X[s][:, j * 512:(j + 1) * 512]
                self.stt(xs, xs, ALPHA, tb[:], ALU.mult, ALU.add, [self.X[s], tb], [self.X[s]])

    def build(self):
        c = self.c
        self.alloc()
        self.setup()
        self.alloc_main()
        self.cbuf = [c.sbuf(f"cb{i}", [128, 516], F32) for i in range(3)]
        self.cbi = 0
        self.sbuf_left = self.nc.sbuf_bytes_remaining
        for t in range(self.NT):
            for s in range(NS):
                c.dma(self.X[s][:], self.x_d[t * T + s * 128:t * T + (s + 1) * 128, :], [], [self.X[s]], self.X[s])
            if self.upto("rot"):
                self.rotary_tables(t)
            for l in range(L):
                if not self.upto("ffn1"):
                    break
                if l == 0:
                    self.make_xT()
                self.ffn(l, "f1", ALPHA)
                self.ln(l, 0)
                if l == 0:
                    self.dump("x1", t)
                if not self.upto("ret"):
                    break
                self.mixer(l, t)
                if l == 0 and "mix" in self.dbg_d:
                    for k in range(8):
                        self.cp("dve", self.POOL[k][:], self.MIXT[k][:], [self.MIXT[k]], [self.POOL[k]])
                        c.dma(self.dbg_d["mix"][k * 128:(k + 1) * 128, t * T:(t + 1) * T], self.POOL[k][:],
                              [self.POOL[k]], [], self.POOL[k])
                if not self.upto("wout"):
                    break
                self.ln(l, 1)
                if l == 0:
                    self.dump("x2", t)
                if not self.upto("ple"):
                    break
                self.ple(l, t)
                if not self.upto("all"):
                    break
                self.ffn(l, "f2", 1.0)
                OUT = None
                if l == L - 1:
                    P = self.POOL
                    OUT = [P[12], P[13], P[14], P[15], P[20], P[21], P[22], P[23]]
                self.ln(l, 2, with_xT=(l + 1 < L), OUT=OUT)
                if l == 0:
                    self.dump("x3", t)
            if self.upto("all"):
                for s in range(NS):
                    for j in range(2):
                        ob = OUT[2 * s + j]
                        c.dma(self.out_d[t * T + s * 128:t * T + (s + 1) * 128, j * 512:(j + 1) * 512], ob[:], [ob], [], ob)
            else:
                for s in range(NS):
                    c.dma(self.out_d[t * T + s * 128:t * T + (s + 1) * 128, :], self.X[s][:], [self.X[s]], [], self.X[s])
        c.wait_all("sp", self.X)
        if self.upto("all"):
            c.wait_all("sp", [self.POOL[i] for i in (12, 13, 14, 15, 20, 21, 22, 23)])
        for e in ("pe", "act", "dve", "pool"):
            pass
        self.stats = dict(ops=c.nops, waits=c.nwaits, sems=c.nsem, sbuf_left=self.sbuf_left)
        c.close()
        return self.nc


NPOOL = 24


def build_program(NT, dbg=(), stop="all"):
    nc = bass.Bass("TRN2", target_bir_lowering=False)
    kb = KB(nc, NT, dbg, stop)
    kb.build()
    return nc, kb


def make_inputs(inp, S):
    wblob = _build_wblob(inp)
    cst = _build_consts()
    fms, tms, lnp = _build_small(inp)
    maps = []
    B = inp["x"].shape[0]
    for b in range(B):
        maps.append({
            "x": np.ascontiguousarray(inp["x"][b, :S]),
            "p": np.ascontiguousarray(inp["p"][:, b, :S]),
            "pos": np.ascontiguousarray(inp["positions"][b:b + 1, :S]),
            "wblob": wblob, "cst": cst, "fms": fms, "tms": tms, "lnp": lnp,
        })
    return maps


def kernel(**inputs):
    inp = {k: np.asarray(v) for k, v in inputs.items()}
    B, S = inp["x"].shape[0], inp["x"].shape[1]
    nc, kb = build_program(S // T)
    maps = make_inputs(inp, S)
    res = run_bass_kernel_spmd(nc, maps, core_ids=list(range(B)))
    out = np.stack([np.asarray(r["out"]) for r in res.results], axis=0)
    return out.astype(np.float32)
```
